# Optimizing a Trainium2 kernel written in Bass

```python
import math
import jax, jax.numpy as jnp
from jax import lax
import numpy as np

D_MODEL = 1024
BATCH = 8
SEQ = 2048
DEPTH = 1
DEC_BATCH = 128
DEC_SEQ = 8
PAST_LEN = 16384
PAGE_SIZE = 128

MIX_WIDTH = D_MODEL
GM_HD = 128
GM_HEADS = (MIX_WIDTH // 2) // GM_HD
GM_WIDTH = GM_HEADS * GM_HD
GM_CHUNK = 128
ML_HD = 128
ML_HEADS = (MIX_WIDTH // 2) // ML_HD
ML_WIDTH = ML_HEADS * ML_HD
ML_CHUNK = 128
CONV_W = 4
D_FF = 2816
ALPHA = (2.0 * DEPTH) ** 0.25
BETA = (8.0 * DEPTH) ** -0.25
LN_EPS = 1e-5
N_IN = 2 * GM_WIDTH + 4 * ML_WIDTH + 2 * ML_HEADS
SPLIT_POINTS = [GM_WIDTH, 2 * GM_WIDTH, 2 * GM_WIDTH + ML_WIDTH, 2 * GM_WIDTH + 2 * ML_WIDTH,
                2 * GM_WIDTH + 3 * ML_WIDTH, 2 * GM_WIDTH + 4 * ML_WIDTH, 2 * GM_WIDTH + 4 * ML_WIDTH + ML_HEADS]
FG_OFF = 2 * GM_WIDTH + 4 * ML_WIDTH + ML_HEADS

kernel_name = "hymba_gmlp_mlstm_macaron_deepnorm_step"


def layer_norm(x, g, b):
    xf = x.astype(jnp.float32)
    mu = jnp.mean(xf, -1, keepdims=True)
    var = jnp.mean(jnp.square(xf - mu), -1, keepdims=True)
    return ((xf - mu) * lax.rsqrt(var + LN_EPS) * g + b).astype(x.dtype)


def head_rms(x, g):
    xf = x.astype(jnp.float32)
    return (xf * lax.rsqrt(jnp.mean(jnp.square(xf), -1, keepdims=True) + LN_EPS) * g).astype(x.dtype)


def swiglu(x, wg, wu, wd):
    return (jax.nn.silu(x @ wg) * (x @ wu)) @ wd


def causal_dwconv(x_ext, w, b, T):
    return sum(x_ext[:, j:j + T] * w[j] for j in range(CONV_W)) + b


def chunk_gmlp(u, v, ln_g, ln_b, w_s, b_s):
    B, T, H, Dh = v.shape
    vn = layer_norm(v, ln_g, ln_b)
    L = min(GM_CHUNK, T)
    NC = -(-T // L)
    pad = NC * L - T
    vp = jnp.pad(vn, ((0, 0), (0, pad), (0, 0), (0, 0))).reshape(B, NC, L, H, Dh)
    ws = jnp.tril(w_s[:, :L, :L])
    mixed = jnp.einsum('hts,bcshd->bcthd', ws, vp) + b_s[:, :L].T[:, :, None]
    mixed = mixed.reshape(B, NC * L, H, Dh)[:, :T]
    return u * mixed, vn


def mlstm_chunkwise(q, k, v, ig, lf, C0, n0, m0):
    B, T, H, D = q.shape
    f32 = jnp.float32
    L = math.gcd(ML_CHUNK, T)
    NC = T // L

    def to_chunks(a):
        return jnp.moveaxis(a.astype(f32).reshape((B, NC, L) + a.shape[2:]), 1, 0)

    xs = (to_chunks(q), to_chunks(k), to_chunks(v), to_chunks(ig), to_chunks(lf))
    causal = jnp.tril(jnp.ones((L, L), bool))

    def step(carry, inp):
        C, n, m = carry
        qc, kc, vc, ic, fc = inp
        bcum = jnp.cumsum(fc, axis=1)
        bt = bcum.transpose(0, 2, 1)
        it = ic.transpose(0, 2, 1)
        dlog = jnp.where(causal, bt[..., :, None] - bt[..., None, :] + it[..., None, :], -jnp.inf)
        inter = bt + m[..., None]
        m_t = jnp.maximum(inter, jnp.max(dlog, -1))
        w_intra = jnp.exp(dlog - m_t[..., None])
        w_inter = jnp.exp(inter - m_t).transpose(0, 2, 1)
        s = jnp.einsum('bthd,bshd->bhts', qc, kc) * w_intra
        num = jnp.einsum('bhts,bshd->bthd', s, vc) + w_inter[..., None] * jnp.einsum('bthd,bhde->bthe', qc, C)
        den = jnp.sum(s, -1).transpose(0, 2, 1) + w_inter * jnp.einsum('bthd,bhd->bth', qc, n)
        floor = jnp.exp(-m_t).transpose(0, 2, 1)
        h = num / jnp.maximum(jnp.abs(den), floor)[..., None]
        m_new = m_t[..., -1]
        dec = jnp.exp(inter[..., -1] - m_new)
        w_k = jnp.exp(bcum[:, -1:, :] - bcum + ic - m_new[:, None, :])
        C_new = dec[..., None, None] * C + jnp.einsum('bsh,bshd,bshe->bhde', w_k, kc, vc)
        n_new = dec[..., None] * n + jnp.einsum('bsh,bshd->bhd', w_k, kc)
        return (C_new, n_new, m_new), h

    (C, n, m), hs = lax.scan(step, (C0.astype(f32), n0.astype(f32), m0.astype(f32)), xs)
    h = jnp.moveaxis(hs, 0, 1).reshape(B, T, H, D)
    return h, C, n, m


def decoder_layer(x, conv_buf, C0, n0, m0, prm):
    (f1_wg, f1_wu, f1_wd, ln1_g, ln1_b, w_in, b_in, gm_ln_g, gm_ln_b, gm_ws, gm_bs,
     conv_w, conv_b, gm_out_g, ml_out_g, w_out, ln2_g, ln2_b,
     f2_wg, f2_wu, f2_wd, ln3_g, ln3_b) = prm
    B, T, _ = x.shape
    dt = x.dtype
    x = layer_norm(ALPHA * x + 0.5 * swiglu(x, f1_wg, f1_wu, f1_wd), ln1_g, ln1_b)
    z = x @ w_in + b_in
    u, v, q, k, vm, o, ig, fg = jnp.split(z, SPLIT_POINTS, axis=-1)
    u = jax.nn.gelu(u).reshape(B, T, GM_HEADS, GM_HD)
    v = jax.nn.gelu(v).reshape(B, T, GM_HEADS, GM_HD)
    gm, vn = chunk_gmlp(u, v, gm_ln_g, gm_ln_b, gm_ws, gm_bs)
    gm_rows = vn[:, ((T - 1) // GM_CHUNK) * GM_CHUNK:]
    qk = jnp.concatenate([q, k], -1)
    qk_ext = jnp.concatenate([conv_buf.astype(qk.dtype), qk], 1)
    new_buf = qk_ext[:, T:]
    qk = jax.nn.silu(causal_dwconv(qk_ext, conv_w, conv_b, T))
    q, k = jnp.split(qk, 2, axis=-1)
    to_heads = lambda a: a.reshape(B, T, ML_HEADS, ML_HD)
    h_ml, C, n, m = mlstm_chunkwise(to_heads(q), to_heads(k) * (ML_HD ** -0.5), to_heads(vm),
                                    ig, jax.nn.log_sigmoid(fg.astype(jnp.float32)), C0, n0, m0)
    h_ml = jax.nn.sigmoid(to_heads(o)) * h_ml.astype(dt)
    mix = jnp.concatenate([head_rms(gm, gm_out_g), head_rms(h_ml, ml_out_g)], axis=2).reshape(B, T, MIX_WIDTH)
    x = layer_norm(ALPHA * x + mix @ w_out, ln2_g, ln2_b)
    x = layer_norm(ALPHA * x + 0.5 * swiglu(x, f2_wg, f2_wu, f2_wd), ln3_g, ln3_b)
    return x, (gm_rows, new_buf, C.astype(C0.dtype), n.astype(n0.dtype), m.astype(m0.dtype))


def setup_inputs(seed: int = 0) -> dict:
    key = jax.random.key(seed)
    ks = iter(jax.random.split(key, 40))
    nrm = lambda shape, scale: scale * jax.random.normal(next(ks), shape, jnp.float32)
    P = DEPTH
    x_prompt = nrm((BATCH, SEQ, D_MODEL), 1.0)
    x_sample = nrm((DEC_BATCH, DEC_SEQ, D_MODEL), 1.0)
    state_conv = nrm((P, DEC_BATCH, CONV_W - 1, 2 * ML_WIDTH), 1.0)
    state_C = nrm((P, DEC_BATCH, ML_HEADS, ML_HD, ML_HD), 0.05)
    state_n = nrm((P, DEC_BATCH, ML_HEADS, ML_HD), 0.1)
    state_m = nrm((P, DEC_BATCH, ML_HEADS), 1.0)
    ffn1_wg = nrm((P, D_MODEL, D_FF), D_MODEL ** -0.5)
    ffn1_wu = nrm((P, D_MODEL, D_FF), D_MODEL ** -0.5)
    ffn1_wd = nrm((P, D_FF, D_MODEL), BETA * D_FF ** -0.5)
    ln1_g = 1.0 + nrm((P, D_MODEL), 0.02)
    ln1_b = nrm((P, D_MODEL), 0.02)
    w_in = nrm((P, D_MODEL, N_IN), D_MODEL ** -0.5)
    b_in = nrm((P, N_IN), 0.02)
    b_in = b_in.at[:, FG_OFF:FG_OFF + ML_HEADS].add(jnp.linspace(3.0, 6.0, ML_HEADS, dtype=jnp.float32))
    gm_ln_g = 1.0 + nrm((P, GM_HEADS, GM_HD), 0.02)
    gm_ln_b = nrm((P, GM_HEADS, GM_HD), 0.02)
    gm_ws = nrm((P, GM_HEADS, GM_CHUNK, GM_CHUNK), GM_CHUNK ** -0.5)
    gm_bs = 1.0 + nrm((P, GM_HEADS, GM_CHUNK), 0.02)
    conv_w = nrm((P, CONV_W, 2 * ML_WIDTH), CONV_W ** -0.5)
    conv_b = nrm((P, 2 * ML_WIDTH), 0.02)
    gm_out_g = 1.0 + nrm((P, GM_HEADS, GM_HD), 0.02)
    ml_out_g = 1.0 + nrm((P, ML_HEADS, ML_HD), 0.02)
    w_out = nrm((P, MIX_WIDTH, D_MODEL), BETA * MIX_WIDTH ** -0.5)
    ln2_g = 1.0 + nrm((P, D_MODEL), 0.02)
    ln2_b = nrm((P, D_MODEL), 0.02)
    ffn2_wg = nrm((P, D_MODEL, D_FF), D_MODEL ** -0.5)
    ffn2_wu = nrm((P, D_MODEL, D_FF), D_MODEL ** -0.5)
    ffn2_wd = nrm((P, D_FF, D_MODEL), BETA * D_FF ** -0.5)
    ln3_g = 1.0 + nrm((P, D_MODEL), 0.02)
    ln3_b = nrm((P, D_MODEL), 0.02)
    return {"x_prompt": x_prompt, "x_sample": x_sample, "state_conv": state_conv, "state_C": state_C,
            "state_n": state_n, "state_m": state_m,
            "ffn1_wg": ffn1_wg, "ffn1_wu": ffn1_wu, "ffn1_wd": ffn1_wd, "ln1_g": ln1_g, "ln1_b": ln1_b,
            "w_in": w_in, "b_in": b_in, "gm_ln_g": gm_ln_g, "gm_ln_b": gm_ln_b, "gm_ws": gm_ws,
            "gm_bs": gm_bs, "conv_w": conv_w, "conv_b": conv_b, "gm_out_g": gm_out_g,
            "ml_out_g": ml_out_g, "w_out": w_out, "ln2_g": ln2_g, "ln2_b": ln2_b,
            "ffn2_wg": ffn2_wg, "ffn2_wu": ffn2_wu, "ffn2_wd": ffn2_wd, "ln3_g": ln3_g, "ln3_b": ln3_b}


def reference(x_prompt, x_sample, state_conv, state_C, state_n, state_m,
              ffn1_wg, ffn1_wu, ffn1_wd, ln1_g, ln1_b, w_in, b_in, gm_ln_g, gm_ln_b, gm_ws,
              gm_bs, conv_w, conv_b, gm_out_g, ml_out_g, w_out, ln2_g, ln2_b,
              ffn2_wg, ffn2_wu, ffn2_wd, ln3_g, ln3_b):
    weights = (ffn1_wg, ffn1_wu, ffn1_wd, ln1_g, ln1_b, w_in, b_in, gm_ln_g, gm_ln_b, gm_ws, gm_bs,
               conv_w, conv_b, gm_out_g, ml_out_g, w_out, ln2_g, ln2_b,
               ffn2_wg, ffn2_wu, ffn2_wd, ln3_g, ln3_b)
    bp = x_prompt.shape[0]
    dt = x_prompt.dtype
    zc = jnp.zeros((bp, CONV_W - 1, 2 * ML_WIDTH), dt)
    zC = jnp.zeros((bp, ML_HEADS, ML_HD, ML_HD), dt)
    zn = jnp.zeros((bp, ML_HEADS, ML_HD), dt)
    zm = jnp.zeros((bp, ML_HEADS), dt)
    y_prompt, y_sample = x_prompt, x_sample
    outs_p, outs_s = [], []
    for l in range(DEPTH):
        prm = tuple(w[l] for w in weights)
        y_prompt, st_p = decoder_layer(y_prompt, zc, zC, zn, zm, prm)
        y_sample, st_s = decoder_layer(y_sample, state_conv[l], state_C[l], state_n[l], state_m[l], prm)
        outs_p.append(st_p)
        outs_s.append(st_s)
    gmv_p, conv_p, C_p, n_p, m_p = [jnp.stack(a) for a in zip(*outs_p)]
    gmv_s, conv_s, C_s, n_s, m_s = [jnp.stack(a) for a in zip(*outs_s)]
    return (y_prompt, y_sample, gmv_p, gmv_s, conv_p, conv_s, C_p, C_s, n_p, n_s, m_p, m_s)
```

```python
import math
import os as _os0
import numpy as np
import concourse.bass as bass
import concourse.mybir as mybir
from concourse.bass_utils import run_bass_kernel_spmd

F32 = mybir.dt.float32
BF16 = mybir.dt.bfloat16
AF = mybir.ActivationFunctionType
ALU = mybir.AluOpType

NCORES = 8
D = 1024
DFF = 2816
NFC = DFF // 128
NT = 17
NTOK = NT * 128
ALPHA = 2.0 ** 0.25
LN_EPS = 1e-5
LNC = math.log(128.0 ** -0.5)
NEG = -1.0e30
PASSES = [(0, 3), (3, 3), (6, 4), (10, 4), (14, 4), (18, 4)]
if _os0.environ.get("PASS_SIZES"):
    _ps = [int(v) for v in _os0.environ["PASS_SIZES"].split(",")]
    assert sum(_ps) == 22 and max(_ps) <= 4
    PASSES = [(sum(_ps[:i]), _ps[i]) for i in range(len(_ps))]
NPC = 4
GROUPS = [[0, 1, 2, 3], [4, 5, 6, 7], [8, 9, 10, 11], [12, 13, 14, 15], [16]]
NDS = 8
import os as _os0
ACT_PEN = float(_os0.environ.get('ACT_PEN', '1.0'))
XLAT = float(_os0.environ.get('XLAT', '0.3'))
CP_PRIO = int(_os0.environ.get('CP_PRIO', '0'))
JIT_SEED = int(_os0.environ.get('JIT_SEED', '0'))
JIT_AMP = float(_os0.environ.get('JIT_AMP', '0.2'))
SAME_LAT = float(_os0.environ.get('SAME_LAT', '0.06'))
PE_SCALE = float(_os0.environ.get('PE_SCALE', '1.0'))
DVE_FIX = float(_os0.environ.get('DVE_FIX', '0.15'))


class _Dummy:
    def then_inc(self, *a, **k):
        return self


class _Rec:
    def __init__(self):
        self.calls = []

    def __getattr__(self, name):
        def f(*a, **k):
            self.calls.append((name, a, k))
            return _Dummy()
        return f


def _free(ap):
    n = 1
    for d in ap.shape[1:]:
        n *= int(d)
    return n


_ACT_GROUP = {"Exp": "exp", "Ln": "exp", "Silu": "silu", "Gelu_apprx_tanh": "gelu", "Sigmoid": "sigmoid", "Sqrt": "sqrt"}


def _est(eng, name, a, k):
    if name == "matmul":
        rhs = a[2] if len(a) > 2 else k["rhs"]
        n = _free(rhs)
        m = 4.0 if rhs.dtype == F32 else 1.0
        return (0.02 + max(n, 64) * m * 0.00042) * PE_SCALE, 0.0, None
    if name == "transpose":
        in_ = k["in_"]
        return (0.02 + max(_free(in_), 64) * 0.00042) * PE_SCALE, 0.0, None
    if name == "dma_start":
        out = k["out"]
        tot = 1
        for d in out.shape:
            tot *= int(d)
        rows = max(1, tot // max(1, int(out.shape[-1])))
        esz = 2 if out.dtype == BF16 else 4
        issue = 0.1 if eng == "sp" else 0.5 + rows * 0.02
        return issue, 2.0 + tot * esz / 150e3, None
    out = k.get("out", a[0] if a else None)
    n = _free(out) if out is not None else 64
    if eng == "act":
        f = k.get("func")
        g = _ACT_GROUP.get(getattr(f, "name", str(f)).split(".")[-1]) if f is not None else None
        return 0.22 + n * 0.001 + (0.1 if k.get("accum_out") is not None else 0.0), 0.0, g
    if eng == "pool":
        return 0.3 + n * 0.0021, 0.0, None
    if name in ("bn_aggr",):
        return 0.2, 0.0, None
    return DVE_FIX + n * 0.00105 + (0.1 if k.get("accum_out") is not None else 0.0), 0.0, None


class Prog:
    ENG = ("pe", "act", "dve", "pool", "sp")
    WINDOW = int(_os0.environ.get('SWIN', '100'))
    ACT_PEN_DEFAULT = 0.0

    def __init__(self):
        self.nodes = []
        self.pending = {e: None for e in self.ENG}
        self.lastw = {}
        self.rd = {}
        self.seg = 0
        self.out_nodes = []
        self.ctx = None
        self.keymap = None

    @staticmethod
    def _canon(keys):
        return [k[:2] if (len(k) > 2 and k[0] == "b" and k[1] in "01234567") else k for k in keys]

    def fence(self):
        self.seg += 1

    def barrier(self, key, fn):
        node = {"eng": "pool", "fns": [fn], "kind": "op", "busy": 0.3, "lat": 0.0, "grp": None}
        nid = len(self.nodes)
        node["preds"] = set(range(nid))
        node["id"] = nid
        node["seg"] = self.seg
        self.nodes.append(node)
        self.lastw[key] = nid
        self.rd[key] = set()
        return nid

    def _add(self, node, reads, writes, after=()):
        if self.keymap is not None:
            reads = self.keymap(reads, True)
            writes = self.keymap(writes, False)
        reads = self._canon(reads)
        writes = self._canon(writes)
        nid = len(self.nodes)
        preds = set()
        for k in after:
            if k in self.lastw:
                preds.add(self.lastw[k])
            for r_ in self.rd.get(k, ()):
                preds.add(r_)
        for k in reads:
            if k in self.lastw:
                preds.add(self.lastw[k])
        for k in writes:
            if k in self.lastw:
                preds.add(self.lastw[k])
            for r_ in self.rd.get(k, ()):
                preds.add(r_)
        preds.discard(nid)
        node["preds"] = preds
        node["id"] = nid
        node["seg"] = self.seg
        self.nodes.append(node)
        for k in reads:
            self.rd.setdefault(k, set()).add(nid)
        for k in writes:
            self.lastw[k] = nid
            self.rd[k] = set()
        return nid

    def op(self, eng, fn, reads=(), writes=(), sig=True, r=None, w=None):
        reads = list(r if r is not None else reads)
        writes = list(w if w is not None else writes)
        pend = self.pending[eng]
        if pend is None:
            pend = {"eng": eng, "fns": [], "reads": [], "writes": [], "kind": "op"}
        r0 = _Rec()
        fn(r0)
        calls0 = r0.calls
        fn = (lambda e, calls0=calls0: [getattr(e, n_)(*a_, **k_) for (n_, a_, k_) in calls0][-1])
        pend["fns"].append(fn)
        pend["reads"] += reads
        pend["writes"] += writes
        if not sig:
            self.pending[eng] = pend
            return None
        self.pending[eng] = None
        rec = _Rec()
        for f in pend["fns"]:
            f(rec)
        pend["calls"] = rec.calls
        busy = 0.0
        grp = None
        for (name, a, k) in rec.calls:
            b_, _, g_ = _est(eng, name, a, k)
            busy += b_
            grp = g_ or grp
        if JIT_SEED:
            self._rs = (getattr(self, "_rs", JIT_SEED * 7919 + 13) * 1103515245 + 12345) % 2147483648
            busy *= 1.0 + JIT_AMP * ((self._rs / 2147483648.0) - 0.5)
        pend["busy"] = busy
        pend["lat"] = 0.0
        pend["grp"] = grp
        return self._add(pend, pend["reads"], pend["writes"])

    def dma(self, qeng, fn, reads=(), writes=(), out=False, r=None, w=None, prio=0, after=()):
        reads = list(r if r is not None else reads)
        writes = list(w if w is not None else writes)
        rec = _Rec()
        fn(rec)
        name, a, k = rec.calls[0]
        fn = (lambda e, name=name, a=a, k=k: getattr(e, name)(*a, **k))
        issue, lat, _ = _est(qeng, name, a, k)
        node = {"eng": qeng, "fns": [fn], "kind": "dma", "busy": issue, "lat": lat, "grp": None, "prio": prio}
        nid = self._add(node, reads, writes, after=after)
        if out:
            self.out_nodes.append(nid)
        return nid

    def schedule(self):
        for e in self.ENG:
            assert self.pending[e] is None, "dangling unsignalled group on %s" % e
        nodes = self.nodes
        cp = [0.0] * len(nodes)
        if CP_PRIO:
            for nd in reversed(nodes):
                i_ = nd["id"]
                tot = cp[i_] + nd["busy"] + nd["lat"]
                for p in nd["preds"]:
                    if cp[p] < tot:
                        cp[p] = tot
        order = {e: [] for e in self.ENG}
        finish = {}
        eng_free = {e: 0.0 for e in self.ENG}
        act_grp = [None]
        nseg = self.seg + 1
        t_base = 0.0
        for sg in range(nseg):
            queues = {e: [n["id"] for n in nodes if n["seg"] == sg and n["eng"] == e] for e in self.ENG}
            heads = {e: 0 for e in self.ENG}
            done = set()
            remaining = sum(len(q) for q in queues.values())
            for e in self.ENG:
                eng_free[e] = max(eng_free[e], t_base)
            while remaining:
                best = None
                for e in self.ENG:
                    q = queues[e]
                    i = heads[e]
                    cnt = 0
                    while i < len(q) and cnt < self.WINDOW:
                        nid = q[i]
                        i += 1
                        if nid in done:
                            continue
                        cnt += 1
                        nd = nodes[nid]
                        ok = True
                        st = eng_free[e]
                        for p in nd["preds"]:
                            if p not in finish:
                                ok = False
                                break
                            lat = SAME_LAT if nodes[p]["eng"] == e and nodes[p]["kind"] == "op" else XLAT
                            if finish[p] + lat > st:
                                st = finish[p] + lat
                        if not ok:
                            continue
                        st_real = st
                        if e == "act" and nd["grp"] is not None and nd["grp"] != act_grp[0]:
                            st = st + ACT_PEN
                        key = (st, nd.get("prio", 0), -cp[nid] if CP_PRIO else 0.0, nid, st_real)
                        if best is None or key < best[0]:
                            best = (key, e, nid)
                        if (not CP_PRIO) and st_real <= eng_free[e] + 1e-9 and st == st_real:
                            break
                assert best is not None, "scheduler deadlock"
                (_, _, _, _, st), e, nid = best
                nd = nodes[nid]
                busy = nd["busy"]
                if e == "act" and nd["grp"] is not None and nd["grp"] != act_grp[0]:
                    busy += 1.3
                    act_grp[0] = nd["grp"]
                eng_free[e] = st + busy
                finish[nid] = st + busy + nd["lat"]
                order[e].append(nid)
                done.add(nid)
                remaining -= 1
                q = queues[e]
                while heads[e] < len(q) and q[heads[e]] in done:
                    heads[e] += 1
            t_base = max([t_base] + [finish[n["id"]] for n in nodes if n["seg"] == sg])
        self.order = order
        self.est_total = t_base
        return order

    def sem_names(self):
        names = ["pe", "act", "dve", "pool"]
        for qe in ("sp", "pool"):
            for i in range(NDS):
                names.append("d%s%d" % (qe, i))
        return names

    def emit(self, block, sems):
        order = self.schedule()
        nodes = self.nodes
        cnt = {e: 0 for e in self.ENG}
        ndma = {e: 0 for e in self.ENG}
        dma_cnt = {}
        tok = {}
        prev_same_sem = {}
        for e in self.ENG:
            for nid in order[e]:
                nd = nodes[nid]
                if nd["kind"] == "op":
                    cnt[e] += 1
                    tok[nid] = (e, cnt[e])
                else:
                    sname = "d%s%d" % (e, ndma[e] % NDS)
                    ndma[e] += 1
                    prev = dma_cnt.get(sname, 0)
                    prev_same_sem[nid] = (sname, prev)
                    dma_cnt[sname] = prev + 16
                    tok[nid] = (sname, prev + 16)
        seg_floor = {}
        for sg in range(1, self.seg + 1):
            fl = {}
            for nd in nodes:
                if nd["seg"] < sg:
                    s, v = tok[nd["id"]]
                    if fl.get(s, 0) < v:
                        fl[s] = v
            seg_floor[sg] = fl
        progs = {}
        for e in self.ENG:
            known = {}
            lst = []
            for nid in order[e]:
                nd = nodes[nid]
                need = dict(seg_floor.get(nd["seg"], {}))
                for p in nd["preds"]:
                    s, v = tok[p]
                    if need.get(s, 0) < v:
                        need[s] = v
                if nd["kind"] == "dma":
                    s, v = prev_same_sem[nid]
                    if v > 0 and need.get(s, 0) < v:
                        need[s] = v
                waits = []
                for s, v in need.items():
                    if s == "pe" and e == "pe":
                        continue
                    if known.get(s, 0) < v:
                        waits.append((s, v))
                        known[s] = v
                lst.append((waits, nd["fns"], tok[nid][0], 16 if nd["kind"] == "dma" else 1))
            progs[e] = lst
        final = {}
        for s, v in dma_cnt.items():
            final[s] = v

        def run(engobj, eng, fin=False):
            for waits, fns, sname, inc in progs[eng]:
                for s, v in waits:
                    engobj.wait_ge(sems[s], v)
                ins = None
                for f in fns:
                    ins = f(engobj)
                ins.then_inc(sems[sname], inc)
            if fin:
                for s, v in final.items():
                    engobj.wait_ge(sems[s], v)

        @block.tensor
        def _(e):
            run(e, "pe")

        @block.scalar
        def _(e):
            run(e, "act")

        @block.vector
        def _(e):
            run(e, "dve")

        @block.gpsimd
        def _(e):
            run(e, "pool")

        @block.sync
        def _(e):
            run(e, "sp", fin=True)


def build_program(dbg=0):
    nc = bass.Bass("TRN2", target_bir_lowering=False)
    P = Prog()

    def din(name, shape):
        return nc.dram_tensor(name, shape, F32, kind="ExternalInput").ap()

    def dout(name, shape):
        return nc.dram_tensor(name, shape, F32, kind="ExternalOutput").ap()

    x_p = din("x_p", [2048, D])
    x_s = din("x_s", [128, D])
    sconv = din("sconv", [48, D])
    sC = din("sC", [16, 4, 128, 128])
    sn = din("sn", [64, 128])
    sm = din("sm", [16, 4])
    wd_ = {}
    for nm, shp in [("f1_wg", [D, DFF]), ("f1_wu", [D, DFF]), ("f1_wd", [DFF, D]), ("ln1_g", [1, D]), ("ln1_b", [1, D]),
                    ("w_in", [D, 3080]), ("b_in", [1, 3080]), ("gm_ln_g", [1, 512]), ("gm_ln_b", [1, 512]),
                    ("gm_ws", [4, 128, 128]), ("gm_bs", [4, 128]), ("conv_w", [4, D]), ("conv_b", [1, D]),
                    ("gm_out_g", [1, 512]), ("ml_out_g", [1, 512]), ("w_out", [D, D]), ("ln2_g", [1, D]), ("ln2_b", [1, D]),
                    ("f2_wg", [D, DFF]), ("f2_wu", [D, DFF]), ("f2_wd", [DFF, D]), ("ln3_g", [1, D]), ("ln3_b", [1, D])]:
        wd_[nm] = din(nm, shp)

    y_p = dout("y_p", [2048, D])
    y_s = dout("y_s", [128, D])
    gmv_p = dout("gmv_p", [128, 512])
    gmv_s = dout("gmv_s", [128, 512])
    conv_p = dout("conv_p", [3, D])
    conv_s = dout("conv_s", [48, D])
    C_p = dout("C_p", [4, 128, 128])
    C_s = dout("C_s", [16, 4, 128, 128])
    n_p = dout("n_p", [4, 128])
    n_s = dout("n_s", [64, 128])
    m_p = dout("m_p", [4, 1])
    m_s = dout("m_s", [16, 4])

    import os as _os
    ARENA_B = 73472
    XT_B = 8 * NTOK * 2
    NSLOT = 16

    import contextlib
    with contextlib.ExitStack() as es:
        def sb(name, shape, dt=F32):
            return es.enter_context(nc.sbuf_tensor(name, shape, dt))

        acc = sb("acc", [128, NT, D])
        xtraw = sb("xtraw", [128, XT_B // 4])
        arena = sb("arena", [128, ARENA_B // 4])
        lnp = sb("lnp", [128, 2, D])
        ident_bf = sb("ident_bf", [128, 128], BF16)
        ident_f = sb("ident_f", [128, 128])
        negmask = sb("negmask", [128, 128])
        negmask_s = sb("negmask_s", [128, 128])
        wsT = sb("wsT", [128, 4, 128], BF16)
        wsT_s = sb("wsT_s", [128, 4, 128], BF16)
        colsC = sb("colsC", [128, 64])
        gmln = sb("gmln", [128, 2, 512])
        b4 = sb("b4", [4, 512], BF16)
        selb = sb("selb", [4, 4, 128], BF16)
        sel = sb("sel", [4, 4, 128])
        onehot = sb("onehot", [128, 16])
        bgate = sb("bgate", [4, 2])
        small = sb("small", [128, 320])
        gt = sb("gt", [4, NSLOT, 144])
        Cst = sb("Cst", [128, 4, 130])
        Cbf = sb("Cbf", [128, 4, 130], BF16)
        psum = [es.enter_context(nc.psum_tensor("ps%d" % i, [128, 512], F32)) for i in range(8)]

        def carve(raw, off, nbytes, dt):
            assert off % 4 == 0 and nbytes % 4 == 0, (off, nbytes)
            return raw[:, off // 4:(off + nbytes) // 4].bitcast(dt)

        XTK = ["xT%dh%d" % (t, h) for t in range(NT) for h in range(2)]

        def c_(fn, eng="pool", r=(), w=()):
            P.op(eng, fn, r, w)

        c_(lambda e: e.memset(ident_f[:], 1.0), w=["ident_f"])
        c_(lambda e: e.affine_select(out=ident_f[:], in_=ident_f[:], pattern=[[-1, 128]], compare_op=ALU.is_equal,
                                     fill=0.0, base=0, channel_multiplier=1), r=["ident_f"], w=["ident_f"])
        c_(lambda e: e.tensor_copy(out=ident_bf[:], in_=ident_f[:]), r=["ident_f"], w=["ident_bf"])
        c_(lambda e: e.memset(negmask[:], 0.0), w=["negmask"])
        c_(lambda e: e.affine_select(out=negmask[:], in_=negmask[:], pattern=[[1, 128]], compare_op=ALU.is_ge,
                                     fill=NEG, base=0, channel_multiplier=-1), r=["negmask"], w=["negmask"])
        c_(lambda e: e.affine_select(out=negmask_s[:].rearrange("p (j r) -> p j r", j=16),
                                     in_=negmask[:].rearrange("p (j r) -> p j r", j=16),
                                     pattern=[[-8, 16], [0, 8]], compare_op=ALU.is_ge,
                                     fill=NEG, base=0, channel_multiplier=1), r=["negmask"], w=["negmask_s"])
        c_(lambda e: e.memset(sel[:], 1.0), w=["sel"])
        c_(lambda e: e.affine_select(out=sel[:], in_=sel[:], pattern=[[-1, 4], [0, 128]], compare_op=ALU.is_equal,
                                     fill=0.0, base=0, channel_multiplier=1), r=["sel"], w=["sel"])
        c_(lambda e: e.tensor_copy(out=selb[:], in_=sel[:]), r=["sel"], w=["selb"])
        c_(lambda e: e.memset(onehot[:], 1.0), w=["onehot"])
        c_(lambda e: e.affine_select(out=onehot[:], in_=onehot[:], pattern=[[-8, 16]], compare_op=ALU.is_ge,
                                     fill=0.0, base=0, channel_multiplier=1), r=["onehot"], w=["onehot"])
        c_(lambda e: e.affine_select(out=onehot[:], in_=onehot[:], pattern=[[8, 16]], compare_op=ALU.is_ge,
                                     fill=0.0, base=7, channel_multiplier=-1), r=["onehot"], w=["onehot"])
        c_(lambda e: e.memset(Cst[:], 0.0), w=["Cst"])
        c_(lambda e: e.memset(Cbf[:], 0.0), w=["Cbf"])
        c_(lambda e: e.memset(gt[:], 0.0), w=["gt"])
        c_(lambda e: e.memset(small[:], 0.0), w=["small"])

        wtmp = carve(arena, 65664, 512 * 8, F32).rearrange("p (a b) -> p a b", a=8)
        rowsC = carve(arena, 65664 + 4096, 512, F32)
        win = wd_["w_in"]
        b_in = wd_["b_in"]
        SK = ["setup_rows"]
        P.dma("sp", lambda e: e.dma_start(out=rowsC[0:8, :], in_=b_in[0, 1024:2048].rearrange("(c p) -> c p", p=128)), w=["rows0"])
        P.dma("sp", lambda e: e.dma_start(out=rowsC[8:16, :], in_=wd_["conv_b"][0, :].rearrange("(c p) -> c p", p=128)), w=["rows1"])
        P.dma("sp", lambda e: e.dma_start(out=rowsC[16:48, :], in_=wd_["conv_w"].rearrange("j (c p) -> (j c) p", p=128)), w=["rows2"])
        P.dma("sp", lambda e: e.dma_start(out=rowsC[48:52, :], in_=wd_["gm_bs"][:, :]), w=["rows3"])
        bs_src = bass.AP(wd_["gm_bs"].tensor, 0, [[128, 4], [0, 16], [1, 8]])
        P.dma("sp", lambda e: e.dma_start(out=rowsC[52:56, :].rearrange("p (j r) -> p j r", j=16), in_=bs_src), w=["rows4"])
        P.dma("sp", lambda e: e.dma_start(out=rowsC[56:60, :], in_=wd_["gm_out_g"][0, :].rearrange("(c p) -> c p", p=128)), w=["rows5"])
        P.dma("sp", lambda e: e.dma_start(out=rowsC[60:64, :], in_=wd_["ml_out_g"][0, :].rearrange("(c p) -> c p", p=128)), w=["rows6"])
        P.op("pe", lambda e: e.transpose(out=psum[6][:, 0:64], in_=rowsC[0:64, :], identity=ident_f[0:64, 0:64]),
             ["rows%d" % i_ for i_ in range(7)] + ["ident_f"], ["b6"])
        P.op("dve", lambda e: e.tensor_copy(out=colsC[:], in_=psum[6][:, 0:64]), ["b6"], ["colsC"])
        P.dma("sp", lambda e: e.dma_start(out=bgate[:, 0:1], in_=b_in[0, 3072:3076].rearrange("(p o) -> p o", o=1)), w=["bgate"])
        P.dma("sp", lambda e: e.dma_start(out=bgate[:, 1:2], in_=b_in[0, 3076:3080].rearrange("(p o) -> p o", o=1)), w=["bgate"])
        P.dma("sp", lambda e: e.dma_start(out=gmln[:, 0, :], in_=wd_["gm_ln_g"].partition_broadcast(128)), w=["gmln"])
        P.dma("sp", lambda e: e.dma_start(out=gmln[:, 1, :], in_=wd_["gm_ln_b"].partition_broadcast(128)), w=["gmln"])
        for i, c0 in enumerate([0, 512, 2048, 2560]):
            P.dma("pool", lambda e, i=i, c0=c0: e.dma_start(out=b4[i:i + 1, :], in_=b_in[0:1, c0:c0 + 512]), w=["bias4"])

        for h in range(4):
            P.dma("sp", lambda e, h=h: e.dma_start(out=wtmp[:, h, :], in_=wd_["gm_ws"][h, :, :]), w=["wtmp%d" % h])
            P.op("pool", lambda e, h=h: e.affine_select(out=wtmp[:, h, :], in_=wtmp[:, h, :], pattern=[[-1, 128]],
                                                        compare_op=ALU.is_ge, fill=0.0, base=0, channel_multiplier=1), ["wtmp%d" % h], ["wtmp%d" % h])
            P.op("pe", lambda e, h=h: e.transpose(out=psum[7][:, h * 128:(h + 1) * 128], in_=wtmp[:, h, :], identity=ident_f[:]),
                 ["wtmp%d" % h, "ident_f"], ["b7"])
        P.op("dve", lambda e: e.tensor_copy(out=wsT[:].rearrange("p a b -> p (a b)"), in_=psum[7][:, :]), ["b7"], ["wsT"])
        w8 = carve(arena, 65664 + 4608, 128, F32).rearrange("p (h c) -> p h c", h=4)
        for h in range(4):
            w8_src = bass.AP(wd_["gm_ws"].tensor, h * 128 * 128, [[0, 16], [128, 8], [1, 8]])
            for j in range(16):
                pass
            P.dma("sp", lambda e, h=h, w8_src=w8_src: e.dma_start(out=w8[:, h, :], in_=w8_src), w=["w8_%d" % h])
        for h in range(4):
            hh = 4 + h
            for j in range(16):
                P.op("dve", lambda e, h=h, hh=hh, j=j: e.tensor_scalar(out=wtmp[:, hh, 8 * j:8 * j + 8], in0=w8[:, h, :],
                                                                       scalar1=onehot[:, j:j + 1], scalar2=None, op0=ALU.mult),
                     ["w8_%d" % h, "onehot"], ["wtmp%d_%d" % (hh, j)])
            WJ = ["wtmp%d_%d" % (hh, j) for j in range(16)]
            P.op("pool", lambda e, hh=hh: e.affine_select(out=wtmp[:, hh, :], in_=wtmp[:, hh, :], pattern=[[-1, 128]],
                                                          compare_op=ALU.is_ge, fill=0.0, base=0, channel_multiplier=1), WJ, WJ)
            P.op("pe", lambda e, h=h, hh=hh: e.transpose(out=psum[6][:, h * 128:(h + 1) * 128], in_=wtmp[:, hh, :], identity=ident_f[:]),
                 WJ + ["ident_f"], ["b6"])
        P.op("dve", lambda e: e.tensor_copy(out=wsT_s[:].rearrange("p a b -> p (a b)"), in_=psum[6][:, :]), ["b6"], ["wsT_s"])

        st6 = small[:, 0:12].rearrange("p (a b) -> p a b", a=2)
        mv = small[:, 12:14]
        sd = small[:, 14:15]
        rstd = small[:, 15:16]
        nmr = small[:, 16:17]

        LN_GAMMA_ENG = "dve"

        ln_junk = carve(arena, 61440, 2048, BF16)

        def layer_norm_tile(t, act_stats=False, affine=True):
            a = acc[:, t, :]
            k = "acc%d" % t
            if act_stats:
                s1 = small[:, 0:1]
                s2 = small[:, 1:2]
                msq = small[:, 2:3]
                P.op("act", lambda e: e.activation(out=ln_junk, in_=a, func=AF.Identity, accum_out=s1), [k], ["lnjunk", "ln_st0"])
                P.op("act", lambda e: e.activation(out=ln_junk, in_=a, func=AF.Square, accum_out=s2), [k], ["lnjunk", "ln_st1"])
                P.op("dve", lambda e: e.tensor_scalar(out=mv[:, 0:1], in0=s1, scalar1=1.0 / D, scalar2=None, op0=ALU.mult), ["ln_st0", "ln_mv"], ["ln_mv"])
                P.op("dve", lambda e: e.tensor_tensor(out=msq, in0=mv[:, 0:1], in1=mv[:, 0:1], op=ALU.mult), ["ln_mv"], ["ln_msq"])
                P.op("dve", lambda e: e.scalar_tensor_tensor(out=mv[:, 1:2], in0=s2, scalar=1.0 / D, in1=msq, op0=ALU.mult, op1=ALU.subtract),
                     ["ln_st1", "ln_msq", "ln_mv"], ["ln_mv"])
            else:
                P.op("dve", lambda e: e.bn_stats(out=st6[:, 0, :], in_=a[:, 0:512]), [k], ["ln_st0"])
                P.op("dve", lambda e: e.bn_stats(out=st6[:, 1, :], in_=a[:, 512:1024]), [k], ["ln_st1"])
                P.op("dve", lambda e: e.bn_aggr(out=mv, in_=small[:, 0:12]), ["ln_st0", "ln_st1"], ["ln_mv"])
            P.op("act", lambda e: e.activation(out=sd, in_=mv[:, 1:2], func=AF.Ln, bias=LN_EPS / (ALPHA * ALPHA), scale=1.0), ["ln_mv"], ["ln_sd"])
            P.op("act", lambda e: e.activation(out=rstd, in_=sd, func=AF.Exp, scale=-0.5), ["ln_sd"], ["ln_rstd"])
            P.op("dve", lambda e: e.tensor_scalar(out=nmr, in0=mv[:, 0:1], scalar1=rstd, scalar2=-1.0, op0=ALU.mult, op1=ALU.mult),
                 ["ln_mv", "ln_rstd"], ["ln_nmr"])
            P.op("act", lambda e: e.activation(out=a, in_=a, func=AF.Identity, bias=nmr, scale=rstd), [k, "ln_rstd", "ln_nmr"], [k])
            if affine:
                P.op(LN_GAMMA_ENG, lambda e: e.tensor_tensor(out=a, in0=a, in1=lnp[:, 0, :], op=ALU.mult), [k, "lnp"], [k])
                P.op("dve", lambda e: e.tensor_tensor(out=a, in0=a, in1=lnp[:, 1, :], op=ALU.add), [k, "lnp"], [k])

        def load_lnp(g, b):
            P.dma("sp", lambda e: e.dma_start(out=lnp[:, 0, :], in_=wd_[g].partition_broadcast(128)), w=["lnp"])
            P.dma("sp", lambda e: e.dma_start(out=lnp[:, 1, :], in_=wd_[b].partition_broadcast(128)), w=["lnp"])

        xT = xtraw[:, :].bitcast(BF16).rearrange("p (kc t) -> p kc t", kc=8)
        WB = 6144 * NPC
        f_wg = [carve(arena, b * WB, 2048 * NPC, BF16).rearrange("p (kc f) -> p kc f", kc=8) for b in range(2)]
        f_wu = [carve(arena, b * WB + 2048 * NPC, 2048 * NPC, BF16).rearrange("p (kc f) -> p kc f", kc=8) for b in range(2)]
        f_wd = [carve(arena, b * WB + 4096 * NPC, 2048 * NPC, BF16).rearrange("p (c d) -> p c d", c=NPC) for b in range(2)]
        o0 = 2 * WB
        f_hT = [carve(arena, o0 + b * 1024 * NPC, 1024 * NPC, BF16).rearrange("p (c t) -> p c t", c=NPC) for b in range(2)]
        o0 += 2 * 1024 * NPC
        f_sg = [carve(arena, o0 + b * 2048, 2048, F32) for b in range(2)]
        o0 += 4096
        assert o0 <= ARENA_B

        def transpose_tile_to(t, dst3, dkey, banks=(6, 7), bkeys=(("b6",), ("b7",)), eng_a="dve", eng_b="act", extra_r=()):
            k = "acc%d" % t
            for half in range(2):
                bank = banks[half]
                bk = list(bkeys[half])
                for q_ in range(4):
                    kc = half * 4 + q_
                    P.op("pe", lambda e, bank=bank, q_=q_, kc=kc: e.transpose(out=psum[bank][:, q_ * 128:(q_ + 1) * 128],
                                                                             in_=acc[:, t, kc * 128:(kc + 1) * 128], identity=ident_f[:]),
                         [k, "ident_f"], bk, sig=(q_ == 3))
                eng = eng_a if half == 0 else eng_b
                if eng == "act":
                    P.op("act", lambda e, bank=bank, half=half: e.activation(
                        out=dst3[:, half * 4:half * 4 + 4, :], in_=psum[bank][:, :].rearrange("p (a b) -> p a b", a=4), func=AF.Copy),
                        bk + list(extra_r), [dkey + ("h%d" % half)])
                else:
                    P.op(eng, lambda e, bank=bank, half=half: e.tensor_copy(
                        out=dst3[:, half * 4:half * 4 + 4, :], in_=psum[bank][:, :].rearrange("p (a b) -> p a b", a=4)),
                        bk + list(extra_r), [dkey + ("h%d" % half)])

        def ffn_stage(sidx, first, pre_affine=False, lnp_after_p0=None):
            pre = "f%d_" % sidx
            wg_v = wd_[pre + "wg"].rearrange("(kc kp) f -> kp kc f", kp=128)
            wu_v = wd_[pre + "wu"].rearrange("(kc kp) f -> kp kc f", kp=128)
            wdn_v = wd_[pre + "wd"].rearrange("(c fp) d -> fp c d", fp=128)
            cnt = 0
            hb_i = 0
            WALIAS = (["w_in%d" % c for c in range(8)] + ["w_out"]) if sidx == 2 else []
            for p, (c0, npc) in enumerate(PASSES):
                b = p % 2
                wprio = -1 if (sidx == 1 and p == 0) else 0
                for ci in range(npc):
                    P.dma("pool", lambda e, b=b, c0=c0, ci=ci: e.dma_start(out=f_wg[b][:, :, ci * 128:(ci + 1) * 128],
                                                                           in_=wg_v[:, :, (c0 + ci) * 128:(c0 + ci + 1) * 128]),
                          w=["wg%d_%d" % (b, ci)], prio=wprio, after=WALIAS)
                    P.dma("pool", lambda e, b=b, c0=c0, ci=ci: e.dma_start(out=f_wu[b][:, :, ci * 128:(ci + 1) * 128],
                                                                           in_=wu_v[:, :, (c0 + ci) * 128:(c0 + ci + 1) * 128]),
                          w=["wu%d_%d" % (b, ci)], prio=wprio, after=WALIAS)
                for ci in range(npc):
                    P.dma("pool", lambda e, b=b, c0=c0, ci=ci: e.dma_start(out=f_wd[b][:, ci, :], in_=wdn_v[:, c0 + ci, :]),
                          w=["wd%d_%d" % (b, ci)], prio=wprio, after=WALIAS)
                last = (p == len(PASSES) - 1)
                if p == 1 and lnp_after_p0 is not None:
                    load_lnp(*lnp_after_p0)
                for g, tiles in enumerate(GROUPS):
                    n = 128 * len(tiles)
                    t0 = tiles[0] * 128
                    xkeys = []
                    for t in tiles:
                        xkeys += ["xT%dh0" % t, "xT%dh1" % t]
                    if p == 0:
                        for t in tiles:
                            k = "acc%d" % t
                            a = acc[:, t, :]
                            if first:
                                src = x_p[t * 128:(t + 1) * 128, :] if t < 16 else x_s[:, :]
                                P.dma("sp", lambda e, a=a, src=src: e.dma_start(out=a, in_=src), w=[k], prio=-1)
                            if pre_affine:
                                P.op("dve", lambda e, a=a: e.tensor_tensor(out=a, in0=a, in1=lnp[:, 0, :], op=ALU.mult), [k, "lnp"], [k])
                                P.op("dve", lambda e, a=a: e.tensor_tensor(out=a, in0=a, in1=lnp[:, 1, :], op=ALU.add), [k, "lnp"], [k])
                            transpose_tile_to(t, xT[:, :, t * 128:(t + 1) * 128], "xT%d" % t, extra_r=(["BAR2"] if sidx == 2 else []))
                    hb = hb_i % 2
                    hb_i += 1
                    for ci in range(npc):
                        s = cnt % 2
                        cnt += 1
                        for kc in range(8):
                            P.op("pe", lambda e, s=s, b=b, kc=kc, ci=ci, t0=t0, n=n: e.matmul(
                                psum[s][:, 0:n], f_wg[b][:, kc, ci * 128:(ci + 1) * 128], xT[:, kc, t0:t0 + n],
                                start=(kc == 0), stop=(kc == 7)), ["wg%d_%d" % (b, ci)] + xkeys, ["psg%d" % s], sig=(kc == 7))
                        for kc in range(8):
                            P.op("pe", lambda e, s=s, b=b, kc=kc, ci=ci, t0=t0, n=n: e.matmul(
                                psum[2 + s][:, 0:n], f_wu[b][:, kc, ci * 128:(ci + 1) * 128], xT[:, kc, t0:t0 + n],
                                start=(kc == 0), stop=(kc == 7)), ["wu%d_%d" % (b, ci)] + xkeys, ["psu%d" % s], sig=(kc == 7))
                        P.op("act", lambda e, s=s, n=n: e.activation(out=f_sg[s][:, 0:n], in_=psum[s][:, 0:n], func=AF.Silu),
                             ["psg%d" % s], ["sg%d" % s])
                        P.op("dve", lambda e, s=s, n=n, hb=hb, ci=ci: e.tensor_tensor(out=f_hT[hb][:, ci, 0:n], in0=f_sg[s][:, 0:n],
                                                                                    in1=psum[2 + s][:, 0:n], op=ALU.mult),
                             ["sg%d" % s, "psu%d" % s], ["hT%d_%d" % (hb, ci)])
                    for ti, t in enumerate(tiles):
                        k = "acc%d" % t
                        for half in range(2):
                            for ci in range(npc):
                                P.op("pe", lambda e, half=half, hb=hb, ci=ci, ti=ti, b=b, npc=npc: e.matmul(
                                    psum[4 + half][:, :], f_hT[hb][:, ci, ti * 128:(ti + 1) * 128],
                                    f_wd[b][:, ci, half * 512:(half + 1) * 512], start=(ci == 0), stop=(ci == npc - 1)),
                                    ["hT%d_%d" % (hb, ci), "wd%d_%d" % (b, ci)], ["psy%d" % half], sig=(ci == npc - 1))
                            P.op("dve", lambda e, half=half, t=t: e.scalar_tensor_tensor(
                                out=acc[:, t, half * 512:(half + 1) * 512], in0=psum[4 + half][:, :], scalar=0.5 / ALPHA,
                                in1=acc[:, t, half * 512:(half + 1) * 512], op0=ALU.mult, op1=ALU.add),
                                ["psy%d" % half, k], [k])
                        if last:
                            layer_norm_tile(t, act_stats=True)

        w_in_sb = carve(arena, 0, 49280, BF16).rearrange("p (kc f) -> p kc f", kc=8)
        w_out_sb = carve(arena, 49280, 16384, BF16).rearrange("p (kc f) -> p kc f", kc=8)
        _regions = [[xtraw, 0, XT_B], [arena, 65664, ARENA_B]]

        def walloc(nbytes, dt):
            nb = (nbytes + 31) // 32 * 32
            for rg in _regions:
                if rg[1] + nb <= rg[2]:
                    v = carve(rg[0], rg[1], nb, dt)
                    rg[1] += nb
                    return v
            raise AssertionError("mixer working set does not fit")

        x1T = walloc(2048, BF16).rearrange("p (a b) -> p a b", a=8)
        mixT = walloc(2048, BF16).rearrange("p (a b) -> p a b", a=8)
        u_act = walloc(2048, F32)
        vn = walloc(2048, F32)
        vn_bf = walloc(1024, BF16)
        v_ext2 = [walloc(1040, BF16)[:, 0:520].rearrange("p (a b) -> p a b", a=4) for _ in range(2)]
        o_sig2 = [walloc(2048, F32) for _ in range(2)]
        extb = walloc(5632, F32)
        ext = extb[:, 0:8 * 131].rearrange("p (a b) -> p a b", a=8)
        ext_s = extb[:, 0:8 * 176].rearrange("p (a j r) -> p a j r", a=8, j=16)
        extbb = walloc(2816, BF16)
        ext_bf = extbb[:, 0:8 * 131].rearrange("p (a b) -> p a b", a=8)
        ext_s_bf = extbb[:, 0:8 * 176].rearrange("p (a j r) -> p a j r", a=8, j=16)
        Dg = lnp[:, :, :].rearrange("p a b -> p (a b)").bitcast(BF16).rearrange("p (i c) -> p i c", i=32)
        qkT_raw = [walloc(2048, BF16) for _ in range(2)]
        qkT_bf2 = [q_.rearrange("p (a b) -> p a b", a=8) for q_ in qkT_raw]
        kw_bf = walloc(1024, BF16).rearrange("p (a b) -> p a b", a=4)
        mixf = walloc(2048, F32)
        mix_bf = walloc(2048, BF16)
        Dm = [walloc(512, F32) for _ in range(2)]
        Pb = [walloc(256, BF16) for _ in range(2)]
        itw = walloc(528, F32)
        junk = walloc(256, BF16)
        C0s = [qkT_raw[1].bitcast(F32).rearrange("p (a b) -> p a b", a=4), o_sig2[1].rearrange("p (a b) -> p a b", a=4)]
        vmask = [walloc(1040, BF16)[:, 0:520].rearrange("p (a b) -> p a b", a=4) for _ in range(2)]
        qTf = walloc(2048, F32).rearrange("p (a b) -> p a b", a=4)
        convc = walloc(1536, F32).rearrange("p (a b) -> p a b", a=8)
        n0row = walloc(512, F32)
        n0T = walloc(256, F32)
        nnewT = walloc(256, F32)

        st4 = small[:, 20:44].rearrange("p (a b) -> p a b", a=4)
        mv4 = small[:, 44:52].rearrange("p (a b) -> p a b", a=4)
        sd4 = small[:, 52:56]
        rstd4 = small[:, 56:60]
        nmr4 = small[:, 60:64]
        ssq = small[:, 64:68]
        rr = small[:, 68:72]
        cA = small[:, 72:76]
        cG = small[:, 76:80]
        cM = small[:, 80:84]
        cAp = small[:, 84:88]
        Gprev_bc = small[:, 88:92]
        Gend_bc = small[:, 92:96]
        d1 = small[:, 96:100]
        winter = small[:, 100:104]
        floor_ = small[:, 104:108]
        d2 = small[:, 108:112]
        wk = small[:, 112:116]
        d3 = small[:, 116:120]
        dec = small[:, 120:124]
        dd = small[:, 124:125]
        rd = small[:, 125:126]
        ncol = small[:, 128:132]
        cGp_s = small[:, 132:136]
        cGe_s = small[:, 136:140]
        dq = small[:, 152:156]
        decbc = small[:, 160:224].rearrange("p (h j) -> p h j", h=4)

        def gs(i, n=128):
            return gt[:, i, 0:n]
        S_IG, S_FG, S_T, S_B, S_M, S_CAR, S_ONE, S_DG, S_RM, S_RA, S_GP, S_GE, S_AC, S_GC, S_MC, S_NR = range(16)

        PSMAP = {"A": {0: 0, 3: 1, 4: 0, 5: 2, 7: 2, 1: 1, 2: 0},
                 "B": {3: 3, 4: 4, 5: 5, 6: 6, 1: 7, 2: 4, 7: 7, 0: 3},
                 "C": {0: 3, 1: 5, 2: 6}}
        phase = ["A"]
        par = [0]

        class _PS:
            def __getitem__(self, old):
                return psum[PSMAP[phase[0]][old]]
        PS = _PS()
        DBL = set(["qk%d" % c for c in range(8)] + ["v_ext", "o_sig", "g_ig", "g_fg", "g_t", "g_B", "g_M", "cols", "cAp",
                                                    "d1", "winter", "floor", "d2", "wk", "d3", "dec"])
        SALIAS = {"g_gp": ["g_ig@1"], "g_ge": ["g_fg@1"], "g_ac": ["g_t@1"], "g_gc": ["g_B@1"], "g_mc": ["g_M@1"],
                  "C0s0": ["qk%d@1" % c for c in range(8)], "C0s1": ["o_sig@1"]}

        def mixer_keymap(keys, is_read):
            keys = list(keys)
            nobar = len(keys) > 0 and keys[0] == "__nobar__"
            if nobar:
                keys = keys[1:]
            out = ["BAR1"] if (is_read and not nobar) else []
            for k_ in keys:
                if len(k_) >= 2 and k_[0] == "b" and k_[1] in "01234567":
                    out.append("b%d" % PSMAP[phase[0]][int(k_[1])])
                elif k_ in DBL:
                    out.append("%s@%d" % (k_, par[0]))
                elif k_ in SALIAS:
                    out += SALIAS[k_]
                else:
                    out.append(k_)
            return out

        def head_rms_to_mixbf(off):
            for h in range(4):
                P.op("act", lambda e, h=h: e.activation(out=junk[:, 0:128], in_=mixf[:, h * 128:(h + 1) * 128], func=AF.Square,
                                                        accum_out=ssq[:, h:h + 1]), ["mixf"], ["ssq%d" % h, "junk"])
            P.op("dve", lambda e: e.tensor_scalar(out=rr, in0=ssq, scalar1=1.0 / 128.0, scalar2=LN_EPS, op0=ALU.mult, op1=ALU.add),
                 ["ssq%d" % h for h in range(4)], ["rr"])
            P.op("act", lambda e: e.activation(out=rr, in_=rr, func=AF.Ln), ["rr"], ["rr"])
            P.op("act", lambda e: e.activation(out=rr, in_=rr, func=AF.Exp, scale=-0.5), ["rr"], ["rr"])
            for h in range(4):
                P.op("dve", lambda e, h=h: e.tensor_scalar(out=mix_bf[:, off + h * 128:off + (h + 1) * 128],
                                                           in0=mixf[:, h * 128:(h + 1) * 128], scalar1=rr[:, h:h + 1], scalar2=None,
                                                           op0=ALU.mult), ["mixf", "rr"], ["mix_bf%d" % (off // 512)])

        def zblock(i, col0, bank):
            key = "b%d" % bank
            for kc in range(8):
                P.op("pe", lambda e, kc=kc: e.matmul(PS[bank][:, :], x1T[:, kc, :], w_in_sb[:, kc, col0:col0 + 512],
                                                     start=(kc == 0), stop=False), ["x1Th0", "x1Th1", "w_in%d" % kc], [key], sig=False)
            P.op("pe", lambda e: e.matmul(PS[bank][:, :], selb[:, i, :], b4[:, :], start=False, stop=True),
                 ["selb", "bias4"], [key])

        MIXSTOP = int(_os.environ.get("MIXSTOP", "99"))
        MIXTILES = int(_os.environ.get("MIXTILES", "99"))

        def mixer_tile(t):
            sample = (t == 16)
            k = "acc%d" % t
            a = acc[:, t, :]
            pp = 0 if sample else (t % 2)
            par[0] = pp
            phase[0] = "A"
            go = 0 if sample else 10 * pp
            co = 184 * pp
            SC = lambda lo, hi: small[:, lo + co:hi + co]
            cA, cG, cM, cAp = SC(72, 76), SC(76, 80), SC(80, 84), SC(84, 88)
            d1, winter, floor_, d2, wk, d3, dec = SC(96, 100), SC(100, 104), SC(104, 108), SC(108, 112), SC(112, 116), SC(116, 120), SC(120, 124)
            qkT_bf, v_ext, o_sig = qkT_bf2[pp], v_ext2[pp], o_sig2[pp]
            n_g = 144 if sample else 128
            transpose_tile_to(t, x1T, "x1T", banks=(0, 1), bkeys=(("b0",), ("b1",)))
            XK = ["x1Th0", "x1Th1"]
            if MIXSTOP <= 1:
                return
            for gi, col in enumerate([3072, 3076]):
                for kc in range(8):
                    P.op("pe", lambda e, gi=gi, col=col, kc=kc: e.matmul(PS[5][0:4, gi * 128:(gi + 1) * 128],
                                                                         w_in_sb[:, kc, col:col + 4], x1T[:, kc, :],
                                                                         start=(kc == 0), stop=(kc == 7)),
                         XK + ["w_in%d" % kc], ["b5a" if gi == 0 else "b5b"], sig=(kc == 7))
            if not sample:
                ig = gs(S_IG + go); fg = gs(S_FG + go); tm = gs(S_T + go); Bt = gs(S_B + go); Mt = gs(S_M + go)
                P.op("act", lambda e: e.activation(out=ig, in_=PS[5][0:4, 0:128], func=AF.Identity, bias=bgate[:, 0:1], scale=1.0),
                     ["b5a", "bgate"], ["g_ig"])
                P.op("act", lambda e: e.activation(out=fg, in_=PS[5][0:4, 128:256], func=AF.Identity, bias=bgate[:, 1:2], scale=1.0),
                     ["b5b", "bgate"], ["g_fg"])
            else:
                ig = gs(S_IG, 144); fg = gs(S_FG, 144); tm = gs(S_T, 144); Bt = gs(S_B, 144); Mt = gs(S_M, 144)
                v3 = lambda ap: ap.rearrange("p (j r) -> p j r", j=16)
                P.op("pool", lambda e: e.memset(ig, 0.0), (), ["g_ig"])
                P.op("pool", lambda e: e.memset(fg, 0.0), (), ["g_fg"])
                P.op("act", lambda e: e.activation(out=v3(ig)[:, :, 1:9], in_=PS[5][0:4, 0:128].rearrange("p (j r) -> p j r", j=16),
                                                   func=AF.Identity, bias=bgate[:, 0:1], scale=1.0), ["b5a", "bgate", "g_ig"], ["g_ig"])
                P.op("act", lambda e: e.activation(out=v3(fg)[:, :, 1:9], in_=PS[5][0:4, 128:256].rearrange("p (j r) -> p j r", j=16),
                                                   func=AF.Identity, bias=bgate[:, 1:2], scale=1.0), ["b5b", "bgate", "g_fg"], ["g_fg"])
            P.op("dve", lambda e: e.scalar_tensor_tensor(out=tm, in0=fg, scalar=-1.0, in1=fg, op0=ALU.mult, op1=ALU.max), ["g_fg"], ["g_t"])
            P.op("act", lambda e: e.activation(out=tm, in_=tm, func=AF.Exp, scale=-1.0), ["g_t"], ["g_t"])
            P.op("act", lambda e: e.activation(out=tm, in_=tm, func=AF.Ln, bias=1.0, scale=1.0), ["g_t"], ["g_t"])
            P.op("dve", lambda e: e.scalar_tensor_tensor(out=fg, in0=fg, scalar=0.0, in1=tm, op0=ALU.min, op1=ALU.subtract),
                 ["g_fg", "g_t"], ["g_fg"])
            Gt = tm
            if not sample:
                P.op("dve", lambda e: e.tensor_tensor_scan(out=Bt, data0=gs(S_ONE), data1=fg, initial=gt[:, S_CAR, 0:1],
                                                           op0=ALU.mult, op1=ALU.add), ["g_fg", "g_car", "g_one"], ["g_B"])
                P.op("dve", lambda e: e.tensor_tensor(out=ig, in0=ig, in1=Bt, op=ALU.subtract), ["g_ig", "g_B"], ["g_ig"])
                P.op("dve", lambda e: e.tensor_tensor_scan(out=Gt, data0=gs(S_ONE), data1=ig, initial=gt[:, S_CAR, 1:2],
                                                           op0=ALU.mult, op1=ALU.max), ["g_ig", "g_car", "g_one", "g_t"], ["g_t"])
                P.op("dve", lambda e: e.tensor_tensor(out=Mt, in0=Gt, in1=Bt, op=ALU.add), ["g_t", "g_B"], ["g_M"])
                Ac, Gc, Mc = ig, Gt, Mt
                G_end = Gt[:, 127:128]
            else:
                v3 = lambda ap: ap.rearrange("p (j r) -> p j r", j=16)
                P.op("pool", lambda e: e.memset(v3(fg)[:, :, 0:1], 0.0), ["g_fg"], ["g_fg"])
                P.op("dve", lambda e: e.tensor_tensor_scan(out=Bt, data0=gs(S_RM, 144), data1=fg, initial=0.0,
                                                           op0=ALU.mult, op1=ALU.add), ["g_fg", "g_rm"], ["g_B"])
                P.op("dve", lambda e: e.tensor_tensor(out=ig, in0=ig, in1=Bt, op=ALU.subtract), ["g_ig", "g_B"], ["g_ig"])
                P.op("pool", lambda e: e.tensor_copy(out=v3(ig)[:, :, 0:1], in_=gt[:, S_NR, 128:144].rearrange("p (j o) -> p j o", o=1)),
                     ["g_ig", "g_m0"], ["g_ig"])
                P.op("dve", lambda e: e.tensor_tensor_scan(out=Gt, data0=gs(S_RA, 144), data1=ig, initial=0.0,
                                                           op0=ALU.add, op1=ALU.max), ["g_ig", "g_ra", "g_t"], ["g_t"])
                P.op("dve", lambda e: e.tensor_tensor(out=Mt, in0=Gt, in1=Bt, op=ALU.add), ["g_t", "g_B"], ["g_M"])
                Ac, Gc, Mc = gs(S_AC), gs(S_GC), gs(S_MC)
                c3 = lambda ap: ap.rearrange("p (j r) -> p j r", j=16)
                P.op("pool", lambda e: e.tensor_copy(out=c3(Ac), in_=v3(ig)[:, :, 1:9]), ["g_ig"], ["g_ac"])
                P.op("pool", lambda e: e.tensor_copy(out=c3(Gc), in_=v3(Gt)[:, :, 1:9]), ["g_t"], ["g_gc"])
                P.op("pool", lambda e: e.tensor_copy(out=c3(Mc), in_=v3(Mt)[:, :, 1:9]), ["g_M"], ["g_mc"])
                for r_ in range(8):
                    P.op("pool", lambda e, r_=r_: e.tensor_copy(out=c3(gs(S_GP))[:, :, r_:r_ + 1], in_=v3(Gt)[:, :, 0:1]), ["g_t"], ["g_gp"])
                    P.op("pool", lambda e, r_=r_: e.tensor_copy(out=c3(gs(S_GE))[:, :, r_:r_ + 1], in_=v3(Gt)[:, :, 8:9]), ["g_t"], ["g_ge"])
            AK = "g_ac" if sample else "g_ig"
            GK = "g_gc" if sample else "g_t"
            MK = "g_mc" if sample else "g_M"
            if MIXSTOP <= 2:
                return
            P.op("pe", lambda e: e.transpose(out=PS[7][:, 258:262], in_=Ac, identity=ident_f[0:4, 0:4]), [AK, "ident_f"], ["b7c"], sig=False)
            P.op("pe", lambda e: e.transpose(out=PS[7][:, 262:266], in_=Gc, identity=ident_f[0:4, 0:4]), [GK, "ident_f"], ["b7c"], sig=False)
            P.op("pe", lambda e: e.transpose(out=PS[7][:, 266:270], in_=Mc, identity=ident_f[0:4, 0:4]), [MK, "ident_f"], ["b7c"])
            P.op("dve", lambda e: e.tensor_copy(out=SC(72, 84), in_=PS[7][:, 258:270]), ["b7c"], ["cols"])
            P.op("dve", lambda e: e.tensor_scalar(out=cAp, in0=cA, scalar1=LNC, scalar2=None, op0=ALU.add), ["cols"], ["cAp"])
            if not sample:
                P.op("dve", lambda e: e.tensor_scalar(out=gt[:, S_DG, 0:4], in0=ident_f[0:4, 0:4], scalar1=G_end, scalar2=None, op0=ALU.mult),
                     ["g_t", "ident_f"], ["g_dg"])
                P.op("pe", lambda e: e.matmul(PS[7][:, 270:274], gs(S_ONE), gt[:, S_DG, 0:4], start=True, stop=True),
                     ["g_one", "g_dg"], ["b7d"])
                P.op("dve", lambda e: e.tensor_copy(out=Gend_bc, in_=PS[7][:, 270:274]), ["b7d"], ["Gend_bc"])
                gp_col, ge_col = Gprev_bc, Gend_bc
                GPK, GEK = "Gprev_bc", "Gend_bc"
            else:
                P.op("pe", lambda e: e.transpose(out=PS[7][:, 270:274], in_=gs(S_GP), identity=ident_f[0:4, 0:4]), ["g_gp", "ident_f"], ["b7d"], sig=False)
                P.op("pe", lambda e: e.transpose(out=PS[7][:, 274:278], in_=gs(S_GE), identity=ident_f[0:4, 0:4]), ["g_ge", "ident_f"], ["b7d"])
                P.op("dve", lambda e: e.tensor_copy(out=small[:, 132:140], in_=PS[7][:, 270:278]), ["b7d"], ["cols_s"])
                gp_col, ge_col = cGp_s, cGe_s
                GPK, GEK = "cols_s", "cols_s"
            P.op("dve", lambda e: e.tensor_tensor(out=d1, in0=gp_col, in1=cG, op=ALU.subtract), [GPK, "cols"], ["d1"])
            P.op("act", lambda e: e.activation(out=winter, in_=d1, func=AF.Exp), ["d1"], ["winter"])
            P.op("act", lambda e: e.activation(out=floor_, in_=cM, func=AF.Exp, scale=-1.0), ["cols"], ["floor"])
            P.op("dve", lambda e: e.tensor_tensor(out=d2, in0=cAp, in1=ge_col, op=ALU.subtract), ["cAp", GEK], ["d2"])
            P.op("act", lambda e: e.activation(out=wk, in_=d2, func=AF.Exp), ["d2"], ["wk"])
            if not sample:
                P.op("dve", lambda e: e.tensor_tensor(out=d3, in0=Gprev_bc, in1=Gend_bc, op=ALU.subtract), ["Gprev_bc", "Gend_bc"], ["d3"])
                P.op("act", lambda e: e.activation(out=dec, in_=d3, func=AF.Exp), ["d3"], ["dec"])
                P.op("dve", lambda e: e.tensor_copy(out=gt[:, S_CAR, 0:1], in_=Bt[:, 127:128]), ["g_B", "g_car"], ["g_car"])
                P.op("dve", lambda e: e.tensor_copy(out=gt[:, S_CAR, 1:2], in_=Gt[:, 127:128]), ["g_t", "g_car"], ["g_car"])
                P.op("dve", lambda e: e.tensor_copy(out=Gprev_bc, in_=Gend_bc), ["Gend_bc", "Gprev_bc"], ["Gprev_bc"])
            else:
                v3 = lambda ap: ap.rearrange("p (j r) -> p j r", j=16)
                drow = gt[:, S_DG, 16:32]
                P.op("dve", lambda e: e.tensor_tensor(out=drow.rearrange("p (j o) -> p j o", o=1), in0=v3(Gt)[:, :, 0:1], in1=v3(Gt)[:, :, 8:9],
                                                      op=ALU.subtract), ["g_t"], ["g_dg"])
                P.op("act", lambda e: e.activation(out=drow, in_=drow, func=AF.Exp), ["g_dg"], ["g_dg"])
                for h in range(4):
                    P.op("pe", lambda e, h=h: e.matmul(PS[7][:, 278 + 16 * h:278 + 16 * (h + 1)], sel[:, h, :], drow, start=True, stop=True),
                         ["sel", "g_dg"], ["b7e"], sig=(h == 3))
                P.op("dve", lambda e: e.tensor_copy(out=small[:, 160:224], in_=PS[7][:, 278:342]), ["b7e"], ["decbc"])

            if MIXSTOP <= 3:
                return
            want_f32 = (t == 15)
            for cc in range(8):
                bank = 3 + cc % 2
                col = (cc // 2) * 128
                key = "b%d" % bank
                for kc in range(8):
                    P.op("pe", lambda e, bank=bank, col=col, cc=cc, kc=kc: e.matmul(
                        PS[bank][:, col:col + 128], w_in_sb[:, kc, 1024 + cc * 128:1024 + (cc + 1) * 128], x1T[:, kc, :],
                        start=(kc == 0), stop=(kc == 7)), XK + ["w_in%d" % kc], [key], sig=(kc == 7))
                if not sample:
                    P.op("act", lambda e, bank=bank, col=col, cc=cc: e.activation(
                        out=ext_bf[:, cc, 3:131], in_=PS[bank][:, col:col + 128], func=AF.Identity, bias=colsC[:, cc:cc + 1], scale=1.0),
                        [key, "colsC"], ["extb%d" % cc])
                    if want_f32:
                        P.op("act", lambda e, bank=bank, col=col, cc=cc: e.activation(
                            out=ext[:, cc, 3:131], in_=PS[bank][:, col:col + 128], func=AF.Identity, bias=colsC[:, cc:cc + 1], scale=1.0),
                            [key, "colsC"], ["ext%d" % cc])
                    cs = 342 if cc % 2 == 0 else 128
                    for j in range(4):
                        P.op("pe", lambda e, cs=cs, cc=cc, j=j: e.matmul(
                            PS[5][:, cs:cs + 128], Dg[:, j * 8 + cc, :], ext_bf[:, cc, j:j + 128], start=(j == 0), stop=(j == 3)),
                            ["lnp", "extb%d" % cc, "extcb"], ["b5"], sig=(j == 3))
                    P.op("act", lambda e, cs=cs, cc=cc: e.activation(out=qkT_bf[:, cc, :], in_=PS[5][:, cs:cs + 128], func=AF.Silu,
                                                                     bias=colsC[:, 8 + cc:9 + cc], scale=1.0), ["b5", "colsC"], ["qk%d" % cc])
                else:
                    P.op("act", lambda e, bank=bank, col=col, cc=cc: e.activation(
                        out=ext_s[:, cc, :, 3:11], in_=PS[bank][:, col:col + 128].rearrange("p (j r) -> p j r", j=16),
                        func=AF.Identity, bias=colsC[:, cc:cc + 1], scale=1.0), [key, "colsC", "extc"], ["ext%d" % cc])
            if sample:
                for cc in range(8):
                    ca = Dm[cc % 2]
                    ck = "Dm%d" % (cc % 2)
                    src = lambda j, cc=cc: ext_s[:, cc, :, j:j + 8]
                    cav = ca[:, :].rearrange("p (j r) -> p j r", j=16)
                    P.op("dve", lambda e, cc=cc, src=src, cav=cav: e.tensor_scalar(out=cav, in0=src(0), scalar1=colsC[:, 16 + cc:17 + cc],
                                                                                 scalar2=None, op0=ALU.mult),
                         ["ext%d" % cc, "extc", "colsC"], [ck])
                    for j in range(1, 4):
                        P.op("dve", lambda e, cc=cc, j=j, src=src, cav=cav: e.scalar_tensor_tensor(
                            out=cav, in0=src(j), scalar=colsC[:, 16 + j * 8 + cc:17 + j * 8 + cc], in1=cav, op0=ALU.mult, op1=ALU.add),
                            ["ext%d" % cc, "extc", ck], [ck])
                    P.op("act", lambda e, cc=cc, ca=ca: e.activation(out=qkT_bf[:, cc, :], in_=ca[:, :], func=AF.Silu,
                                                                     bias=colsC[:, 8 + cc:9 + cc], scale=1.0), [ck, "colsC"], ["qk%d" % cc])
                    if cc < 4:
                        P.op("act", lambda e, cc=cc, ca=ca: e.activation(out=qTf[:, cc, :], in_=ca[:, :], func=AF.Silu,
                                                                         bias=colsC[:, 8 + cc:9 + cc], scale=1.0), [ck, "colsC"], ["qTf%d" % cc])
            EXK = ["ext%d" % cc for cc in range(8)]
            if t == 15:
                for cc in range(8):
                    bank = 1 + cc // 4
                    P.op("pe", lambda e, cc=cc, bank=bank: e.transpose(out=PS[bank][0:3, (cc % 4) * 128:(cc % 4 + 1) * 128],
                                                                       in_=ext[:, cc, 128:131], identity=ident_f[:]),
                         ["ext%d" % cc, "ident_f"], ["b%d" % bank], sig=(cc % 4 == 3))
                P.op("dve", lambda e: e.tensor_copy(out=mixf[0:3, 0:512], in_=PS[1][0:3, :]), ["b1"], ["mixf"])
                P.op("act", lambda e: e.activation(out=vn[0:3, 0:512], in_=PS[2][0:3, :], func=AF.Copy), ["b2"], ["vn"])
                P.dma("sp", lambda e: e.dma_start(out=conv_p[:, 0:512], in_=mixf[0:3, 0:512]), ["mixf"], [], out=True)
                P.dma("sp", lambda e: e.dma_start(out=conv_p[:, 512:1024], in_=vn[0:3, 0:512]), ["vn"], [], out=True)
            if not sample and t < 15:
                P.op("pool", lambda e: e.tensor_copy(out=ext_bf[:, :, 0:3], in_=ext_bf[:, :, 128:131]),
                     ["extb%d" % c_ for c_ in range(8)] + ["extcb"], ["extcb"])
            if sample:
                P.op("pool", lambda e: e.tensor_copy(out=convc[:, :, :].rearrange("p a (j r) -> p a j r", j=16), in_=ext_s[:, :, :, 8:11]),
                     EXK, ["convc"])
                for cc in range(8):
                    bank = 1 + cc // 4
                    P.op("pe", lambda e, cc=cc, bank=bank: e.transpose(out=PS[bank][0:48, (cc % 4) * 128:(cc % 4 + 1) * 128],
                                                                       in_=convc[:, cc, :], identity=ident_f[:]),
                         ["convc", "ident_f"], ["b%d" % bank], sig=(cc % 4 == 3))
                P.op("dve", lambda e: e.tensor_copy(out=mixf[0:48, 0:512], in_=PS[1][0:48, :]), ["b1"], ["mixf"])
                P.op("act", lambda e: e.activation(out=vn[0:48, 0:512], in_=PS[2][0:48, :], func=AF.Copy), ["b2"], ["vn"])
                P.dma("sp", lambda e: e.dma_start(out=conv_s[:, 0:512], in_=mixf[0:48, 0:512]), ["mixf"], [], out=True)
                P.dma("sp", lambda e: e.dma_start(out=conv_s[:, 512:1024], in_=vn[0:48, 0:512]), ["vn"], [], out=True)

            if MIXSTOP <= 4:
                return
            zblock(1, 512, 1)
            P.op("act", lambda e: e.activation(out=vn[:, :], in_=PS[1][:, :], func=AF.Gelu_apprx_tanh), ["b1"], ["vn"])
            zblock(0, 0, 2)
            P.op("act", lambda e: e.activation(out=u_act[:, :], in_=PS[2][:, :], func=AF.Gelu_apprx_tanh), ["b2"], ["u_act"])
            for h in range(4):
                P.op("dve", lambda e, h=h: e.bn_stats(out=st4[:, h, :], in_=vn[:, h * 128:(h + 1) * 128]), ["vn"], ["st4_%d" % h])
                P.op("dve", lambda e, h=h: e.bn_aggr(out=mv4[:, h, :], in_=st4[:, h, :]), ["st4_%d" % h], ["mv4_%d" % h])
            MVK = ["mv4_%d" % h for h in range(4)]
            P.op("act", lambda e: e.activation(out=sd4, in_=mv4[:, :, 1], func=AF.Ln, bias=LN_EPS, scale=1.0), MVK, ["sd4"])
            P.op("act", lambda e: e.activation(out=rstd4, in_=sd4, func=AF.Exp, scale=-0.5), ["sd4"], ["rstd4"])
            P.op("dve", lambda e: e.scalar_tensor_tensor(out=nmr4, in0=mv4[:, :, 0], scalar=-1.0, in1=rstd4, op0=ALU.mult, op1=ALU.mult),
                 MVK + ["rstd4"], ["nmr4"])
            for h in range(4):
                P.op("act", lambda e, h=h: e.activation(out=vn[:, h * 128:(h + 1) * 128], in_=vn[:, h * 128:(h + 1) * 128], func=AF.Identity,
                                                        bias=nmr4[:, h:h + 1], scale=rstd4[:, h:h + 1]), ["vn", "rstd4", "nmr4"], ["vn"])
            P.op("dve", lambda e: e.tensor_tensor(out=vn[:, :], in0=vn[:, :], in1=gmln[:, 0, :], op=ALU.mult), ["vn", "gmln"], ["vn"])
            P.op("dve", lambda e: e.tensor_tensor(out=vn[:, :], in0=vn[:, :], in1=gmln[:, 1, :], op=ALU.add), ["vn", "gmln"], ["vn"])
            P.op("act", lambda e: e.activation(out=vn_bf[:, :], in_=vn[:, :], func=AF.Copy), ["vn"], ["vn_bf"])
            if t == 15:
                P.dma("sp", lambda e: e.dma_start(out=gmv_p[:, :], in_=vn[:, :]), ["vn"], [], out=True)
            if sample:
                P.dma("sp", lambda e: e.dma_start(out=gmv_s[:, :], in_=vn[:, :]), ["vn"], [], out=True)
            zblock(2, 2048, 1)
            P.op("act", lambda e: e.activation(out=v_ext[:, :, 0:128], in_=PS[1][:, :].rearrange("p (a b) -> p a b", a=4), func=AF.Copy),
                 ["b1"], ["v_ext"])
            zblock(3, 2560, 2)
            P.op("act", lambda e: e.activation(out=o_sig[:, :], in_=PS[2][:, :], func=AF.Sigmoid), ["b2"], ["o_sig"])

            if MIXSTOP <= 5:
                return
            phase[0] = "B"
            wsx = wsT_s if sample else wsT
            wsk = "wsT_s" if sample else "wsT"
            bsc = 52 if sample else 48
            for h in range(4):
                P.op("pe", lambda e, h=h: e.matmul(PS[3][:, h * 128:(h + 1) * 128], wsx[:, h, :], vn_bf[:, h * 128:(h + 1) * 128],
                                                   start=True, stop=True), [wsk, "vn_bf"], ["b3_%d" % h])
                P.op("dve", lambda e, h=h: e.scalar_tensor_tensor(out=mixf[:, h * 128:(h + 1) * 128], in0=PS[3][:, h * 128:(h + 1) * 128],
                                                                  scalar=colsC[:, bsc + h:bsc + h + 1], in1=u_act[:, h * 128:(h + 1) * 128],
                                                                  op0=ALU.add, op1=ALU.mult), ["b3_%d" % h, "u_act", "colsC"], ["mixf"])
            head_rms_to_mixbf(0)

            if MIXSTOP <= 6:
                return
            ps4_bf = PS[4][:, 0:256].bitcast(BF16)
            for h in range(4):
                P.op("pe", lambda e, h=h: e.transpose(out=ps4_bf[:, h * 128:(h + 1) * 128], in_=qkT_bf[:, 4 + h, :], identity=ident_bf[:]),
                     ["qk%d" % (4 + h), "ident_bf"], ["b4_0", "b4_1"], sig=(h == 3))
            for h in range(4):
                P.op("dve", lambda e, h=h: e.tensor_scalar(out=kw_bf[:, h, :], in0=ps4_bf[:, h * 128:(h + 1) * 128], scalar1=wk[:, h:h + 1],
                                                           scalar2=None, op0=ALU.mult), ["b4_0", "b4_1", "wk"], ["kw%d" % h])
            nm = negmask_s if sample else negmask
            if sample:
                for h in range(4):
                    P.op("pe", lambda e, h=h: e.transpose(out=PS[4][:, h * 128:(h + 1) * 128], in_=qTf[:, h, :], identity=ident_f[:]),
                         ["qTf%d" % h, "ident_f"], ["b4_%d" % h], sig=(h == 3))
                B4K = ["b4_%d" % h for h in range(4)]
                P.op("act", lambda e: e.activation(out=u_act[:, :], in_=PS[4][:, :], func=AF.Copy), B4K, ["u_act"])
                for r_ in range(8):
                    P.dma("sp", lambda e, r_=r_: e.dma_start(out=mixf[r_:128:8, :], in_=sn.rearrange("(j h) d -> j (h d)", h=4)),
                          [], ["mixf"])
                for h in range(4):
                    P.op("dve", lambda e, h=h: e.scalar_tensor_tensor(out=junk[:, 0:128], in0=u_act[:, h * 128:(h + 1) * 128], scalar=1.0,
                                                                      in1=mixf[:, h * 128:(h + 1) * 128], op0=ALU.mult, op1=ALU.mult,
                                                                      accum_out=dq[:, h:h + 1]), ["u_act", "mixf"], ["dq%d" % h, "junk"])
                P.dma("sp", lambda e: e.dma_start(out=n0row[0:64, :], in_=sn[:, :]), [], ["n0row"])
                P.op("pe", lambda e: e.transpose(out=PS[7][:, 342:406], in_=n0row[0:64, :], identity=ident_f[0:64, 0:64]),
                     ["n0row", "ident_f"], ["b7f"])
                P.op("dve", lambda e: e.tensor_copy(out=n0T[:, :], in_=PS[7][:, 342:406]), ["b7f"], ["n0T"])
                for j in range(16):
                    cb = C0s[j % 2]
                    ckey = "C0s%d" % (j % 2)
                    P.dma("sp", lambda e, j=j, cb=cb: e.dma_start(out=cb[:, :, :], in_=sC[j].rearrange("h d e -> d h e")), [], [ckey])
                    for h in range(4):
                        P.op("pe", lambda e, j=j, h=h, cb=cb: e.matmul(PS[3][:, h * 128 + j * 8:h * 128 + j * 8 + 8], cb[:, h, :],
                                                                       qTf[:, h, j * 8:(j + 1) * 8], start=True, stop=True),
                             [ckey, "qTf%d" % h], ["b3_%d" % h])
                    vmk = vmask[j % 2]
                    vkey = "vmask%d" % (j % 2)
                    P.op("dve", lambda e, j=j, vmk=vmk: e.tensor_scalar(out=vmk[:, :, :], in0=v_ext[:, :, :], scalar1=onehot[:, j:j + 1],
                                                                        scalar2=None, op0=ALU.mult), ["v_ext", "onehot"], [vkey])
                    for h in range(4):
                        off = 0
                        pb_ = 1 + h % 2
                        pk = "b%d" % pb_
                        P.op("pe", lambda e, h=h, off=off, vmk=vmk, pb_=pb_: e.matmul(PS[pb_][:, off:off + 129], kw_bf[:, h, :], vmk[:, h, 0:129],
                                                                            start=True, stop=True), ["kw%d" % h, vkey], [pk])
                        P.op("dve", lambda e, j=j, h=h, off=off, cb=cb, pb_=pb_: e.scalar_tensor_tensor(
                            out=cb[:, h, :], in0=cb[:, h, :], scalar=decbc[:, h, j:j + 1], in1=PS[pb_][:, off:off + 128],
                            op0=ALU.mult, op1=ALU.add), [ckey, pk, "decbc"], [ckey])
                        P.op("dve", lambda e, j=j, h=h, off=off, pb_=pb_: e.scalar_tensor_tensor(
                            out=nnewT[:, j * 4 + h:j * 4 + h + 1], in0=n0T[:, j * 4 + h:j * 4 + h + 1], scalar=decbc[:, h, j:j + 1],
                            in1=PS[pb_][:, off + 128:off + 129], op0=ALU.mult, op1=ALU.add), ["n0T", pk, "decbc"], ["nnewT"])
                    P.dma("sp", lambda e, j=j, cb=cb: e.dma_start(out=C_s[j].rearrange("h d e -> d h e"), in_=cb[:, :, :]), [ckey], [], out=True)
                P.op("pe", lambda e: e.transpose(out=PS[7][0:64, 342:470], in_=nnewT[:, :], identity=ident_f[:]), ["nnewT", "ident_f"], ["b7f"])
                P.op("dve", lambda e: e.tensor_copy(out=n0row[0:64, :], in_=PS[7][0:64, 342:470]), ["b7f"], ["n0row"])
                P.dma("sp", lambda e: e.dma_start(out=n_s[:, :], in_=n0row[0:64, :]), ["n0row"], [], out=True)
                B3K = ["b3_%d" % h for h in range(4)]
                P.op("act", lambda e: e.activation(out=vn[:, :], in_=PS[3][:, :], func=AF.Copy), B3K, ["vn"])
                for h in range(4):
                    P.op("pe", lambda e, h=h: e.transpose(out=PS[4][:, h * 128:(h + 1) * 128], in_=vn[:, h * 128:(h + 1) * 128], identity=ident_f[:]),
                         ["vn", "ident_f"], ["b4_%d" % h])

            for h in range(4):
                s_ = h % 2
                bank = 5 + s_
                ka, kb, kc_ = "b%da" % bank, "b%db" % bank, "b%dc" % bank
                P.op("pe", lambda e, h=h, bank=bank: e.matmul(PS[bank][:, 0:128], sel[:, h, :], Gc, start=True, stop=True),
                     ["sel", GK], [ka])
                P.op("dve", lambda e, s_=s_, bank=bank: e.scalar_tensor_tensor(out=Dm[s_][:, :], in0=PS[bank][:, 0:128], scalar=-1.0,
                                                                               in1=nm[:, :], op0=ALU.mult, op1=ALU.add),
                     [ka, "negmask", "negmask_s"], ["Dm%d" % s_])
                P.op("act", lambda e, s_=s_, h=h: e.activation(out=Dm[s_][:, :], in_=Dm[s_][:, :], func=AF.Exp, bias=cAp[:, h:h + 1], scale=1.0),
                     ["Dm%d" % s_, "cAp"], ["Dm%d" % s_])
                P.op("pe", lambda e, h=h, bank=bank: e.matmul(PS[bank][:, 128:256], qkT_bf[:, 4 + h, :], qkT_bf[:, h, :], start=True, stop=True),
                     ["qk%d" % h, "qk%d" % (4 + h)], [kb])
                P.op("dve", lambda e, s_=s_, bank=bank: e.tensor_tensor(out=Pb[s_][:, :], in0=PS[bank][:, 128:256], in1=Dm[s_][:, :], op=ALU.mult),
                     [kb, "Dm%d" % s_], ["Pb%d" % s_])
                P.op("pe", lambda e, h=h, s_=s_, bank=bank: e.matmul(PS[bank][:, 256:385], Pb[s_][:, :], v_ext[:, h, 0:129], start=True, stop=True),
                     ["Pb%d" % s_, "v_ext"], [kc_])
                if not sample:
                    P.op("pe", lambda e, h=h: e.matmul(PS[1][:, 0:129], qkT_bf[:, h, :], Cbf[:, h, 0:129], start=True, stop=True),
                         ["qk%d" % h, "Cbf%d" % h], ["b1"])
                    P.op("act", lambda e, h=h: e.activation(out=itw[:, 0:129], in_=PS[1][:, 0:129], func=AF.Identity, scale=winter[:, h:h + 1]),
                         ["b1", "winter"], ["itw"])
                else:
                    P.op("act", lambda e, h=h: e.activation(out=itw[:, 0:128], in_=PS[4][:, h * 128:(h + 1) * 128], func=AF.Identity,
                                                            scale=winter[:, h:h + 1]), ["b4_%d" % h, "winter"], ["itw"])
                    P.op("dve", lambda e, h=h: e.tensor_tensor(out=itw[:, 128:129], in0=dq[:, h:h + 1], in1=winter[:, h:h + 1], op=ALU.mult),
                         ["dq%d" % h, "winter", "itw"], ["itw"])
                P.op("dve", lambda e, bank=bank: e.tensor_tensor(out=itw[:, 0:129], in0=PS[bank][:, 256:385], in1=itw[:, 0:129], op=ALU.add),
                     [kc_, "itw"], ["itw"])
                P.op("dve", lambda e: e.scalar_tensor_tensor(out=dd, in0=itw[:, 128:129], scalar=-1.0, in1=itw[:, 128:129],
                                                             op0=ALU.mult, op1=ALU.max), ["itw"], ["dd"])
                P.op("dve", lambda e, h=h: e.tensor_tensor(out=dd, in0=dd, in1=floor_[:, h:h + 1], op=ALU.max), ["dd", "floor"], ["dd"])
                P.op("dve", lambda e: e.reciprocal(out=rd, in_=dd), ["dd"], ["rd"])
                P.op("dve", lambda e, h=h: e.scalar_tensor_tensor(out=mixf[:, h * 128:(h + 1) * 128], in0=itw[:, 0:128], scalar=rd,
                                                                  in1=o_sig[:, h * 128:(h + 1) * 128], op0=ALU.mult, op1=ALU.mult),
                     ["itw", "rd", "o_sig", "mix_bf0"], ["mixf"])
                if not sample:
                    P.op("pe", lambda e, h=h: e.matmul(PS[2][:, 0:129], kw_bf[:, h, :], v_ext[:, h, 0:129], start=True, stop=True),
                         ["kw%d" % h, "v_ext"], ["b2"])
                    P.op("dve", lambda e, h=h: e.scalar_tensor_tensor(out=Cst[:, h, 0:129], in0=Cst[:, h, 0:129], scalar=dec[:, h:h + 1],
                                                                      in1=PS[2][:, 0:129], op0=ALU.mult, op1=ALU.add),
                         ["Cst%d" % h, "dec", "b2"], ["Cst%d" % h])
                    P.op("act", lambda e, h=h: e.activation(out=Cbf[:, h, 0:129], in_=Cst[:, h, 0:129], func=AF.Copy), ["Cst%d" % h], ["Cbf%d" % h])
            head_rms_to_mixbf(512)

            if MIXSTOP <= 7:
                return
            if t == 15:
                P.dma("sp", lambda e: e.dma_start(out=m_p[:, :], in_=Mt[:, 127:128]), ["g_M"], [], out=True)
                CK = ["Cst%d" % h for h in range(4)]
                P.dma("sp", lambda e: e.dma_start(out=C_p.rearrange("h d e -> d h e"), in_=Cst[:, :, 0:128]), CK, [], out=True)
                P.op("pool", lambda e: e.tensor_copy(out=ncol, in_=Cst[:, :, 128]), CK, ["ncol"])
                P.op("pe", lambda e: e.transpose(out=PS[7][0:4, 342:470], in_=ncol, identity=ident_f[:]), ["ncol", "ident_f"], ["b7f"])
                P.op("dve", lambda e: e.tensor_copy(out=gs(S_NR), in_=PS[7][0:4, 342:470]), ["b7f"], ["g_nr"])
                P.dma("sp", lambda e: e.dma_start(out=n_p[:, :], in_=gs(S_NR)), ["g_nr"], [], out=True)
            if sample:
                v3 = lambda ap: ap.rearrange("p (j r) -> p j r", j=16)
                P.op("pool", lambda e: e.tensor_copy(out=gt[:, S_DG, 32:48].rearrange("p (j o) -> p j o", o=1), in_=v3(Mt)[:, :, 8:9]), ["g_M", "g_dg"], ["g_dg"])
                P.dma("sp", lambda e: e.dma_start(out=m_s.rearrange("j h -> h j"), in_=gt[:, S_DG, 32:48], allow_slow_non_contiguous=True),
                      ["g_dg"], [], out=True)

            if MIXSTOP <= 8:
                return
            phase[0] = "C"
            for kc in range(8):
                bank = 0
                P.op("pe", lambda e, kc=kc: e.transpose(out=PS[0][:, :].bitcast(BF16)[:, kc * 128:(kc + 1) * 128],
                                                        in_=mix_bf[:, kc * 128:(kc + 1) * 128], identity=ident_bf[:]),
                     ["mix_bf0", "mix_bf1", "ident_bf"], ["b0"], sig=(kc == 7))
            P.op("act", lambda e: e.activation(out=mixT[:, :, :], in_=PS[0][:, :].bitcast(BF16).rearrange("p (a b) -> p a b", a=8), func=AF.Copy),
                 ["b0"], ["mixT"])
            for half in range(2):
                bank = 1 + half
                for kc in range(8):
                    P.op("pe", lambda e, kc=kc, half=half, bank=bank: e.matmul(PS[bank][:, :], mixT[:, kc, :],
                                                                               w_out_sb[:, kc, half * 512:(half + 1) * 512],
                                                                               start=(kc == 0), stop=(kc == 7)),
                         ["mixT", "w_out"], ["b%d" % bank], sig=(kc == 7))
                P.op("dve", lambda e, half=half, bank=bank: e.scalar_tensor_tensor(
                    out=acc[:, t, half * 512:(half + 1) * 512], in0=PS[bank][:, :], scalar=1.0 / ALPHA,
                    in1=acc[:, t, half * 512:(half + 1) * 512], op0=ALU.mult, op1=ALU.add), [k, "b%d" % bank], [k])
            layer_norm_tile(t, affine=False)

        def mixer_stage(tiles):
            P.keymap = mixer_keymap
            phase[0] = "A"
            par[0] = 0
            win_v = wd_["w_in"].rearrange("(kc kp) f -> kp kc f", kp=128)
            AK_ = ["wg0_%d" % c for c in range(NPC)] + ["wu0_%d" % c for c in range(NPC)] + ["wd0_%d" % c for c in range(NPC)]
            BK_ = ["wg1_%d" % c for c in range(NPC)] + ["wu1_%d" % c for c in range(NPC)] + ["wd1_%d" % c for c in range(NPC)]
            HK_ = ["hT%d_%d" % (b_, c) for b_ in range(2) for c in range(NPC)] + ["sg0", "sg1"]
            for kc in range(8):
                aft = AK_ if kc < 3 else (AK_ + BK_ + HK_ if kc == 3 else BK_ + HK_)
                P.dma("pool", lambda e, kc=kc: e.dma_start(out=w_in_sb[:, kc, :], in_=win_v[:, kc, :]), r=["__nobar__"], w=["__nobar__", "w_in%d" % kc],
                      after=aft)
            P.dma("pool", lambda e: e.dma_start(out=w_out_sb[:, :, :], in_=wd_["w_out"].rearrange("(kc kp) f -> kp kc f", kp=128)),
                  r=["__nobar__"], w=["__nobar__", "w_out"], after=HK_ + ["lnjunk"])
            for kc in range(8):
                if kc % 2 == 0:
                    P.op("act", lambda e, kc=kc: e.activation(out=w_out_sb[:, kc, :], in_=w_out_sb[:, kc, :], func=AF.Identity,
                                                              scale=colsC[:, 56 + kc:57 + kc]), ["__nobar__", "w_out", "colsC"], ["__nobar__", "w_out"])
                else:
                    P.op("dve", lambda e, kc=kc: e.tensor_scalar(out=w_out_sb[:, kc, :], in0=w_out_sb[:, kc, :], scalar1=colsC[:, 56 + kc:57 + kc],
                                                                 scalar2=None, op0=ALU.mult), ["__nobar__", "w_out", "colsC"], ["__nobar__", "w_out"])
            P.op("pool", lambda e: e.memset(extb[:, :], 0.0), (), ["extc"] + ["ext%d" % c for c in range(8)])
            P.op("pool", lambda e: e.memset(extbb[:, :], 0.0), (), ["extcb"] + ["extb%d" % c for c in range(8)])
            for idx in range(32):
                P.op("dve", lambda e, idx=idx: e.tensor_scalar(out=Dg[:, idx, :], in0=ident_bf[:, :], scalar1=colsC[:, 16 + idx:17 + idx],
                                                               scalar2=None, op0=ALU.mult), ["ident_bf", "colsC", "lnp"], ["lnp"])
            for q_ in range(2):
                par[0] = q_
                P.op("pool", lambda e, q_=q_: e.memset(v_ext2[q_][:, :, :], 1.0), (), ["v_ext"])
            par[0] = 0
            P.op("pool", lambda e: e.memset(gs(S_ONE, 144), 1.0), (), ["g_one"])
            P.op("pool", lambda e: e.memset(gs(S_RM, 144), 1.0), (), ["g_rm"])
            P.op("pool", lambda e: e.memset(gs(S_RM, 144).rearrange("p (j r) -> p j r", j=16)[:, :, 0:1], 0.0), ["g_rm"], ["g_rm"])
            P.op("pool", lambda e: e.memset(gs(S_RA, 144), 0.0), (), ["g_ra"])
            P.op("pool", lambda e: e.memset(gs(S_RA, 144).rearrange("p (j r) -> p j r", j=16)[:, :, 0:1], NEG), ["g_ra"], ["g_ra"])
            P.op("pool", lambda e: e.memset(gt[:, S_CAR, 0:2], 0.0), (), ["g_car"])
            P.op("pool", lambda e: e.memset(Gprev_bc, 0.0), (), ["Gprev_bc"])
            if 16 in tiles:
                tiles = [16] + [t_ for t_ in tiles if t_ != 16]
            for t in tiles:
                if t == 0:
                    phase[0] = "A"
                    par[0] = 0
                    P.op("pool", lambda e: e.memset(extb[:, :], 0.0), ["extc"] + ["ext%d" % c for c in range(8)],
                         ["extc"] + ["ext%d" % c for c in range(8)])
                    P.op("pool", lambda e: e.memset(extbb[:, :], 0.0), ["extcb"] + ["extb%d" % c for c in range(8)], ["extcb"] + ["extb%d" % c for c in range(8)])
                if t == 16:
                    phase[0] = "A"
                    par[0] = 0
                    P.op("pool", lambda e: e.memset(extb[:, :], 0.0), ["extc"] + ["ext%d" % c for c in range(8)],
                         ["extc"] + ["ext%d" % c for c in range(8)])
                    P.op("pool", lambda e: e.memset(extbb[:, :], 0.0), ["extcb"] + ["extb%d" % c for c in range(8)], ["extcb"] + ["extb%d" % c for c in range(8)])
                    P.dma("sp", lambda e: e.dma_start(out=mixf[0:48, :], in_=sconv[:, 0:512]), [], ["mixf"])
                    P.dma("sp", lambda e: e.dma_start(out=vn[0:48, :], in_=sconv[:, 512:1024]), [], ["vn"])
                    for cc in range(8):
                        srcb = mixf if cc < 4 else vn
                        P.op("pe", lambda e, cc=cc, srcb=srcb: e.transpose(out=PS[1][:, cc * 48:(cc + 1) * 48],
                                                                           in_=srcb[0:48, (cc % 4) * 128:(cc % 4 + 1) * 128],
                                                                           identity=ident_f[0:48, 0:48]),
                             ["mixf", "vn", "ident_f"], ["b1"], sig=(cc == 7))
                    P.op("dve", lambda e: e.tensor_copy(out=ext_s[:, :, :, 0:3], in_=PS[1][:, 0:384].rearrange("p (a j r) -> p a j r", a=8, j=16)),
                         ["b1", "extc"], ["extc"])
                    P.dma("sp", lambda e: e.dma_start(out=gt[:, S_NR, 128:144], in_=sm.rearrange("j h -> h j"),
                                                      allow_slow_non_contiguous=True), [], ["g_m0"])
                mixer_tile(t)
            P.keymap = None

        if _os.environ.get("SKIPFFN1"):
            for t in range(NT):
                src = x_p[t * 128:(t + 1) * 128, :] if t < 16 else x_s[:, :]
                P.dma("sp", lambda e, t=t, src=src: e.dma_start(out=acc[:, t, :], in_=src), w=["acc%d" % t])
        else:
            load_lnp("ln1_g", "ln1_b")
            ffn_stage(1, True)

        if dbg == 1:
            dbg_o = dout("dbg", [NTOK, D])
            for t in range(NT):
                P.dma("sp", lambda e, t=t: e.dma_start(out=dbg_o[t * 128:(t + 1) * 128, :], in_=acc[:, t, :]), ["acc%d" % t], [], out=True)
        else:
            P.barrier("BAR1", lambda e: e.memset(small[:, 318:319], 0.0))
            mixer_stage([int(v) for v in _os.environ['MIXLIST'].split(',')] if _os.environ.get('MIXLIST') else [t for t in (list(range(NT)) if dbg != 2 else list(range(16))) if t < MIXTILES])
            if dbg in (2, 3):
                dbg_o = dout("dbg", [NTOK, D])
                for t in range(NT):
                    P.dma("sp", lambda e, t=t: e.dma_start(out=dbg_o[t * 128:(t + 1) * 128, :], in_=acc[:, t, :]), ["acc%d" % t], [], out=True)
            else:
                P.barrier("BAR2", lambda e: e.memset(small[:, 319:320], 0.0))
                load_lnp("ln2_g", "ln2_b")
                ffn_stage(2, False, pre_affine=True, lnp_after_p0=("ln3_g", "ln3_b"))
                for t in range(NT):
                    dst = y_p[t * 128:(t + 1) * 128, :] if t < 16 else y_s[:, :]
                    P.dma("sp", lambda e, t=t, dst=dst: e.dma_start(out=dst, in_=acc[:, t, :]), ["acc%d" % t], [], out=True)

        sems = {}
        for nm_ in P.sem_names():
            sems[nm_] = es.enter_context(nc.semaphore(nm_))
        with nc.Block() as block:
            P.emit(block, sems)
    return nc


def _prep_inputs(inputs):
    f32 = lambda a: np.ascontiguousarray(np.asarray(a, dtype=np.float32))
    shared = {
        "f1_wg": f32(inputs["ffn1_wg"][0]), "f1_wu": f32(inputs["ffn1_wu"][0]), "f1_wd": f32(inputs["ffn1_wd"][0]),
        "ln1_g": f32(inputs["ln1_g"]), "ln1_b": f32(inputs["ln1_b"]),
        "w_in": f32(inputs["w_in"][0]), "b_in": f32(inputs["b_in"]),
        "gm_ln_g": f32(inputs["gm_ln_g"]).reshape(1, 512), "gm_ln_b": f32(inputs["gm_ln_b"]).reshape(1, 512),
        "gm_ws": f32(inputs["gm_ws"][0]), "gm_bs": f32(inputs["gm_bs"][0]),
        "conv_w": f32(inputs["conv_w"][0]), "conv_b": f32(inputs["conv_b"]),
        "gm_out_g": f32(inputs["gm_out_g"]).reshape(1, 512), "ml_out_g": f32(inputs["ml_out_g"]).reshape(1, 512),
        "w_out": f32(inputs["w_out"][0]), "ln2_g": f32(inputs["ln2_g"]), "ln2_b": f32(inputs["ln2_b"]),
        "f2_wg": f32(inputs["ffn2_wg"][0]), "f2_wu": f32(inputs["ffn2_wu"][0]), "f2_wd": f32(inputs["ffn2_wd"][0]),
        "ln3_g": f32(inputs["ln3_g"]), "ln3_b": f32(inputs["ln3_b"]),
    }
    xp = f32(inputs["x_prompt"]); xs = f32(inputs["x_sample"])
    sc = f32(inputs["state_conv"][0]); sC = f32(inputs["state_C"][0]); sn = f32(inputs["state_n"][0]); sm = f32(inputs["state_m"][0])
    maps = []
    for i in range(NCORES):
        m = dict(shared)
        sl = slice(16 * i, 16 * i + 16)
        m["x_p"] = xp[i]
        m["x_s"] = np.ascontiguousarray(xs[sl].reshape(128, D))
        m["sconv"] = np.ascontiguousarray(sc[sl].reshape(48, D))
        m["sC"] = np.ascontiguousarray(sC[sl])
        m["sn"] = np.ascontiguousarray(sn[sl].reshape(64, 128))
        m["sm"] = np.ascontiguousarray(sm[sl])
        maps.append(m)
    return maps


def kernel(**inputs):
    maps = _prep_inputs(inputs)
    nc = build_program()
    res = run_bass_kernel_spmd(nc, maps, core_ids=list(range(NCORES)))
    R = res.results
    cat = lambda k: [np.asarray(r[k]) for r in R]
    y_p = np.stack(cat("y_p"), 0)
    y_s = np.concatenate(cat("y_s"), 0).reshape(128, 8, D)
    gmv_p = np.stack(cat("gmv_p"), 0).reshape(1, 8, 128, 4, 128)
    gmv_s = np.concatenate(cat("gmv_s"), 0).reshape(1, 128, 8, 4, 128)
    conv_p = np.stack(cat("conv_p"), 0).reshape(1, 8, 3, D)
    conv_s = np.concatenate(cat("conv_s"), 0).reshape(1, 128, 3, D)
    C_p = np.stack(cat("C_p"), 0).reshape(1, 8, 4, 128, 128)
    C_s = np.concatenate(cat("C_s"), 0).reshape(1, 128, 4, 128, 128)
    n_p = np.stack(cat("n_p"), 0).reshape(1, 8, 4, 128)
    n_s = np.concatenate(cat("n_s"), 0).reshape(1, 128, 4, 128)
    m_p = np.stack(cat("m_p"), 0).reshape(1, 8, 4)
    m_s = np.concatenate(cat("m_s"), 0).reshape(1, 128, 4)
    return (y_p, y_s, gmv_p, gmv_s, conv_p, conv_s, C_p, C_s, n_p, n_s, m_p, m_s)
```

```python
import math
import os as _os0
import numpy as np
import concourse.bass as bass
import concourse.mybir as mybir
from concourse.bass_utils import run_bass_kernel_spmd

F32 = mybir.dt.float32
BF16 = mybir.dt.bfloat16
AF = mybir.ActivationFunctionType
ALU = mybir.AluOpType

NCORES = 8
D = 1024
DFF = 2816
NFC = DFF // 128
NT = 17
NTOK = NT * 128
ALPHA = 2.0 ** 0.25
LN_EPS = 1e-5
LNC = math.log(128.0 ** -0.5)
NEG = -1.0e30
PASSES = [(0, 3), (3, 3), (6, 4), (10, 4), (14, 4), (18, 4)]
if _os0.environ.get("PASS_SIZES"):
    _ps = [int(v) for v in _os0.environ["PASS_SIZES"].split(",")]
    assert sum(_ps) == 22 and max(_ps) <= 4
    PASSES = [(sum(_ps[:i]), _ps[i]) for i in range(len(_ps))]
NPC = 4
GROUPS = [[0, 1, 2, 3], [4, 5, 6, 7], [8, 9, 10, 11], [12, 13, 14, 15], [16]]
NDS = 8
import os as _os0
ACT_PEN = float(_os0.environ.get('ACT_PEN', '1.0'))
XLAT = float(_os0.environ.get('XLAT', '0.3'))
CP_PRIO = int(_os0.environ.get('CP_PRIO', '0'))
JIT_SEED = int(_os0.environ.get('JIT_SEED', '0'))
JIT_AMP = float(_os0.environ.get('JIT_AMP', '0.2'))
SAME_LAT = float(_os0.environ.get('SAME_LAT', '0.06'))
PE_SCALE = float(_os0.environ.get('PE_SCALE', '1.0'))
DVE_FIX = float(_os0.environ.get('DVE_FIX', '0.15'))


class _Dummy:
    def then_inc(self, *a, **k):
        return self


class _Rec:
    def __init__(self):
        self.calls = []

    def __getattr__(self, name):
        def f(*a, **k):
            self.calls.append((name, a, k))
            return _Dummy()
        return f


def _free(ap):
    n = 1
    for d in ap.shape[1:]:
        n *= int(d)
    return n


_ACT_GROUP = {"Exp": "exp", "Ln": "exp", "Silu": "silu", "Gelu_apprx_tanh": "gelu", "Sigmoid": "sigmoid", "Sqrt": "sqrt"}


def _est(eng, name, a, k):
    if name == "matmul":
        rhs = a[2] if len(a) > 2 else k["rhs"]
        n = _free(rhs)
        m = 4.0 if rhs.dtype == F32 else 1.0
        return (0.02 + max(n, 64) * m * 0.00042) * PE_SCALE, 0.0, None
    if name == "transpose":
        in_ = k["in_"]
        return (0.02 + max(_free(in_), 64) * 0.00042) * PE_SCALE, 0.0, None
    if name == "dma_start":
        out = k["out"]
        tot = 1
        for d in out.shape:
            tot *= int(d)
        rows = max(1, tot // max(1, int(out.shape[-1])))
        esz = 2 if out.dtype == BF16 else 4
        issue = 0.1 if eng == "sp" else 0.5 + rows * 0.02
        return issue, 2.0 + tot * esz / 150e3, None
    out = k.get("out", a[0] if a else None)
    n = _free(out) if out is not None else 64
    if eng == "act":
        f = k.get("func")
        g = _ACT_GROUP.get(getattr(f, "name", str(f)).split(".")[-1]) if f is not None else None
        return 0.22 + n * 0.001 + (0.1 if k.get("accum_out") is not None else 0.0), 0.0, g
    if eng == "pool":
        return 0.3 + n * 0.0021, 0.0, None
    if name in ("bn_aggr",):
        return 0.2, 0.0, None
    return DVE_FIX + n * 0.00105 + (0.1 if k.get("accum_out") is not None else 0.0), 0.0, None


class Prog:
    ENG = ("pe", "act", "dve", "pool", "sp")
    WINDOW = int(_os0.environ.get('SWIN', '100'))
    ACT_PEN_DEFAULT = 0.0

    def __init__(self):
        self.nodes = []
        self.pending = {e: None for e in self.ENG}
        self.lastw = {}
        self.rd = {}
        self.seg = 0
        self.out_nodes = []
        self.ctx = None
        self.keymap = None
        self.last_rd_eng = {}

    @staticmethod
    def _canon(keys):
        return [k[:2] if (len(k) > 2 and k[0] == "b" and k[1] in "01234567") else k for k in keys]

    def fence(self):
        self.seg += 1

    def barrier(self, key, fn):
        node = {"eng": "pool", "fns": [fn], "kind": "op", "busy": 0.3, "lat": 0.0, "grp": None}
        nid = len(self.nodes)
        node["preds"] = set(range(nid))
        node["id"] = nid
        node["seg"] = self.seg
        self.nodes.append(node)
        self.lastw[key] = nid
        self.rd[key] = set()
        return nid

    def _add(self, node, reads, writes, after=()):
        if self.keymap is not None:
            reads = self.keymap(reads, True)
            writes = self.keymap(writes, False)
        reads = self._canon(reads)
        writes = self._canon(writes)
        nid = len(self.nodes)
        preds = set()
        for k in after:
            if k in self.lastw:
                preds.add(self.lastw[k])
            for r_ in self.rd.get(k, ()):
                preds.add(r_)
        for k in reads:
            if k in self.lastw:
                preds.add(self.lastw[k])
        for k in writes:
            if k in self.lastw:
                preds.add(self.lastw[k])
            for r_ in self.rd.get(k, ()):
                preds.add(r_)
        for k in reads:
            if (len(k) == 2 and k[0] == "b" and k[1] in "01234567") or k.startswith("ps"):
                lr = self.last_rd_eng.get(k)
                if lr is not None and lr[1] != node["eng"]:
                    preds.add(lr[0])
                self.last_rd_eng[k] = (nid, node["eng"])
        for k in writes:
            if k in self.last_rd_eng:
                self.last_rd_eng[k] = None
        preds.discard(nid)
        node["preds"] = preds
        node["id"] = nid
        node["seg"] = self.seg
        self.nodes.append(node)
        for k in reads:
            self.rd.setdefault(k, set()).add(nid)
        for k in writes:
            self.lastw[k] = nid
            self.rd[k] = set()
        return nid

    def op(self, eng, fn, reads=(), writes=(), sig=True, r=None, w=None):
        reads = list(r if r is not None else reads)
        writes = list(w if w is not None else writes)
        pend = self.pending[eng]
        if pend is None:
            pend = {"eng": eng, "fns": [], "reads": [], "writes": [], "kind": "op"}
        r0 = _Rec()
        fn(r0)
        calls0 = r0.calls
        fn = (lambda e, calls0=calls0: [getattr(e, n_)(*a_, **k_) for (n_, a_, k_) in calls0][-1])
        pend["fns"].append(fn)
        pend["reads"] += reads
        pend["writes"] += writes
        if not sig:
            self.pending[eng] = pend
            return None
        self.pending[eng] = None
        rec = _Rec()
        for f in pend["fns"]:
            f(rec)
        pend["calls"] = rec.calls
        busy = 0.0
        grp = None
        for (name, a, k) in rec.calls:
            b_, _, g_ = _est(eng, name, a, k)
            busy += b_
            grp = g_ or grp
        if JIT_SEED:
            self._rs = (getattr(self, "_rs", JIT_SEED * 7919 + 13) * 1103515245 + 12345) % 2147483648
            busy *= 1.0 + JIT_AMP * ((self._rs / 2147483648.0) - 0.5)
        pend["busy"] = busy
        pend["lat"] = 0.0
        pend["grp"] = grp
        return self._add(pend, pend["reads"], pend["writes"])

    def dma(self, qeng, fn, reads=(), writes=(), out=False, r=None, w=None, prio=0, after=()):
        reads = list(r if r is not None else reads)
        writes = list(w if w is not None else writes)
        rec = _Rec()
        fn(rec)
        name, a, k = rec.calls[0]
        fn = (lambda e, name=name, a=a, k=k: getattr(e, name)(*a, **k))
        issue, lat, _ = _est(qeng, name, a, k)
        node = {"eng": qeng, "fns": [fn], "kind": "dma", "busy": issue, "lat": lat, "grp": None, "prio": prio}
        nid = self._add(node, reads, writes, after=after)
        if out:
            self.out_nodes.append(nid)
        return nid

    def schedule(self):
        for e in self.ENG:
            assert self.pending[e] is None, "dangling unsignalled group on %s" % e
        nodes = self.nodes
        cp = [0.0] * len(nodes)
        if CP_PRIO:
            for nd in reversed(nodes):
                i_ = nd["id"]
                tot = cp[i_] + nd["busy"] + nd["lat"]
                for p in nd["preds"]:
                    if cp[p] < tot:
                        cp[p] = tot
        order = {e: [] for e in self.ENG}
        finish = {}
        eng_free = {e: 0.0 for e in self.ENG}
        act_grp = [None]
        nseg = self.seg + 1
        t_base = 0.0
        for sg in range(nseg):
            queues = {e: [n["id"] for n in nodes if n["seg"] == sg and n["eng"] == e] for e in self.ENG}
            heads = {e: 0 for e in self.ENG}
            done = set()
            remaining = sum(len(q) for q in queues.values())
            for e in self.ENG:
                eng_free[e] = max(eng_free[e], t_base)
            while remaining:
                best = None
                for e in self.ENG:
                    q = queues[e]
                    i = heads[e]
                    cnt = 0
                    while i < len(q) and cnt < self.WINDOW:
                        nid = q[i]
                        i += 1
                        if nid in done:
                            continue
                        cnt += 1
                        nd = nodes[nid]
                        ok = True
                        st = eng_free[e]
                        for p in nd["preds"]:
                            if p not in finish:
                                ok = False
                                break
                            lat = SAME_LAT if nodes[p]["eng"] == e and nodes[p]["kind"] == "op" else XLAT
                            if finish[p] + lat > st:
                                st = finish[p] + lat
                        if not ok:
                            continue
                        st_real = st
                        if e == "act" and nd["grp"] is not None and nd["grp"] != act_grp[0]:
                            st = st + ACT_PEN
                        key = (st, nd.get("prio", 0), -cp[nid] if CP_PRIO else 0.0, nid, st_real)
                        if best is None or key < best[0]:
                            best = (key, e, nid)
                        if (not CP_PRIO) and st_real <= eng_free[e] + 1e-9 and st == st_real:
                            break
                assert best is not None, "scheduler deadlock"
                (_, _, _, _, st), e, nid = best
                nd = nodes[nid]
                busy = nd["busy"]
                if e == "act" and nd["grp"] is not None and nd["grp"] != act_grp[0]:
                    busy += 1.3
                    act_grp[0] = nd["grp"]
                eng_free[e] = st + busy
                finish[nid] = st + busy + nd["lat"]
                order[e].append(nid)
                done.add(nid)
                remaining -= 1
                q = queues[e]
                while heads[e] < len(q) and q[heads[e]] in done:
                    heads[e] += 1
            t_base = max([t_base] + [finish[n["id"]] for n in nodes if n["seg"] == sg])
        self.order = order
        self.est_total = t_base
        return order

    def sem_names(self):
        names = ["pe", "act", "dve", "pool"]
        for qe in ("sp", "pool"):
            for i in range(NDS):
                names.append("d%s%d" % (qe, i))
        return names

    def emit(self, block, sems):
        order = self.schedule()
        nodes = self.nodes
        cnt = {e: 0 for e in self.ENG}
        ndma = {e: 0 for e in self.ENG}
        dma_cnt = {}
        tok = {}
        prev_same_sem = {}
        for e in self.ENG:
            for nid in order[e]:
                nd = nodes[nid]
                if nd["kind"] == "op":
                    cnt[e] += 1
                    tok[nid] = (e, cnt[e])
                else:
                    sname = "d%s%d" % (e, ndma[e] % NDS)
                    ndma[e] += 1
                    prev = dma_cnt.get(sname, 0)
                    prev_same_sem[nid] = (sname, prev)
                    dma_cnt[sname] = prev + 16
                    tok[nid] = (sname, prev + 16)
        seg_floor = {}
        for sg in range(1, self.seg + 1):
            fl = {}
            for nd in nodes:
                if nd["seg"] < sg:
                    s, v = tok[nd["id"]]
                    if fl.get(s, 0) < v:
                        fl[s] = v
            seg_floor[sg] = fl
        progs = {}
        for e in self.ENG:
            known = {}
            lst = []
            for nid in order[e]:
                nd = nodes[nid]
                need = dict(seg_floor.get(nd["seg"], {}))
                for p in nd["preds"]:
                    s, v = tok[p]
                    if need.get(s, 0) < v:
                        need[s] = v
                if nd["kind"] == "dma":
                    s, v = prev_same_sem[nid]
                    if v > 0 and need.get(s, 0) < v:
                        need[s] = v
                waits = []
                for s, v in need.items():
                    if s == "pe" and e == "pe":
                        continue
                    if known.get(s, 0) < v:
                        waits.append((s, v))
                        known[s] = v
                lst.append((waits, nd["fns"], tok[nid][0], 16 if nd["kind"] == "dma" else 1))
            progs[e] = lst
        final = {}
        for s, v in dma_cnt.items():
            final[s] = v

        def run(engobj, eng, fin=False):
            for waits, fns, sname, inc in progs[eng]:
                for s, v in waits:
                    engobj.wait_ge(sems[s], v)
                ins = None
                for f in fns:
                    ins = f(engobj)
                ins.then_inc(sems[sname], inc)
            if fin:
                for s, v in final.items():
                    engobj.wait_ge(sems[s], v)

        @block.tensor
        def _(e):
            run(e, "pe")

        @block.scalar
        def _(e):
            run(e, "act")

        @block.vector
        def _(e):
            run(e, "dve")

        @block.gpsimd
        def _(e):
            run(e, "pool")

        @block.sync
        def _(e):
            run(e, "sp", fin=True)


def build_program(dbg=0):
    nc = bass.Bass("TRN2", target_bir_lowering=False)
    P = Prog()

    def din(name, shape):
        return nc.dram_tensor(name, shape, F32, kind="ExternalInput").ap()

    def dout(name, shape):
        return nc.dram_tensor(name, shape, F32, kind="ExternalOutput").ap()

    x_p = din("x_p", [2048, D])
    x_s = din("x_s", [128, D])
    sconv = din("sconv", [48, D])
    sC = din("sC", [16, 4, 128, 128])
    sn = din("sn", [64, 128])
    sm = din("sm", [16, 4])
    wd_ = {}
    for nm, shp in [("f1_wg", [D, DFF]), ("f1_wu", [D, DFF]), ("f1_wd", [DFF, D]), ("ln1_g", [1, D]), ("ln1_b", [1, D]),
                    ("w_in", [D, 3080]), ("b_in", [1, 3080]), ("gm_ln_g", [1, 512]), ("gm_ln_b", [1, 512]),
                    ("gm_ws", [4, 128, 128]), ("gm_bs", [4, 128]), ("conv_w", [4, D]), ("conv_b", [1, D]),
                    ("gm_out_g", [1, 512]), ("ml_out_g", [1, 512]), ("w_out", [D, D]), ("ln2_g", [1, D]), ("ln2_b", [1, D]),
                    ("f2_wg", [D, DFF]), ("f2_wu", [D, DFF]), ("f2_wd", [DFF, D]), ("ln3_g", [1, D]), ("ln3_b", [1, D])]:
        wd_[nm] = din(nm, shp)

    y_p = dout("y_p", [2048, D])
    y_s = dout("y_s", [128, D])
    gmv_p = dout("gmv_p", [128, 512])
    gmv_s = dout("gmv_s", [128, 512])
    conv_p = dout("conv_p", [3, D])
    conv_s = dout("conv_s", [48, D])
    C_p = dout("C_p", [4, 128, 128])
    C_s = dout("C_s", [16, 4, 128, 128])
    n_p = dout("n_p", [4, 128])
    n_s = dout("n_s", [64, 128])
    m_p = dout("m_p", [4, 1])
    m_s = dout("m_s", [16, 4])

    import os as _os
    ARENA_B = 73472
    XT_B = 8 * NTOK * 2
    NSLOT = 16

    import contextlib
    with contextlib.ExitStack() as es:
        def sb(name, shape, dt=F32):
            return es.enter_context(nc.sbuf_tensor(name, shape, dt))

        acc = sb("acc", [128, NT, D])
        xtraw = sb("xtraw", [128, XT_B // 4])
        arena = sb("arena", [128, ARENA_B // 4])
        lnp = sb("lnp", [128, 2, D])
        ident_bf = sb("ident_bf", [128, 128], BF16)
        ident_f = sb("ident_f", [128, 128])
        negmask = sb("negmask", [128, 128])
        negmask_s = sb("negmask_s", [128, 128])
        wsT = sb("wsT", [128, 4, 128], BF16)
        wsT_s = sb("wsT_s", [128, 4, 128], BF16)
        colsC = sb("colsC", [128, 64])
        gmln = sb("gmln", [128, 2, 512])
        b4 = sb("b4", [4, 512], BF16)
        selb = sb("selb", [4, 4, 128], BF16)
        sel = sb("sel", [4, 4, 128])
        onehot = sb("onehot", [128, 16])
        bgate = sb("bgate", [4, 2])
        small = sb("small", [128, 320])
        gt = sb("gt", [4, NSLOT, 144])
        Cst = sb("Cst", [128, 4, 130])
        Cbf = sb("Cbf", [128, 4, 130], BF16)
        psum = [es.enter_context(nc.psum_tensor("ps%d" % i, [128, 512], F32)) for i in range(8)]

        def carve(raw, off, nbytes, dt):
            assert off % 4 == 0 and nbytes % 4 == 0, (off, nbytes)
            return raw[:, off // 4:(off + nbytes) // 4].bitcast(dt)

        XTK = ["xT%dh%d" % (t, h) for t in range(NT) for h in range(2)]

        def c_(fn, eng="pool", r=(), w=()):
            P.op(eng, fn, r, w)

        c_(lambda e: e.memset(ident_f[:], 1.0), w=["ident_f"])
        c_(lambda e: e.affine_select(out=ident_f[:], in_=ident_f[:], pattern=[[-1, 128]], compare_op=ALU.is_equal,
                                     fill=0.0, base=0, channel_multiplier=1), r=["ident_f"], w=["ident_f"])
        c_(lambda e: e.tensor_copy(out=ident_bf[:], in_=ident_f[:]), r=["ident_f"], w=["ident_bf"])
        c_(lambda e: e.memset(negmask[:], 0.0), w=["negmask"])
        c_(lambda e: e.affine_select(out=negmask[:], in_=negmask[:], pattern=[[1, 128]], compare_op=ALU.is_ge,
                                     fill=NEG, base=0, channel_multiplier=-1), r=["negmask"], w=["negmask"])
        c_(lambda e: e.affine_select(out=negmask_s[:].rearrange("p (j r) -> p j r", j=16),
                                     in_=negmask[:].rearrange("p (j r) -> p j r", j=16),
                                     pattern=[[-8, 16], [0, 8]], compare_op=ALU.is_ge,
                                     fill=NEG, base=0, channel_multiplier=1), r=["negmask"], w=["negmask_s"])
        c_(lambda e: e.memset(sel[:], 1.0), w=["sel"])
        c_(lambda e: e.affine_select(out=sel[:], in_=sel[:], pattern=[[-1, 4], [0, 128]], compare_op=ALU.is_equal,
                                     fill=0.0, base=0, channel_multiplier=1), r=["sel"], w=["sel"])
        c_(lambda e: e.tensor_copy(out=selb[:], in_=sel[:]), r=["sel"], w=["selb"])
        c_(lambda e: e.memset(onehot[:], 1.0), w=["onehot"])
        c_(lambda e: e.affine_select(out=onehot[:], in_=onehot[:], pattern=[[-8, 16]], compare_op=ALU.is_ge,
                                     fill=0.0, base=0, channel_multiplier=1), r=["onehot"], w=["onehot"])
        c_(lambda e: e.affine_select(out=onehot[:], in_=onehot[:], pattern=[[8, 16]], compare_op=ALU.is_ge,
                                     fill=0.0, base=7, channel_multiplier=-1), r=["onehot"], w=["onehot"])
        c_(lambda e: e.memset(Cst[:], 0.0), w=["Cst"])
        c_(lambda e: e.memset(Cbf[:], 0.0), w=["Cbf"])
        c_(lambda e: e.memset(gt[:], 0.0), w=["gt"])
        c_(lambda e: e.memset(small[:], 0.0), w=["small"])

        wtmp = carve(arena, 65664, 512 * 8, F32).rearrange("p (a b) -> p a b", a=8)
        rowsC = carve(arena, 65664 + 4096, 512, F32)
        win = wd_["w_in"]
        b_in = wd_["b_in"]
        SK = ["setup_rows"]
        P.dma("sp", lambda e: e.dma_start(out=rowsC[0:8, :], in_=b_in[0, 1024:2048].rearrange("(c p) -> c p", p=128)), w=["rows0"])
        P.dma("sp", lambda e: e.dma_start(out=rowsC[8:16, :], in_=wd_["conv_b"][0, :].rearrange("(c p) -> c p", p=128)), w=["rows1"])
        P.dma("sp", lambda e: e.dma_start(out=rowsC[16:48, :], in_=wd_["conv_w"].rearrange("j (c p) -> (j c) p", p=128)), w=["rows2"])
        P.dma("sp", lambda e: e.dma_start(out=rowsC[48:52, :], in_=wd_["gm_bs"][:, :]), w=["rows3"])
        bs_src = bass.AP(wd_["gm_bs"].tensor, 0, [[128, 4], [0, 16], [1, 8]])
        P.dma("sp", lambda e: e.dma_start(out=rowsC[52:56, :].rearrange("p (j r) -> p j r", j=16), in_=bs_src), w=["rows4"])
        P.dma("sp", lambda e: e.dma_start(out=rowsC[56:60, :], in_=wd_["gm_out_g"][0, :].rearrange("(c p) -> c p", p=128)), w=["rows5"])
        P.dma("sp", lambda e: e.dma_start(out=rowsC[60:64, :], in_=wd_["ml_out_g"][0, :].rearrange("(c p) -> c p", p=128)), w=["rows6"])
        P.op("pe", lambda e: e.transpose(out=psum[6][:, 0:64], in_=rowsC[0:64, :], identity=ident_f[0:64, 0:64]),
             ["rows%d" % i_ for i_ in range(7)] + ["ident_f"], ["b6"])
        P.op("dve", lambda e: e.tensor_copy(out=colsC[:], in_=psum[6][:, 0:64]), ["b6"], ["colsC"])
        P.dma("sp", lambda e: e.dma_start(out=bgate[:, 0:1], in_=b_in[0, 3072:3076].rearrange("(p o) -> p o", o=1)), w=["bgate"])
        P.dma("sp", lambda e: e.dma_start(out=bgate[:, 1:2], in_=b_in[0, 3076:3080].rearrange("(p o) -> p o", o=1)), w=["bgate"])
        P.dma("sp", lambda e: e.dma_start(out=gmln[:, 0, :], in_=wd_["gm_ln_g"].partition_broadcast(128)), w=["gmln"])
        P.dma("sp", lambda e: e.dma_start(out=gmln[:, 1, :], in_=wd_["gm_ln_b"].partition_broadcast(128)), w=["gmln"])
        for i, c0 in enumerate([0, 512, 2048, 2560]):
            P.dma("pool", lambda e, i=i, c0=c0: e.dma_start(out=b4[i:i + 1, :], in_=b_in[0:1, c0:c0 + 512]), w=["bias4"])

        for h in range(4):
            P.dma("sp", lambda e, h=h: e.dma_start(out=wtmp[:, h, :], in_=wd_["gm_ws"][h, :, :]), w=["wtmp%d" % h])
            P.op("pool", lambda e, h=h: e.affine_select(out=wtmp[:, h, :], in_=wtmp[:, h, :], pattern=[[-1, 128]],
                                                        compare_op=ALU.is_ge, fill=0.0, base=0, channel_multiplier=1), ["wtmp%d" % h], ["wtmp%d" % h])
            P.op("pe", lambda e, h=h: e.transpose(out=psum[7][:, h * 128:(h + 1) * 128], in_=wtmp[:, h, :], identity=ident_f[:]),
                 ["wtmp%d" % h, "ident_f"], ["b7"])
        P.op("dve", lambda e: e.tensor_copy(out=wsT[:].rearrange("p a b -> p (a b)"), in_=psum[7][:, :]), ["b7"], ["wsT"])
        w8 = carve(arena, 65664 + 4608, 128, F32).rearrange("p (h c) -> p h c", h=4)
        for h in range(4):
            w8_src = bass.AP(wd_["gm_ws"].tensor, h * 128 * 128, [[0, 16], [128, 8], [1, 8]])
            for j in range(16):
                pass
            P.dma("sp", lambda e, h=h, w8_src=w8_src: e.dma_start(out=w8[:, h, :], in_=w8_src), w=["w8_%d" % h])
        for h in range(4):
            hh = 4 + h
            for j in range(16):
                P.op("dve", lambda e, h=h, hh=hh, j=j: e.tensor_scalar(out=wtmp[:, hh, 8 * j:8 * j + 8], in0=w8[:, h, :],
                                                                       scalar1=onehot[:, j:j + 1], scalar2=None, op0=ALU.mult),
                     ["w8_%d" % h, "onehot"], ["wtmp%d_%d" % (hh, j)])
            WJ = ["wtmp%d_%d" % (hh, j) for j in range(16)]
            P.op("pool", lambda e, hh=hh: e.affine_select(out=wtmp[:, hh, :], in_=wtmp[:, hh, :], pattern=[[-1, 128]],
                                                          compare_op=ALU.is_ge, fill=0.0, base=0, channel_multiplier=1), WJ, WJ)
            P.op("pe", lambda e, h=h, hh=hh: e.transpose(out=psum[6][:, h * 128:(h + 1) * 128], in_=wtmp[:, hh, :], identity=ident_f[:]),
                 WJ + ["ident_f"], ["b6"])
        P.op("dve", lambda e: e.tensor_copy(out=wsT_s[:].rearrange("p a b -> p (a b)"), in_=psum[6][:, :]), ["b6"], ["wsT_s"])

        st6 = small[:, 0:12].rearrange("p (a b) -> p a b", a=2)
        mv = small[:, 12:14]
        sd = small[:, 14:15]
        rstd = small[:, 15:16]
        nmr = small[:, 16:17]

        LN_GAMMA_ENG = "dve"

        ln_junk = carve(arena, 61440, 2048, BF16)

        def layer_norm_tile(t, act_stats=False, affine=True, norm_on_dve=False):
            a = acc[:, t, :]
            k = "acc%d" % t
            if act_stats:
                s1 = small[:, 0:1]
                s2 = small[:, 1:2]
                msq = small[:, 2:3]
                P.op("act", lambda e: e.activation(out=ln_junk, in_=a, func=AF.Identity, accum_out=s1), [k], ["lnjunk", "ln_st0"])
                P.op("act", lambda e: e.activation(out=ln_junk, in_=a, func=AF.Square, accum_out=s2), [k], ["lnjunk", "ln_st1"])
                P.op("dve", lambda e: e.tensor_scalar(out=mv[:, 0:1], in0=s1, scalar1=1.0 / D, scalar2=None, op0=ALU.mult), ["ln_st0", "ln_mv"], ["ln_mv"])
                P.op("dve", lambda e: e.tensor_tensor(out=msq, in0=mv[:, 0:1], in1=mv[:, 0:1], op=ALU.mult), ["ln_mv"], ["ln_msq"])
                P.op("dve", lambda e: e.scalar_tensor_tensor(out=mv[:, 1:2], in0=s2, scalar=1.0 / D, in1=msq, op0=ALU.mult, op1=ALU.subtract),
                     ["ln_st1", "ln_msq", "ln_mv"], ["ln_mv"])
            else:
                P.op("dve", lambda e: e.bn_stats(out=st6[:, 0, :], in_=a[:, 0:512]), [k], ["ln_st0"])
                P.op("dve", lambda e: e.bn_stats(out=st6[:, 1, :], in_=a[:, 512:1024]), [k], ["ln_st1"])
                P.op("dve", lambda e: e.bn_aggr(out=mv, in_=small[:, 0:12]), ["ln_st0", "ln_st1"], ["ln_mv"])
            P.op("act", lambda e: e.activation(out=sd, in_=mv[:, 1:2], func=AF.Ln, bias=LN_EPS / (ALPHA * ALPHA), scale=1.0), ["ln_mv"], ["ln_sd"])
            P.op("act", lambda e: e.activation(out=rstd, in_=sd, func=AF.Exp, scale=-0.5), ["ln_sd"], ["ln_rstd"])
            P.op("dve", lambda e: e.tensor_scalar(out=nmr, in0=mv[:, 0:1], scalar1=rstd, scalar2=-1.0, op0=ALU.mult, op1=ALU.mult),
                 ["ln_mv", "ln_rstd"], ["ln_nmr"])
            if norm_on_dve:
                P.op("dve", lambda e: e.tensor_scalar(out=a, in0=a, scalar1=rstd, scalar2=nmr, op0=ALU.mult, op1=ALU.add), [k, "ln_rstd", "ln_nmr"], [k])
            else:
                P.op("act", lambda e: e.activation(out=a, in_=a, func=AF.Identity, bias=nmr, scale=rstd), [k, "ln_rstd", "ln_nmr"], [k])
            if affine:
                P.op(LN_GAMMA_ENG, lambda e: e.tensor_tensor(out=a, in0=a, in1=lnp[:, 0, :], op=ALU.mult), [k, "lnp"], [k])
                P.op("dve", lambda e: e.tensor_tensor(out=a, in0=a, in1=lnp[:, 1, :], op=ALU.add), [k, "lnp"], [k])

        def load_lnp(g, b):
            P.dma("sp", lambda e: e.dma_start(out=lnp[:, 0, :], in_=wd_[g].partition_broadcast(128)), w=["lnp"])
            P.dma("sp", lambda e: e.dma_start(out=lnp[:, 1, :], in_=wd_[b].partition_broadcast(128)), w=["lnp"])

        xT = xtraw[:, :].bitcast(BF16).rearrange("p (kc t) -> p kc t", kc=8)
        WB = 6144 * NPC
        f_wg = [carve(arena, b * WB, 2048 * NPC, BF16).rearrange("p (kc f) -> p kc f", kc=8) for b in range(2)]
        f_wu = [carve(arena, b * WB + 2048 * NPC, 2048 * NPC, BF16).rearrange("p (kc f) -> p kc f", kc=8) for b in range(2)]
        f_wd = [carve(arena, b * WB + 4096 * NPC, 2048 * NPC, BF16).rearrange("p (c d) -> p c d", c=NPC) for b in range(2)]
        o0 = 2 * WB
        f_hT = [carve(arena, o0 + b * 1024 * NPC, 1024 * NPC, BF16).rearrange("p (c t) -> p c t", c=NPC) for b in range(2)]
        o0 += 2 * 1024 * NPC
        f_sg = [carve(arena, o0 + b * 2048, 2048, F32) for b in range(2)]
        o0 += 4096
        assert o0 <= ARENA_B

        def transpose_tile_to(t, dst3, dkey, banks=(6, 7), bkeys=(("b6",), ("b7",)), eng_a="dve", eng_b="act", extra_r=()):
            k = "acc%d" % t
            for half in range(2):
                bank = banks[half]
                bk = list(bkeys[half])
                for q_ in range(4):
                    kc = half * 4 + q_
                    P.op("pe", lambda e, bank=bank, q_=q_, kc=kc: e.transpose(out=psum[bank][:, q_ * 128:(q_ + 1) * 128],
                                                                             in_=acc[:, t, kc * 128:(kc + 1) * 128], identity=ident_f[:]),
                         [k, "ident_f"], bk, sig=(q_ == 3))
                eng = eng_a if half == 0 else eng_b
                if eng == "act":
                    P.op("act", lambda e, bank=bank, half=half: e.activation(
                        out=dst3[:, half * 4:half * 4 + 4, :], in_=psum[bank][:, :].rearrange("p (a b) -> p a b", a=4), func=AF.Copy),
                        bk + list(extra_r), [dkey + ("h%d" % half)])
                else:
                    P.op(eng, lambda e, bank=bank, half=half: e.tensor_copy(
                        out=dst3[:, half * 4:half * 4 + 4, :], in_=psum[bank][:, :].rearrange("p (a b) -> p a b", a=4)),
                        bk + list(extra_r), [dkey + ("h%d" % half)])

        def ffn_stage(sidx, first, pre_affine=False, lnp_after_p0=None):
            pre = "f%d_" % sidx
            wg_v = wd_[pre + "wg"].rearrange("(kc kp) f -> kp kc f", kp=128)
            wu_v = wd_[pre + "wu"].rearrange("(kc kp) f -> kp kc f", kp=128)
            wdn_v = wd_[pre + "wd"].rearrange("(c fp) d -> fp c d", fp=128)
            cnt = 0
            hb_i = 0
            WALIAS = (["w_in%d" % c for c in range(8)] + ["w_out"]) if sidx == 2 else []
            for p, (c0, npc) in enumerate(PASSES):
                b = p % 2
                wprio = -1 if (sidx == 1 and p == 0) else 0
                for ci in range(npc):
                    P.dma("pool", lambda e, b=b, c0=c0, ci=ci: e.dma_start(out=f_wg[b][:, :, ci * 128:(ci + 1) * 128],
                                                                           in_=wg_v[:, :, (c0 + ci) * 128:(c0 + ci + 1) * 128]),
                          w=["wg%d_%d" % (b, ci)], prio=wprio, after=WALIAS)
                    P.dma("pool", lambda e, b=b, c0=c0, ci=ci: e.dma_start(out=f_wu[b][:, :, ci * 128:(ci + 1) * 128],
                                                                           in_=wu_v[:, :, (c0 + ci) * 128:(c0 + ci + 1) * 128]),
                          w=["wu%d_%d" % (b, ci)], prio=wprio, after=WALIAS)
                for ci in range(npc):
                    P.dma("pool", lambda e, b=b, c0=c0, ci=ci: e.dma_start(out=f_wd[b][:, ci, :], in_=wdn_v[:, c0 + ci, :]),
                          w=["wd%d_%d" % (b, ci)], prio=wprio, after=WALIAS)
                last = (p == len(PASSES) - 1)
                if p == 1 and lnp_after_p0 is not None:
                    load_lnp(*lnp_after_p0)
                for g, tiles in enumerate(GROUPS):
                    n = 128 * len(tiles)
                    t0 = tiles[0] * 128
                    xkeys = []
                    for t in tiles:
                        xkeys += ["xT%dh0" % t, "xT%dh1" % t]
                    if p == 0:
                        for t in tiles:
                            k = "acc%d" % t
                            a = acc[:, t, :]
                            if first:
                                src = x_p[t * 128:(t + 1) * 128, :] if t < 16 else x_s[:, :]
                                P.dma("sp", lambda e, a=a, src=src: e.dma_start(out=a, in_=src), w=[k], prio=-1)
                            if pre_affine:
                                P.op("dve", lambda e, a=a: e.tensor_tensor(out=a, in0=a, in1=lnp[:, 0, :], op=ALU.mult), [k, "lnp"], [k])
                                P.op("dve", lambda e, a=a: e.tensor_tensor(out=a, in0=a, in1=lnp[:, 1, :], op=ALU.add), [k, "lnp"], [k])
                            transpose_tile_to(t, xT[:, :, t * 128:(t + 1) * 128], "xT%d" % t, extra_r=(["BAR2"] if sidx == 2 else []))
                    hb = hb_i % 2
                    hb_i += 1
                    for ci in range(npc):
                        s = cnt % 2
                        cnt += 1
                        for kc in range(8):
                            P.op("pe", lambda e, s=s, b=b, kc=kc, ci=ci, t0=t0, n=n: e.matmul(
                                psum[s][:, 0:n], f_wg[b][:, kc, ci * 128:(ci + 1) * 128], xT[:, kc, t0:t0 + n],
                                start=(kc == 0), stop=(kc == 7)), ["wg%d_%d" % (b, ci)] + xkeys, ["psg%d" % s], sig=(kc == 7))
                        for kc in range(8):
                            P.op("pe", lambda e, s=s, b=b, kc=kc, ci=ci, t0=t0, n=n: e.matmul(
                                psum[2 + s][:, 0:n], f_wu[b][:, kc, ci * 128:(ci + 1) * 128], xT[:, kc, t0:t0 + n],
                                start=(kc == 0), stop=(kc == 7)), ["wu%d_%d" % (b, ci)] + xkeys, ["psu%d" % s], sig=(kc == 7))
                        P.op("act", lambda e, s=s, n=n: e.activation(out=f_sg[s][:, 0:n], in_=psum[s][:, 0:n], func=AF.Silu),
                             ["psg%d" % s], ["sg%d" % s])
                        P.op("dve", lambda e, s=s, n=n, hb=hb, ci=ci: e.tensor_tensor(out=f_hT[hb][:, ci, 0:n], in0=f_sg[s][:, 0:n],
                                                                                    in1=psum[2 + s][:, 0:n], op=ALU.mult),
                             ["sg%d" % s, "psu%d" % s], ["hT%d_%d" % (hb, ci)])
                    for ti, t in enumerate(tiles):
                        k = "acc%d" % t
                        for half in range(2):
                            for ci in range(npc):
                                P.op("pe", lambda e, half=half, hb=hb, ci=ci, ti=ti, b=b, npc=npc: e.matmul(
                                    psum[4 + half][:, :], f_hT[hb][:, ci, ti * 128:(ti + 1) * 128],
                                    f_wd[b][:, ci, half * 512:(half + 1) * 512], start=(ci == 0), stop=(ci == npc - 1)),
                                    ["hT%d_%d" % (hb, ci), "wd%d_%d" % (b, ci)], ["psy%d" % half], sig=(ci == npc - 1))
                            P.op("dve", lambda e, half=half, t=t: e.scalar_tensor_tensor(
                                out=acc[:, t, half * 512:(half + 1) * 512], in0=psum[4 + half][:, :], scalar=0.5 / ALPHA,
                                in1=acc[:, t, half * 512:(half + 1) * 512], op0=ALU.mult, op1=ALU.add),
                                ["psy%d" % half, k], [k])
                        if last:
                            layer_norm_tile(t, act_stats=True)

        w_in_sb = carve(arena, 0, 49280, BF16).rearrange("p (kc f) -> p kc f", kc=8)
        w_out_sb = carve(arena, 49280, 16384, BF16).rearrange("p (kc f) -> p kc f", kc=8)
        _regions = [[xtraw, 0, XT_B], [arena, 65664, ARENA_B]]

        def walloc(nbytes, dt):
            nb = (nbytes + 31) // 32 * 32
            for rg in _regions:
                if rg[1] + nb <= rg[2]:
                    v = carve(rg[0], rg[1], nb, dt)
                    rg[1] += nb
                    return v
            raise AssertionError("mixer working set does not fit")

        x1T = walloc(2048, BF16).rearrange("p (a b) -> p a b", a=8)
        mixT = walloc(2048, BF16).rearrange("p (a b) -> p a b", a=8)
        u_act = walloc(2048, F32)
        vn = walloc(2048, F32)
        vn_bf = walloc(1024, BF16)
        v_ext2 = [walloc(1040, BF16)[:, 0:520].rearrange("p (a b) -> p a b", a=4) for _ in range(2)]
        o_sig2 = [walloc(2048, F32) for _ in range(2)]
        extb = walloc(5632, F32)
        ext = extb[:, 0:8 * 131].rearrange("p (a b) -> p a b", a=8)
        ext_s = extb[:, 0:8 * 176].rearrange("p (a j r) -> p a j r", a=8, j=16)
        extbb = walloc(2816, BF16)
        ext_bf = extbb[:, 0:8 * 131].rearrange("p (a b) -> p a b", a=8)
        ext_s_bf = extbb[:, 0:8 * 176].rearrange("p (a j r) -> p a j r", a=8, j=16)
        Dg = lnp[:, :, :].rearrange("p a b -> p (a b)").bitcast(BF16).rearrange("p (i c) -> p i c", i=32)
        qkT_raw = [walloc(2048, BF16) for _ in range(2)]
        qkT_bf2 = [q_.rearrange("p (a b) -> p a b", a=8) for q_ in qkT_raw]
        kw_bf = walloc(1024, BF16).rearrange("p (a b) -> p a b", a=4)
        mixf = walloc(2048, F32)
        mix_bf = walloc(2048, BF16)
        Dm = [walloc(512, F32) for _ in range(2)]
        Pb = [walloc(256, BF16) for _ in range(2)]
        itw = walloc(528, F32)
        junk = walloc(256, BF16)
        C0s = [qkT_raw[1].bitcast(F32).rearrange("p (a b) -> p a b", a=4), o_sig2[1].rearrange("p (a b) -> p a b", a=4)]
        vmask = [walloc(1040, BF16)[:, 0:520].rearrange("p (a b) -> p a b", a=4) for _ in range(2)]
        qTf = walloc(2048, F32).rearrange("p (a b) -> p a b", a=4)
        convc = walloc(1536, F32).rearrange("p (a b) -> p a b", a=8)
        n0row = walloc(512, F32)
        n0T = walloc(256, F32)
        nnewT = walloc(256, F32)

        st4 = small[:, 20:44].rearrange("p (a b) -> p a b", a=4)
        mv4 = small[:, 44:52].rearrange("p (a b) -> p a b", a=4)
        sd4 = small[:, 52:56]
        rstd4 = small[:, 56:60]
        nmr4 = small[:, 60:64]
        ssq = small[:, 64:68]
        rr = small[:, 68:72]
        cA = small[:, 72:76]
        cG = small[:, 76:80]
        cM = small[:, 80:84]
        cAp = small[:, 84:88]
        Gprev_bc = small[:, 88:92]
        Gend_bc = small[:, 92:96]
        d1 = small[:, 96:100]
        winter = small[:, 100:104]
        floor_ = small[:, 104:108]
        d2 = small[:, 108:112]
        wk = small[:, 112:116]
        d3 = small[:, 116:120]
        dec = small[:, 120:124]
        dd = small[:, 124:125]
        rd = small[:, 125:126]
        ncol = small[:, 128:132]
        cGp_s = small[:, 132:136]
        cGe_s = small[:, 136:140]
        dq = small[:, 152:156]
        decbc = small[:, 160:224].rearrange("p (h j) -> p h j", h=4)

        def gs(i, n=128):
            return gt[:, i, 0:n]
        S_IG, S_FG, S_T, S_B, S_M, S_CAR, S_ONE, S_DG, S_RM, S_RA, S_GP, S_GE, S_AC, S_GC, S_MC, S_NR = range(16)

        PSMAP = {"A": {0: 0, 3: 1, 4: 0, 5: 2, 7: 2, 1: 1, 2: 0},
                 "B": {3: 3, 4: 4, 5: 5, 6: 6, 1: 7, 2: 4, 7: 7, 0: 3},
                 "C": {0: 3, 1: 5, 2: 6}}
        phase = ["A"]
        par = [0]

        class _PS:
            def __getitem__(self, old):
                return psum[PSMAP[phase[0]][old]]
        PS = _PS()
        DBL = set(["qk%d" % c for c in range(8)] + ["v_ext", "o_sig", "g_ig", "g_fg", "g_t", "g_B", "g_M", "cols", "cAp",
                                                    "d1", "winter", "floor", "d2", "wk", "d3", "dec"])
        SALIAS = {"g_gp": ["g_ig@1"], "g_ge": ["g_fg@1"], "g_ac": ["g_t@1"], "g_gc": ["g_B@1"], "g_mc": ["g_M@1"],
                  "C0s0": ["qk%d@1" % c for c in range(8)], "C0s1": ["o_sig@1"]}

        def mixer_keymap(keys, is_read):
            keys = list(keys)
            nobar = len(keys) > 0 and keys[0] == "__nobar__"
            if nobar:
                keys = keys[1:]
            out = ["BAR1"] if (is_read and not nobar) else []
            for k_ in keys:
                if len(k_) >= 2 and k_[0] == "b" and k_[1] in "01234567":
                    out.append("b%d" % PSMAP[phase[0]][int(k_[1])])
                elif k_ in DBL:
                    out.append("%s@%d" % (k_, par[0]))
                elif k_ in SALIAS:
                    out += SALIAS[k_]
                else:
                    out.append(k_)
            return out

        def head_rms_to_mixbf(off):
            for h in range(4):
                P.op("act", lambda e, h=h: e.activation(out=junk[:, 0:128], in_=mixf[:, h * 128:(h + 1) * 128], func=AF.Square,
                                                        accum_out=ssq[:, h:h + 1]), ["mixf"], ["ssq%d" % h, "junk"])
            P.op("dve", lambda e: e.tensor_scalar(out=rr, in0=ssq, scalar1=1.0 / 128.0, scalar2=LN_EPS, op0=ALU.mult, op1=ALU.add),
                 ["ssq%d" % h for h in range(4)], ["rr"])
            P.op("act", lambda e: e.activation(out=rr, in_=rr, func=AF.Ln), ["rr"], ["rr"])
            P.op("act", lambda e: e.activation(out=rr, in_=rr, func=AF.Exp, scale=-0.5), ["rr"], ["rr"])
            for h in range(4):
                P.op("dve", lambda e, h=h: e.tensor_scalar(out=mix_bf[:, off + h * 128:off + (h + 1) * 128],
                                                           in0=mixf[:, h * 128:(h + 1) * 128], scalar1=rr[:, h:h + 1], scalar2=None,
                                                           op0=ALU.mult), ["mixf", "rr"], ["mix_bf%d" % (off // 512)])

        def zblock(i, col0, bank):
            key = "b%d" % bank
            for kc in range(8):
                P.op("pe", lambda e, kc=kc: e.matmul(PS[bank][:, :], x1T[:, kc, :], w_in_sb[:, kc, col0:col0 + 512],
                                                     start=(kc == 0), stop=False), ["x1Th0", "x1Th1", "w_in%d" % kc], [key], sig=False)
            P.op("pe", lambda e: e.matmul(PS[bank][:, :], selb[:, i, :], b4[:, :], start=False, stop=True),
                 ["selb", "bias4"], [key])

        MIXSTOP = int(_os.environ.get("MIXSTOP", "99"))
        MIXTILES = int(_os.environ.get("MIXTILES", "99"))

        def mixer_tile(t):
            sample = (t == 16)
            k = "acc%d" % t
            a = acc[:, t, :]
            pp = 0 if sample else (t % 2)
            par[0] = pp
            phase[0] = "A"
            go = 0 if sample else 10 * pp
            co = 184 * pp
            SC = lambda lo, hi: small[:, lo + co:hi + co]
            cA, cG, cM, cAp = SC(72, 76), SC(76, 80), SC(80, 84), SC(84, 88)
            d1, winter, floor_, d2, wk, d3, dec = SC(96, 100), SC(100, 104), SC(104, 108), SC(108, 112), SC(112, 116), SC(116, 120), SC(120, 124)
            qkT_bf, v_ext, o_sig = qkT_bf2[pp], v_ext2[pp], o_sig2[pp]
            n_g = 144 if sample else 128
            transpose_tile_to(t, x1T, "x1T", banks=(0, 1), bkeys=(("b0",), ("b1",)))
            XK = ["x1Th0", "x1Th1"]
            if MIXSTOP <= 1:
                return
            for gi, col in enumerate([3072, 3076]):
                for kc in range(8):
                    P.op("pe", lambda e, gi=gi, col=col, kc=kc: e.matmul(PS[5][0:4, gi * 128:(gi + 1) * 128],
                                                                         w_in_sb[:, kc, col:col + 4], x1T[:, kc, :],
                                                                         start=(kc == 0), stop=(kc == 7)),
                         XK + ["w_in%d" % kc], ["b5a" if gi == 0 else "b5b"], sig=(kc == 7))
            if not sample:
                ig = gs(S_IG + go); fg = gs(S_FG + go); tm = gs(S_T + go); Bt = gs(S_B + go); Mt = gs(S_M + go)
                P.op("act", lambda e: e.activation(out=ig, in_=PS[5][0:4, 0:128], func=AF.Identity, bias=bgate[:, 0:1], scale=1.0),
                     ["b5a", "bgate"], ["g_ig"])
                P.op("act", lambda e: e.activation(out=fg, in_=PS[5][0:4, 128:256], func=AF.Identity, bias=bgate[:, 1:2], scale=1.0),
                     ["b5b", "bgate"], ["g_fg"])
            else:
                ig = gs(S_IG, 144); fg = gs(S_FG, 144); tm = gs(S_T, 144); Bt = gs(S_B, 144); Mt = gs(S_M, 144)
                v3 = lambda ap: ap.rearrange("p (j r) -> p j r", j=16)
                P.op("pool", lambda e: e.memset(ig, 0.0), (), ["g_ig"])
                P.op("pool", lambda e: e.memset(fg, 0.0), (), ["g_fg"])
                P.op("act", lambda e: e.activation(out=v3(ig)[:, :, 1:9], in_=PS[5][0:4, 0:128].rearrange("p (j r) -> p j r", j=16),
                                                   func=AF.Identity, bias=bgate[:, 0:1], scale=1.0), ["b5a", "bgate", "g_ig"], ["g_ig"])
                P.op("act", lambda e: e.activation(out=v3(fg)[:, :, 1:9], in_=PS[5][0:4, 128:256].rearrange("p (j r) -> p j r", j=16),
                                                   func=AF.Identity, bias=bgate[:, 1:2], scale=1.0), ["b5b", "bgate", "g_fg"], ["g_fg"])
            P.op("dve", lambda e: e.scalar_tensor_tensor(out=tm, in0=fg, scalar=-1.0, in1=fg, op0=ALU.mult, op1=ALU.max), ["g_fg"], ["g_t"])
            P.op("act", lambda e: e.activation(out=tm, in_=tm, func=AF.Exp, scale=-1.0), ["g_t"], ["g_t"])
            P.op("act", lambda e: e.activation(out=tm, in_=tm, func=AF.Ln, bias=1.0, scale=1.0), ["g_t"], ["g_t"])
            P.op("dve", lambda e: e.scalar_tensor_tensor(out=fg, in0=fg, scalar=0.0, in1=tm, op0=ALU.min, op1=ALU.subtract),
                 ["g_fg", "g_t"], ["g_fg"])
            Gt = tm
            if not sample:
                P.op("dve", lambda e: e.tensor_tensor_scan(out=Bt, data0=gs(S_ONE), data1=fg, initial=gt[:, S_CAR, 0:1],
                                                           op0=ALU.mult, op1=ALU.add), ["g_fg", "g_car", "g_one"], ["g_B"])
                P.op("dve", lambda e: e.tensor_tensor(out=ig, in0=ig, in1=Bt, op=ALU.subtract), ["g_ig", "g_B"], ["g_ig"])
                P.op("dve", lambda e: e.tensor_tensor_scan(out=Gt, data0=gs(S_ONE), data1=ig, initial=gt[:, S_CAR, 1:2],
                                                           op0=ALU.mult, op1=ALU.max), ["g_ig", "g_car", "g_one", "g_t"], ["g_t"])
                P.op("dve", lambda e: e.tensor_tensor(out=Mt, in0=Gt, in1=Bt, op=ALU.add), ["g_t", "g_B"], ["g_M"])
                Ac, Gc, Mc = ig, Gt, Mt
                G_end = Gt[:, 127:128]
            else:
                v3 = lambda ap: ap.rearrange("p (j r) -> p j r", j=16)
                P.op("pool", lambda e: e.memset(v3(fg)[:, :, 0:1], 0.0), ["g_fg"], ["g_fg"])
                P.op("dve", lambda e: e.tensor_tensor_scan(out=Bt, data0=gs(S_RM, 144), data1=fg, initial=0.0,
                                                           op0=ALU.mult, op1=ALU.add), ["g_fg", "g_rm"], ["g_B"])
                P.op("dve", lambda e: e.tensor_tensor(out=ig, in0=ig, in1=Bt, op=ALU.subtract), ["g_ig", "g_B"], ["g_ig"])
                P.op("pool", lambda e: e.tensor_copy(out=v3(ig)[:, :, 0:1], in_=gt[:, S_NR, 128:144].rearrange("p (j o) -> p j o", o=1)),
                     ["g_ig", "g_m0"], ["g_ig"])
                P.op("dve", lambda e: e.tensor_tensor_scan(out=Gt, data0=gs(S_RA, 144), data1=ig, initial=0.0,
                                                           op0=ALU.add, op1=ALU.max), ["g_ig", "g_ra", "g_t"], ["g_t"])
                P.op("dve", lambda e: e.tensor_tensor(out=Mt, in0=Gt, in1=Bt, op=ALU.add), ["g_t", "g_B"], ["g_M"])
                Ac, Gc, Mc = gs(S_AC), gs(S_GC), gs(S_MC)
                c3 = lambda ap: ap.rearrange("p (j r) -> p j r", j=16)
                P.op("pool", lambda e: e.tensor_copy(out=c3(Ac), in_=v3(ig)[:, :, 1:9]), ["g_ig"], ["g_ac"])
                P.op("pool", lambda e: e.tensor_copy(out=c3(Gc), in_=v3(Gt)[:, :, 1:9]), ["g_t"], ["g_gc"])
                P.op("pool", lambda e: e.tensor_copy(out=c3(Mc), in_=v3(Mt)[:, :, 1:9]), ["g_M"], ["g_mc"])
                for r_ in range(8):
                    P.op("pool", lambda e, r_=r_: e.tensor_copy(out=c3(gs(S_GP))[:, :, r_:r_ + 1], in_=v3(Gt)[:, :, 0:1]), ["g_t"], ["g_gp"])
                    P.op("pool", lambda e, r_=r_: e.tensor_copy(out=c3(gs(S_GE))[:, :, r_:r_ + 1], in_=v3(Gt)[:, :, 8:9]), ["g_t"], ["g_ge"])
            AK = "g_ac" if sample else "g_ig"
            GK = "g_gc" if sample else "g_t"
            MK = "g_mc" if sample else "g_M"
            if MIXSTOP <= 2:
                return
            P.op("pe", lambda e: e.transpose(out=PS[7][:, 258:262], in_=Ac, identity=ident_f[0:4, 0:4]), [AK, "ident_f"], ["b7c"], sig=False)
            P.op("pe", lambda e: e.transpose(out=PS[7][:, 262:266], in_=Gc, identity=ident_f[0:4, 0:4]), [GK, "ident_f"], ["b7c"], sig=False)
            P.op("pe", lambda e: e.transpose(out=PS[7][:, 266:270], in_=Mc, identity=ident_f[0:4, 0:4]), [MK, "ident_f"], ["b7c"])
            P.op("dve", lambda e: e.tensor_copy(out=SC(72, 84), in_=PS[7][:, 258:270]), ["b7c"], ["cols"])
            P.op("dve", lambda e: e.tensor_scalar(out=cAp, in0=cA, scalar1=LNC, scalar2=None, op0=ALU.add), ["cols"], ["cAp"])
            if not sample:
                P.op("dve", lambda e: e.tensor_scalar(out=gt[:, S_DG, 0:4], in0=ident_f[0:4, 0:4], scalar1=G_end, scalar2=None, op0=ALU.mult),
                     ["g_t", "ident_f"], ["g_dg"])
                P.op("pe", lambda e: e.matmul(PS[7][:, 270:274], gs(S_ONE), gt[:, S_DG, 0:4], start=True, stop=True),
                     ["g_one", "g_dg"], ["b7d"])
                P.op("dve", lambda e: e.tensor_copy(out=Gend_bc, in_=PS[7][:, 270:274]), ["b7d"], ["Gend_bc"])
                gp_col, ge_col = Gprev_bc, Gend_bc
                GPK, GEK = "Gprev_bc", "Gend_bc"
            else:
                P.op("pe", lambda e: e.transpose(out=PS[7][:, 270:274], in_=gs(S_GP), identity=ident_f[0:4, 0:4]), ["g_gp", "ident_f"], ["b7d"], sig=False)
                P.op("pe", lambda e: e.transpose(out=PS[7][:, 274:278], in_=gs(S_GE), identity=ident_f[0:4, 0:4]), ["g_ge", "ident_f"], ["b7d"])
                P.op("dve", lambda e: e.tensor_copy(out=small[:, 132:140], in_=PS[7][:, 270:278]), ["b7d"], ["cols_s"])
                gp_col, ge_col = cGp_s, cGe_s
                GPK, GEK = "cols_s", "cols_s"
            P.op("dve", lambda e: e.tensor_tensor(out=d1, in0=gp_col, in1=cG, op=ALU.subtract), [GPK, "cols"], ["d1"])
            P.op("act", lambda e: e.activation(out=winter, in_=d1, func=AF.Exp), ["d1"], ["winter"])
            P.op("act", lambda e: e.activation(out=floor_, in_=cM, func=AF.Exp, scale=-1.0), ["cols"], ["floor"])
            P.op("dve", lambda e: e.tensor_tensor(out=d2, in0=cAp, in1=ge_col, op=ALU.subtract), ["cAp", GEK], ["d2"])
            P.op("act", lambda e: e.activation(out=wk, in_=d2, func=AF.Exp), ["d2"], ["wk"])
            if not sample:
                P.op("dve", lambda e: e.tensor_tensor(out=d3, in0=Gprev_bc, in1=Gend_bc, op=ALU.subtract), ["Gprev_bc", "Gend_bc"], ["d3"])
                P.op("act", lambda e: e.activation(out=dec, in_=d3, func=AF.Exp), ["d3"], ["dec"])
                P.op("dve", lambda e: e.tensor_copy(out=gt[:, S_CAR, 0:1], in_=Bt[:, 127:128]), ["g_B", "g_car"], ["g_car"])
                P.op("dve", lambda e: e.tensor_copy(out=gt[:, S_CAR, 1:2], in_=Gt[:, 127:128]), ["g_t", "g_car"], ["g_car"])
                P.op("dve", lambda e: e.tensor_copy(out=Gprev_bc, in_=Gend_bc), ["Gend_bc", "Gprev_bc"], ["Gprev_bc"])
            else:
                v3 = lambda ap: ap.rearrange("p (j r) -> p j r", j=16)
                drow = gt[:, S_DG, 16:32]
                P.op("dve", lambda e: e.tensor_tensor(out=drow.rearrange("p (j o) -> p j o", o=1), in0=v3(Gt)[:, :, 0:1], in1=v3(Gt)[:, :, 8:9],
                                                      op=ALU.subtract), ["g_t"], ["g_dg"])
                P.op("act", lambda e: e.activation(out=drow, in_=drow, func=AF.Exp), ["g_dg"], ["g_dg"])
                for h in range(4):
                    P.op("pe", lambda e, h=h: e.matmul(PS[7][:, 278 + 16 * h:278 + 16 * (h + 1)], sel[:, h, :], drow, start=True, stop=True),
                         ["sel", "g_dg"], ["b7e"], sig=(h == 3))
                P.op("dve", lambda e: e.tensor_copy(out=small[:, 160:224], in_=PS[7][:, 278:342]), ["b7e"], ["decbc"])

            if MIXSTOP <= 3:
                return
            want_f32 = (t == 15)
            for cc in range(8):
                bank = 3 + cc % 2
                col = (cc // 2) * 128
                key = "b%d" % bank
                for kc in range(8):
                    P.op("pe", lambda e, bank=bank, col=col, cc=cc, kc=kc: e.matmul(
                        PS[bank][:, col:col + 128], w_in_sb[:, kc, 1024 + cc * 128:1024 + (cc + 1) * 128], x1T[:, kc, :],
                        start=(kc == 0), stop=(kc == 7)), XK + ["w_in%d" % kc], [key], sig=(kc == 7))
                if not sample:
                    P.op("act", lambda e, bank=bank, col=col, cc=cc: e.activation(
                        out=ext_bf[:, cc, 3:131], in_=PS[bank][:, col:col + 128], func=AF.Identity, bias=colsC[:, cc:cc + 1], scale=1.0),
                        [key, "colsC"], ["extb%d" % cc])
                    if want_f32:
                        P.op("act", lambda e, bank=bank, col=col, cc=cc: e.activation(
                            out=ext[:, cc, 3:131], in_=PS[bank][:, col:col + 128], func=AF.Identity, bias=colsC[:, cc:cc + 1], scale=1.0),
                            [key, "colsC"], ["ext%d" % cc])
                    cs = 342 if cc % 2 == 0 else 128
                    for j in range(4):
                        P.op("pe", lambda e, cs=cs, cc=cc, j=j: e.matmul(
                            PS[5][:, cs:cs + 128], Dg[:, j * 8 + cc, :], ext_bf[:, cc, j:j + 128], start=(j == 0), stop=(j == 3)),
                            ["lnp", "extb%d" % cc, "extcb"], ["b5"], sig=(j == 3))
                    P.op("act", lambda e, cs=cs, cc=cc: e.activation(out=qkT_bf[:, cc, :], in_=PS[5][:, cs:cs + 128], func=AF.Silu,
                                                                     bias=colsC[:, 8 + cc:9 + cc], scale=1.0), ["b5", "colsC"], ["qk%d" % cc])
                else:
                    P.op("act", lambda e, bank=bank, col=col, cc=cc: e.activation(
                        out=ext_s[:, cc, :, 3:11], in_=PS[bank][:, col:col + 128].rearrange("p (j r) -> p j r", j=16),
                        func=AF.Identity, bias=colsC[:, cc:cc + 1], scale=1.0), [key, "colsC", "extc"], ["ext%d" % cc])
            if sample:
                for cc in range(8):
                    ca = Dm[cc % 2]
                    ck = "Dm%d" % (cc % 2)
                    src = lambda j, cc=cc: ext_s[:, cc, :, j:j + 8]
                    cav = ca[:, :].rearrange("p (j r) -> p j r", j=16)
                    P.op("dve", lambda e, cc=cc, src=src, cav=cav: e.tensor_scalar(out=cav, in0=src(0), scalar1=colsC[:, 16 + cc:17 + cc],
                                                                                 scalar2=None, op0=ALU.mult),
                         ["ext%d" % cc, "extc", "colsC"], [ck])
                    for j in range(1, 4):
                        P.op("dve", lambda e, cc=cc, j=j, src=src, cav=cav: e.scalar_tensor_tensor(
                            out=cav, in0=src(j), scalar=colsC[:, 16 + j * 8 + cc:17 + j * 8 + cc], in1=cav, op0=ALU.mult, op1=ALU.add),
                            ["ext%d" % cc, "extc", ck], [ck])
                    P.op("act", lambda e, cc=cc, ca=ca: e.activation(out=qkT_bf[:, cc, :], in_=ca[:, :], func=AF.Silu,
                                                                     bias=colsC[:, 8 + cc:9 + cc], scale=1.0), [ck, "colsC"], ["qk%d" % cc])
                    if cc < 4:
                        P.op("act", lambda e, cc=cc, ca=ca: e.activation(out=qTf[:, cc, :], in_=ca[:, :], func=AF.Silu,
                                                                         bias=colsC[:, 8 + cc:9 + cc], scale=1.0), [ck, "colsC"], ["qTf%d" % cc])
            EXK = ["ext%d" % cc for cc in range(8)]
            if t == 15:
                for cc in range(8):
                    bank = 1 + cc // 4
                    P.op("pe", lambda e, cc=cc, bank=bank: e.transpose(out=PS[bank][0:3, (cc % 4) * 128:(cc % 4 + 1) * 128],
                                                                       in_=ext[:, cc, 128:131], identity=ident_f[:]),
                         ["ext%d" % cc, "ident_f"], ["b%d" % bank], sig=(cc % 4 == 3))
                P.op("dve", lambda e: e.tensor_copy(out=mixf[0:3, 0:512], in_=PS[1][0:3, :]), ["b1"], ["mixf"])
                P.op("act", lambda e: e.activation(out=vn[0:3, 0:512], in_=PS[2][0:3, :], func=AF.Copy), ["b2"], ["vn"])
                P.dma("sp", lambda e: e.dma_start(out=conv_p[:, 0:512], in_=mixf[0:3, 0:512]), ["mixf"], [], out=True)
                P.dma("sp", lambda e: e.dma_start(out=conv_p[:, 512:1024], in_=vn[0:3, 0:512]), ["vn"], [], out=True)
            if not sample and t < 15:
                P.op("pool", lambda e: e.tensor_copy(out=ext_bf[:, :, 0:3], in_=ext_bf[:, :, 128:131]),
                     ["extb%d" % c_ for c_ in range(8)] + ["extcb"], ["extcb"])
            if sample:
                P.op("pool", lambda e: e.tensor_copy(out=convc[:, :, :].rearrange("p a (j r) -> p a j r", j=16), in_=ext_s[:, :, :, 8:11]),
                     EXK, ["convc"])
                for cc in range(8):
                    bank = 1 + cc // 4
                    P.op("pe", lambda e, cc=cc, bank=bank: e.transpose(out=PS[bank][0:48, (cc % 4) * 128:(cc % 4 + 1) * 128],
                                                                       in_=convc[:, cc, :], identity=ident_f[:]),
                         ["convc", "ident_f"], ["b%d" % bank], sig=(cc % 4 == 3))
                P.op("dve", lambda e: e.tensor_copy(out=mixf[0:48, 0:512], in_=PS[1][0:48, :]), ["b1"], ["mixf"])
                P.op("act", lambda e: e.activation(out=vn[0:48, 0:512], in_=PS[2][0:48, :], func=AF.Copy), ["b2"], ["vn"])
                P.dma("sp", lambda e: e.dma_start(out=conv_s[:, 0:512], in_=mixf[0:48, 0:512]), ["mixf"], [], out=True)
                P.dma("sp", lambda e: e.dma_start(out=conv_s[:, 512:1024], in_=vn[0:48, 0:512]), ["vn"], [], out=True)

            if MIXSTOP <= 4:
                return
            zblock(1, 512, 1)
            P.op("act", lambda e: e.activation(out=vn[:, :], in_=PS[1][:, :], func=AF.Gelu_apprx_tanh), ["b1"], ["vn"])
            zblock(0, 0, 2)
            P.op("act", lambda e: e.activation(out=u_act[:, :], in_=PS[2][:, :], func=AF.Gelu_apprx_tanh), ["b2"], ["u_act"])
            for h in range(4):
                P.op("dve", lambda e, h=h: e.bn_stats(out=st4[:, h, :], in_=vn[:, h * 128:(h + 1) * 128]), ["vn"], ["st4_%d" % h])
                P.op("dve", lambda e, h=h: e.bn_aggr(out=mv4[:, h, :], in_=st4[:, h, :]), ["st4_%d" % h], ["mv4_%d" % h])
            MVK = ["mv4_%d" % h for h in range(4)]
            P.op("act", lambda e: e.activation(out=sd4, in_=mv4[:, :, 1], func=AF.Ln, bias=LN_EPS, scale=1.0), MVK, ["sd4"])
            P.op("act", lambda e: e.activation(out=rstd4, in_=sd4, func=AF.Exp, scale=-0.5), ["sd4"], ["rstd4"])
            P.op("dve", lambda e: e.scalar_tensor_tensor(out=nmr4, in0=mv4[:, :, 0], scalar=-1.0, in1=rstd4, op0=ALU.mult, op1=ALU.mult),
                 MVK + ["rstd4"], ["nmr4"])
            for h in range(4):
                P.op("dve", lambda e, h=h: e.tensor_scalar(out=vn[:, h * 128:(h + 1) * 128], in0=vn[:, h * 128:(h + 1) * 128],
                                                           scalar1=rstd4[:, h:h + 1], scalar2=nmr4[:, h:h + 1], op0=ALU.mult, op1=ALU.add),
                     ["vn", "rstd4", "nmr4"], ["vn"])
            P.op("dve", lambda e: e.tensor_tensor(out=vn[:, :], in0=vn[:, :], in1=gmln[:, 0, :], op=ALU.mult), ["vn", "gmln"], ["vn"])
            P.op("dve", lambda e: e.tensor_tensor(out=vn[:, :], in0=vn[:, :], in1=gmln[:, 1, :], op=ALU.add), ["vn", "gmln"], ["vn"])
            P.op("dve", lambda e: e.tensor_copy(out=vn_bf[:, :], in_=vn[:, :]), ["vn"], ["vn_bf"])
            if t == 15:
                P.dma("sp", lambda e: e.dma_start(out=gmv_p[:, :], in_=vn[:, :]), ["vn"], [], out=True)
            if sample:
                P.dma("sp", lambda e: e.dma_start(out=gmv_s[:, :], in_=vn[:, :]), ["vn"], [], out=True)
            zblock(2, 2048, 1)
            P.op("act", lambda e: e.activation(out=v_ext[:, :, 0:128], in_=PS[1][:, :].rearrange("p (a b) -> p a b", a=4), func=AF.Copy),
                 ["b1"], ["v_ext"])
            zblock(3, 2560, 2)
            P.op("act", lambda e: e.activation(out=o_sig[:, :], in_=PS[2][:, :], func=AF.Sigmoid), ["b2"], ["o_sig"])

            if MIXSTOP <= 5:
                return
            phase[0] = "B"
            wsx = wsT_s if sample else wsT
            wsk = "wsT_s" if sample else "wsT"
            bsc = 52 if sample else 48
            for h in range(4):
                P.op("pe", lambda e, h=h: e.matmul(PS[3][:, h * 128:(h + 1) * 128], wsx[:, h, :], vn_bf[:, h * 128:(h + 1) * 128],
                                                   start=True, stop=True), [wsk, "vn_bf"], ["b3_%d" % h])
                P.op("dve", lambda e, h=h: e.scalar_tensor_tensor(out=mixf[:, h * 128:(h + 1) * 128], in0=PS[3][:, h * 128:(h + 1) * 128],
                                                                  scalar=colsC[:, bsc + h:bsc + h + 1], in1=u_act[:, h * 128:(h + 1) * 128],
                                                                  op0=ALU.add, op1=ALU.mult), ["b3_%d" % h, "u_act", "colsC"], ["mixf"])
            head_rms_to_mixbf(0)

            if MIXSTOP <= 6:
                return
            ps4_bf = PS[4][:, 0:256].bitcast(BF16)
            for h in range(4):
                P.op("pe", lambda e, h=h: e.transpose(out=ps4_bf[:, h * 128:(h + 1) * 128], in_=qkT_bf[:, 4 + h, :], identity=ident_bf[:]),
                     ["qk%d" % (4 + h), "ident_bf"], ["b4_0", "b4_1"], sig=(h == 3))
            for h in range(4):
                P.op("dve", lambda e, h=h: e.tensor_scalar(out=kw_bf[:, h, :], in0=ps4_bf[:, h * 128:(h + 1) * 128], scalar1=wk[:, h:h + 1],
                                                           scalar2=None, op0=ALU.mult), ["b4_0", "b4_1", "wk"], ["kw%d" % h])
            nm = negmask_s if sample else negmask
            if sample:
                for h in range(4):
                    P.op("pe", lambda e, h=h: e.transpose(out=PS[4][:, h * 128:(h + 1) * 128], in_=qTf[:, h, :], identity=ident_f[:]),
                         ["qTf%d" % h, "ident_f"], ["b4_%d" % h], sig=(h == 3))
                B4K = ["b4_%d" % h for h in range(4)]
                P.op("act", lambda e: e.activation(out=u_act[:, :], in_=PS[4][:, :], func=AF.Copy), B4K, ["u_act"])
                for r_ in range(8):
                    P.dma("sp", lambda e, r_=r_: e.dma_start(out=mixf[r_:128:8, :], in_=sn.rearrange("(j h) d -> j (h d)", h=4)),
                          [], ["mixf"])
                for h in range(4):
                    P.op("dve", lambda e, h=h: e.scalar_tensor_tensor(out=junk[:, 0:128], in0=u_act[:, h * 128:(h + 1) * 128], scalar=1.0,
                                                                      in1=mixf[:, h * 128:(h + 1) * 128], op0=ALU.mult, op1=ALU.mult,
                                                                      accum_out=dq[:, h:h + 1]), ["u_act", "mixf"], ["dq%d" % h, "junk"])
                P.dma("sp", lambda e: e.dma_start(out=n0row[0:64, :], in_=sn[:, :]), [], ["n0row"])
                P.op("pe", lambda e: e.transpose(out=PS[7][:, 342:406], in_=n0row[0:64, :], identity=ident_f[0:64, 0:64]),
                     ["n0row", "ident_f"], ["b7f"])
                P.op("dve", lambda e: e.tensor_copy(out=n0T[:, :], in_=PS[7][:, 342:406]), ["b7f"], ["n0T"])
                for j in range(16):
                    cb = C0s[j % 2]
                    ckey = "C0s%d" % (j % 2)
                    P.dma("sp", lambda e, j=j, cb=cb: e.dma_start(out=cb[:, :, :], in_=sC[j].rearrange("h d e -> d h e")), [], [ckey])
                    for h in range(4):
                        P.op("pe", lambda e, j=j, h=h, cb=cb: e.matmul(PS[3][:, h * 128 + j * 8:h * 128 + j * 8 + 8], cb[:, h, :],
                                                                       qTf[:, h, j * 8:(j + 1) * 8], start=True, stop=True),
                             [ckey, "qTf%d" % h], ["b3_%d" % h])
                    vmk = vmask[j % 2]
                    vkey = "vmask%d" % (j % 2)
                    P.op("dve", lambda e, j=j, vmk=vmk: e.tensor_scalar(out=vmk[:, :, :], in0=v_ext[:, :, :], scalar1=onehot[:, j:j + 1],
                                                                        scalar2=None, op0=ALU.mult), ["v_ext", "onehot"], [vkey])
                    for h in range(4):
                        off = 0
                        pb_ = 1 + h % 2
                        pk = "b%d" % pb_
                        P.op("pe", lambda e, h=h, off=off, vmk=vmk, pb_=pb_: e.matmul(PS[pb_][:, off:off + 129], kw_bf[:, h, :], vmk[:, h, 0:129],
                                                                            start=True, stop=True), ["kw%d" % h, vkey], [pk])
                        P.op("dve", lambda e, j=j, h=h, off=off, cb=cb, pb_=pb_: e.scalar_tensor_tensor(
                            out=cb[:, h, :], in0=cb[:, h, :], scalar=decbc[:, h, j:j + 1], in1=PS[pb_][:, off:off + 128],
                            op0=ALU.mult, op1=ALU.add), [ckey, pk, "decbc"], [ckey])
                        P.op("dve", lambda e, j=j, h=h, off=off, pb_=pb_: e.scalar_tensor_tensor(
                            out=nnewT[:, j * 4 + h:j * 4 + h + 1], in0=n0T[:, j * 4 + h:j * 4 + h + 1], scalar=decbc[:, h, j:j + 1],
                            in1=PS[pb_][:, off + 128:off + 129], op0=ALU.mult, op1=ALU.add), ["n0T", pk, "decbc"], ["nnewT"])
                    P.dma("sp", lambda e, j=j, cb=cb: e.dma_start(out=C_s[j].rearrange("h d e -> d h e"), in_=cb[:, :, :]), [ckey], [], out=True)
                P.op("pe", lambda e: e.transpose(out=PS[7][0:64, 342:470], in_=nnewT[:, :], identity=ident_f[:]), ["nnewT", "ident_f"], ["b7f"])
                P.op("dve", lambda e: e.tensor_copy(out=n0row[0:64, :], in_=PS[7][0:64, 342:470]), ["b7f"], ["n0row"])
                P.dma("sp", lambda e: e.dma_start(out=n_s[:, :], in_=n0row[0:64, :]), ["n0row"], [], out=True)
                B3K = ["b3_%d" % h for h in range(4)]
                P.op("act", lambda e: e.activation(out=vn[:, :], in_=PS[3][:, :], func=AF.Copy), B3K, ["vn"])
                for h in range(4):
                    P.op("pe", lambda e, h=h: e.transpose(out=PS[4][:, h * 128:(h + 1) * 128], in_=vn[:, h * 128:(h + 1) * 128], identity=ident_f[:]),
                         ["vn", "ident_f"], ["b4_%d" % h])

            for h in range(4):
                s_ = h % 2
                bank = 5 + s_
                ka, kb, kc_ = "b%da" % bank, "b%db" % bank, "b%dc" % bank
                P.op("pe", lambda e, h=h, bank=bank: e.matmul(PS[bank][:, 0:128], sel[:, h, :], Gc, start=True, stop=True),
                     ["sel", GK], [ka])
                P.op("dve", lambda e, s_=s_, bank=bank: e.scalar_tensor_tensor(out=Dm[s_][:, :], in0=PS[bank][:, 0:128], scalar=-1.0,
                                                                               in1=nm[:, :], op0=ALU.mult, op1=ALU.add),
                     [ka, "negmask", "negmask_s"], ["Dm%d" % s_])
                P.op("act", lambda e, s_=s_, h=h: e.activation(out=Dm[s_][:, :], in_=Dm[s_][:, :], func=AF.Exp, bias=cAp[:, h:h + 1], scale=1.0),
                     ["Dm%d" % s_, "cAp"], ["Dm%d" % s_])
                P.op("pe", lambda e, h=h, bank=bank: e.matmul(PS[bank][:, 128:256], qkT_bf[:, 4 + h, :], qkT_bf[:, h, :], start=True, stop=True),
                     ["qk%d" % h, "qk%d" % (4 + h)], [kb])
                P.op("dve", lambda e, s_=s_, bank=bank: e.tensor_tensor(out=Pb[s_][:, :], in0=PS[bank][:, 128:256], in1=Dm[s_][:, :], op=ALU.mult),
                     [kb, "Dm%d" % s_], ["Pb%d" % s_])
                P.op("pe", lambda e, h=h, s_=s_, bank=bank: e.matmul(PS[bank][:, 256:385], Pb[s_][:, :], v_ext[:, h, 0:129], start=True, stop=True),
                     ["Pb%d" % s_, "v_ext"], [kc_])
                if not sample:
                    P.op("pe", lambda e, h=h: e.matmul(PS[1][:, 0:129], qkT_bf[:, h, :], Cbf[:, h, 0:129], start=True, stop=True),
                         ["qk%d" % h, "Cbf%d" % h], ["b1"])
                    P.op("dve", lambda e, h=h: e.tensor_scalar(out=itw[:, 0:129], in0=PS[1][:, 0:129], scalar1=winter[:, h:h + 1], scalar2=None,
                                                               op0=ALU.mult), ["b1", "winter"], ["itw"])
                else:
                    P.op("act", lambda e, h=h: e.activation(out=itw[:, 0:128], in_=PS[4][:, h * 128:(h + 1) * 128], func=AF.Identity,
                                                            scale=winter[:, h:h + 1]), ["b4_%d" % h, "winter"], ["itw"])
                    P.op("dve", lambda e, h=h: e.tensor_tensor(out=itw[:, 128:129], in0=dq[:, h:h + 1], in1=winter[:, h:h + 1], op=ALU.mult),
                         ["dq%d" % h, "winter", "itw"], ["itw"])
                P.op("dve", lambda e, bank=bank: e.tensor_tensor(out=itw[:, 0:129], in0=PS[bank][:, 256:385], in1=itw[:, 0:129], op=ALU.add),
                     [kc_, "itw"], ["itw"])
                P.op("dve", lambda e: e.scalar_tensor_tensor(out=dd, in0=itw[:, 128:129], scalar=-1.0, in1=itw[:, 128:129],
                                                             op0=ALU.mult, op1=ALU.max), ["itw"], ["dd"])
                P.op("dve", lambda e, h=h: e.tensor_tensor(out=dd, in0=dd, in1=floor_[:, h:h + 1], op=ALU.max), ["dd", "floor"], ["dd"])
                P.op("dve", lambda e: e.reciprocal(out=rd, in_=dd), ["dd"], ["rd"])
                P.op("dve", lambda e, h=h: e.scalar_tensor_tensor(out=mixf[:, h * 128:(h + 1) * 128], in0=itw[:, 0:128], scalar=rd,
                                                                  in1=o_sig[:, h * 128:(h + 1) * 128], op0=ALU.mult, op1=ALU.mult),
                     ["itw", "rd", "o_sig", "mix_bf0"], ["mixf"])
                if not sample:
                    P.op("pe", lambda e, h=h: e.matmul(PS[2][:, 0:129], kw_bf[:, h, :], v_ext[:, h, 0:129], start=True, stop=True),
                         ["kw%d" % h, "v_ext"], ["b2"])
                    P.op("dve", lambda e, h=h: e.scalar_tensor_tensor(out=Cst[:, h, 0:129], in0=Cst[:, h, 0:129], scalar=dec[:, h:h + 1],
                                                                      in1=PS[2][:, 0:129], op0=ALU.mult, op1=ALU.add),
                         ["Cst%d" % h, "dec", "b2"], ["Cst%d" % h])
                    P.op("dve", lambda e, h=h: e.tensor_copy(out=Cbf[:, h, 0:129], in_=Cst[:, h, 0:129]), ["Cst%d" % h], ["Cbf%d" % h])
            head_rms_to_mixbf(512)

            if MIXSTOP <= 7:
                return
            if t == 15:
                P.dma("sp", lambda e: e.dma_start(out=m_p[:, :], in_=Mt[:, 127:128]), ["g_M"], [], out=True)
                CK = ["Cst%d" % h for h in range(4)]
                P.dma("sp", lambda e: e.dma_start(out=C_p.rearrange("h d e -> d h e"), in_=Cst[:, :, 0:128]), CK, [], out=True)
                P.op("pool", lambda e: e.tensor_copy(out=ncol, in_=Cst[:, :, 128]), CK, ["ncol"])
                P.op("pe", lambda e: e.transpose(out=PS[7][0:4, 342:470], in_=ncol, identity=ident_f[:]), ["ncol", "ident_f"], ["b7f"])
                P.op("dve", lambda e: e.tensor_copy(out=gs(S_NR), in_=PS[7][0:4, 342:470]), ["b7f"], ["g_nr"])
                P.dma("sp", lambda e: e.dma_start(out=n_p[:, :], in_=gs(S_NR)), ["g_nr"], [], out=True)
            if sample:
                v3 = lambda ap: ap.rearrange("p (j r) -> p j r", j=16)
                P.op("pool", lambda e: e.tensor_copy(out=gt[:, S_DG, 32:48].rearrange("p (j o) -> p j o", o=1), in_=v3(Mt)[:, :, 8:9]), ["g_M", "g_dg"], ["g_dg"])
                P.dma("sp", lambda e: e.dma_start(out=m_s.rearrange("j h -> h j"), in_=gt[:, S_DG, 32:48], allow_slow_non_contiguous=True),
                      ["g_dg"], [], out=True)

            if MIXSTOP <= 8:
                return
            phase[0] = "C"
            for kc in range(8):
                bank = 0
                P.op("pe", lambda e, kc=kc: e.transpose(out=PS[0][:, :].bitcast(BF16)[:, kc * 128:(kc + 1) * 128],
                                                        in_=mix_bf[:, kc * 128:(kc + 1) * 128], identity=ident_bf[:]),
                     ["mix_bf0", "mix_bf1", "ident_bf"], ["b0"], sig=(kc == 7))
            P.op("dve", lambda e: e.tensor_copy(out=mixT[:, :, :], in_=PS[0][:, :].bitcast(BF16).rearrange("p (a b) -> p a b", a=8)),
                 ["b0"], ["mixT"])
            for half in range(2):
                bank = 1 + half
                for kc in range(8):
                    P.op("pe", lambda e, kc=kc, half=half, bank=bank: e.matmul(PS[bank][:, :], mixT[:, kc, :],
                                                                               w_out_sb[:, kc, half * 512:(half + 1) * 512],
                                                                               start=(kc == 0), stop=(kc == 7)),
                         ["mixT", "w_out"], ["b%d" % bank], sig=(kc == 7))
                P.op("dve", lambda e, half=half, bank=bank: e.scalar_tensor_tensor(
                    out=acc[:, t, half * 512:(half + 1) * 512], in0=PS[bank][:, :], scalar=1.0 / ALPHA,
                    in1=acc[:, t, half * 512:(half + 1) * 512], op0=ALU.mult, op1=ALU.add), [k, "b%d" % bank], [k])
            layer_norm_tile(t, affine=False, norm_on_dve=True)

        def mixer_stage(tiles):
            P.keymap = mixer_keymap
            phase[0] = "A"
            par[0] = 0
            win_v = wd_["w_in"].rearrange("(kc kp) f -> kp kc f", kp=128)
            AK_ = ["wg0_%d" % c for c in range(NPC)] + ["wu0_%d" % c for c in range(NPC)] + ["wd0_%d" % c for c in range(NPC)]
            BK_ = ["wg1_%d" % c for c in range(NPC)] + ["wu1_%d" % c for c in range(NPC)] + ["wd1_%d" % c for c in range(NPC)]
            HK_ = ["hT%d_%d" % (b_, c) for b_ in range(2) for c in range(NPC)] + ["sg0", "sg1"]
            for kc in range(8):
                aft = AK_ if kc < 3 else (AK_ + BK_ + HK_ if kc == 3 else BK_ + HK_)
                P.dma("pool", lambda e, kc=kc: e.dma_start(out=w_in_sb[:, kc, :], in_=win_v[:, kc, :]), r=["__nobar__"], w=["__nobar__", "w_in%d" % kc],
                      after=aft)
            P.dma("pool", lambda e: e.dma_start(out=w_out_sb[:, :, :], in_=wd_["w_out"].rearrange("(kc kp) f -> kp kc f", kp=128)),
                  r=["__nobar__"], w=["__nobar__", "w_out"], after=HK_ + ["lnjunk"])
            for kc in range(8):
                if kc % 2 == 0:
                    P.op("act", lambda e, kc=kc: e.activation(out=w_out_sb[:, kc, :], in_=w_out_sb[:, kc, :], func=AF.Identity,
                                                              scale=colsC[:, 56 + kc:57 + kc]), ["__nobar__", "w_out", "colsC"], ["__nobar__", "w_out"])
                else:
                    P.op("dve", lambda e, kc=kc: e.tensor_scalar(out=w_out_sb[:, kc, :], in0=w_out_sb[:, kc, :], scalar1=colsC[:, 56 + kc:57 + kc],
                                                                 scalar2=None, op0=ALU.mult), ["__nobar__", "w_out", "colsC"], ["__nobar__", "w_out"])
            P.op("pool", lambda e: e.memset(extb[:, :], 0.0), (), ["extc"] + ["ext%d" % c for c in range(8)])
            P.op("pool", lambda e: e.memset(extbb[:, :], 0.0), (), ["extcb"] + ["extb%d" % c for c in range(8)])
            for idx in range(32):
                P.op("dve", lambda e, idx=idx: e.tensor_scalar(out=Dg[:, idx, :], in0=ident_bf[:, :], scalar1=colsC[:, 16 + idx:17 + idx],
                                                               scalar2=None, op0=ALU.mult), ["ident_bf", "colsC", "lnp"], ["lnp"])
            for q_ in range(2):
                par[0] = q_
                P.op("pool", lambda e, q_=q_: e.memset(v_ext2[q_][:, :, :], 1.0), (), ["v_ext"])
            par[0] = 0
            P.op("pool", lambda e: e.memset(gs(S_ONE, 144), 1.0), (), ["g_one"])
            P.op("pool", lambda e: e.memset(gs(S_RM, 144), 1.0), (), ["g_rm"])
            P.op("pool", lambda e: e.memset(gs(S_RM, 144).rearrange("p (j r) -> p j r", j=16)[:, :, 0:1], 0.0), ["g_rm"], ["g_rm"])
            P.op("pool", lambda e: e.memset(gs(S_RA, 144), 0.0), (), ["g_ra"])
            P.op("pool", lambda e: e.memset(gs(S_RA, 144).rearrange("p (j r) -> p j r", j=16)[:, :, 0:1], NEG), ["g_ra"], ["g_ra"])
            P.op("pool", lambda e: e.memset(gt[:, S_CAR, 0:2], 0.0), (), ["g_car"])
            P.op("pool", lambda e: e.memset(Gprev_bc, 0.0), (), ["Gprev_bc"])
            if 16 in tiles:
                tiles = [16] + [t_ for t_ in tiles if t_ != 16]
            for t in tiles:
                if t == 0:
                    phase[0] = "A"
                    par[0] = 0
                    P.op("pool", lambda e: e.memset(extb[:, :], 0.0), ["extc"] + ["ext%d" % c for c in range(8)],
                         ["extc"] + ["ext%d" % c for c in range(8)])
                    P.op("pool", lambda e: e.memset(extbb[:, :], 0.0), ["extcb"] + ["extb%d" % c for c in range(8)], ["extcb"] + ["extb%d" % c for c in range(8)])
                if t == 16:
                    phase[0] = "A"
                    par[0] = 0
                    P.op("pool", lambda e: e.memset(extb[:, :], 0.0), ["extc"] + ["ext%d" % c for c in range(8)],
                         ["extc"] + ["ext%d" % c for c in range(8)])
                    P.op("pool", lambda e: e.memset(extbb[:, :], 0.0), ["extcb"] + ["extb%d" % c for c in range(8)], ["extcb"] + ["extb%d" % c for c in range(8)])
                    P.dma("sp", lambda e: e.dma_start(out=mixf[0:48, :], in_=sconv[:, 0:512]), [], ["mixf"])
                    P.dma("sp", lambda e: e.dma_start(out=vn[0:48, :], in_=sconv[:, 512:1024]), [], ["vn"])
                    for cc in range(8):
                        srcb = mixf if cc < 4 else vn
                        P.op("pe", lambda e, cc=cc, srcb=srcb: e.transpose(out=PS[1][:, cc * 48:(cc + 1) * 48],
                                                                           in_=srcb[0:48, (cc % 4) * 128:(cc % 4 + 1) * 128],
                                                                           identity=ident_f[0:48, 0:48]),
                             ["mixf", "vn", "ident_f"], ["b1"], sig=(cc == 7))
                    P.op("dve", lambda e: e.tensor_copy(out=ext_s[:, :, :, 0:3], in_=PS[1][:, 0:384].rearrange("p (a j r) -> p a j r", a=8, j=16)),
                         ["b1", "extc"], ["extc"])
                    P.dma("sp", lambda e: e.dma_start(out=gt[:, S_NR, 128:144], in_=sm.rearrange("j h -> h j"),
                                                      allow_slow_non_contiguous=True), [], ["g_m0"])
                mixer_tile(t)
            P.keymap = None

        if _os.environ.get("SKIPFFN1"):
            for t in range(NT):
                src = x_p[t * 128:(t + 1) * 128, :] if t < 16 else x_s[:, :]
                P.dma("sp", lambda e, t=t, src=src: e.dma_start(out=acc[:, t, :], in_=src), w=["acc%d" % t])
        else:
            load_lnp("ln1_g", "ln1_b")
            ffn_stage(1, True)

        if dbg == 1:
            dbg_o = dout("dbg", [NTOK, D])
            for t in range(NT):
                P.dma("sp", lambda e, t=t: e.dma_start(out=dbg_o[t * 128:(t + 1) * 128, :], in_=acc[:, t, :]), ["acc%d" % t], [], out=True)
        else:
            P.barrier("BAR1", lambda e: e.memset(small[:, 318:319], 0.0))
            mixer_stage([int(v) for v in _os.environ['MIXLIST'].split(',')] if _os.environ.get('MIXLIST') else [t for t in (list(range(NT)) if dbg != 2 else list(range(16))) if t < MIXTILES])
            if dbg in (2, 3):
                dbg_o = dout("dbg", [NTOK, D])
                for t in range(NT):
                    P.dma("sp", lambda e, t=t: e.dma_start(out=dbg_o[t * 128:(t + 1) * 128, :], in_=acc[:, t, :]), ["acc%d" % t], [], out=True)
            else:
                P.barrier("BAR2", lambda e: e.memset(small[:, 319:320], 0.0))
                load_lnp("ln2_g", "ln2_b")
                ffn_stage(2, False, pre_affine=True, lnp_after_p0=("ln3_g", "ln3_b"))
                for t in range(NT):
                    dst = y_p[t * 128:(t + 1) * 128, :] if t < 16 else y_s[:, :]
                    P.dma("sp", lambda e, t=t, dst=dst: e.dma_start(out=dst, in_=acc[:, t, :]), ["acc%d" % t], [], out=True)

        sems = {}
        for nm_ in P.sem_names():
            sems[nm_] = es.enter_context(nc.semaphore(nm_))
        with nc.Block() as block:
            P.emit(block, sems)
    return nc


def _prep_inputs(inputs):
    f32 = lambda a: np.ascontiguousarray(np.asarray(a, dtype=np.float32))
    shared = {
        "f1_wg": f32(inputs["ffn1_wg"][0]), "f1_wu": f32(inputs["ffn1_wu"][0]), "f1_wd": f32(inputs["ffn1_wd"][0]),
        "ln1_g": f32(inputs["ln1_g"]), "ln1_b": f32(inputs["ln1_b"]),
        "w_in": f32(inputs["w_in"][0]), "b_in": f32(inputs["b_in"]),
        "gm_ln_g": f32(inputs["gm_ln_g"]).reshape(1, 512), "gm_ln_b": f32(inputs["gm_ln_b"]).reshape(1, 512),
        "gm_ws": f32(inputs["gm_ws"][0]), "gm_bs": f32(inputs["gm_bs"][0]),
        "conv_w": f32(inputs["conv_w"][0]), "conv_b": f32(inputs["conv_b"]),
        "gm_out_g": f32(inputs["gm_out_g"]).reshape(1, 512), "ml_out_g": f32(inputs["ml_out_g"]).reshape(1, 512),
        "w_out": f32(inputs["w_out"][0]), "ln2_g": f32(inputs["ln2_g"]), "ln2_b": f32(inputs["ln2_b"]),
        "f2_wg": f32(inputs["ffn2_wg"][0]), "f2_wu": f32(inputs["ffn2_wu"][0]), "f2_wd": f32(inputs["ffn2_wd"][0]),
        "ln3_g": f32(inputs["ln3_g"]), "ln3_b": f32(inputs["ln3_b"]),
    }
    xp = f32(inputs["x_prompt"]); xs = f32(inputs["x_sample"])
    sc = f32(inputs["state_conv"][0]); sC = f32(inputs["state_C"][0]); sn = f32(inputs["state_n"][0]); sm = f32(inputs["state_m"][0])
    maps = []
    for i in range(NCORES):
        m = dict(shared)
        sl = slice(16 * i, 16 * i + 16)
        m["x_p"] = xp[i]
        m["x_s"] = np.ascontiguousarray(xs[sl].reshape(128, D))
        m["sconv"] = np.ascontiguousarray(sc[sl].reshape(48, D))
        m["sC"] = np.ascontiguousarray(sC[sl])
        m["sn"] = np.ascontiguousarray(sn[sl].reshape(64, 128))
        m["sm"] = np.ascontiguousarray(sm[sl])
        maps.append(m)
    return maps


def kernel(**inputs):
    maps = _prep_inputs(inputs)
    nc = build_program()
    res = run_bass_kernel_spmd(nc, maps, core_ids=list(range(NCORES)))
    R = res.results
    cat = lambda k: [np.asarray(r[k]) for r in R]
    y_p = np.stack(cat("y_p"), 0)
    y_s = np.concatenate(cat("y_s"), 0).reshape(128, 8, D)
    gmv_p = np.stack(cat("gmv_p"), 0).reshape(1, 8, 128, 4, 128)
    gmv_s = np.concatenate(cat("gmv_s"), 0).reshape(1, 128, 8, 4, 128)
    conv_p = np.stack(cat("conv_p"), 0).reshape(1, 8, 3, D)
    conv_s = np.concatenate(cat("conv_s"), 0).reshape(1, 128, 3, D)
    C_p = np.stack(cat("C_p"), 0).reshape(1, 8, 4, 128, 128)
    C_s = np.concatenate(cat("C_s"), 0).reshape(1, 128, 4, 128, 128)
    n_p = np.stack(cat("n_p"), 0).reshape(1, 8, 4, 128)
    n_s = np.concatenate(cat("n_s"), 0).reshape(1, 128, 4, 128)
    m_p = np.stack(cat("m_p"), 0).reshape(1, 8, 4)
    m_s = np.concatenate(cat("m_s"), 0).reshape(1, 128, 4)
    return (y_p, y_s, gmv_p, gmv_s, conv_p, conv_s, C_p, C_s, n_p, n_s, m_p, m_s)
```

```python
import math
import os as _os0
import numpy as np
import concourse.bass as bass
import concourse.mybir as mybir
from concourse.bass_utils import run_bass_kernel_spmd

F32 = mybir.dt.float32
BF16 = mybir.dt.bfloat16
AF = mybir.ActivationFunctionType
ALU = mybir.AluOpType

NCORES = 8
D = 1024
DFF = 2816
NFC = DFF // 128
NT = 17
NTOK = NT * 128
ALPHA = 2.0 ** 0.25
LN_EPS = 1e-5
LNC = math.log(128.0 ** -0.5)
NEG = -1.0e30
PASSES = [(0, 3), (3, 3), (6, 4), (10, 4), (14, 4), (18, 4)]
if _os0.environ.get("PASS_SIZES"):
    _ps = [int(v) for v in _os0.environ["PASS_SIZES"].split(",")]
    assert sum(_ps) == 22 and max(_ps) <= 4
    PASSES = [(sum(_ps[:i]), _ps[i]) for i in range(len(_ps))]
NPC = 4
GROUPS = [[0, 1, 2, 3], [4, 5, 6, 7], [8, 9, 10, 11], [12, 13, 14, 15], [16]]
NDS = 8
import os as _os0
ACT_PEN = float(_os0.environ.get('ACT_PEN', '0.3'))
XLAT = float(_os0.environ.get('XLAT', '0.3'))
CP_PRIO = int(_os0.environ.get('CP_PRIO', '0'))
JIT_SEED = int(_os0.environ.get('JIT_SEED', '0'))
JIT_AMP = float(_os0.environ.get('JIT_AMP', '0.2'))
SAME_LAT = float(_os0.environ.get('SAME_LAT', '0.06'))
PE_SCALE = float(_os0.environ.get('PE_SCALE', '1.0'))
DVE_FIX = float(_os0.environ.get('DVE_FIX', '0.15'))


class _Dummy:
    def then_inc(self, *a, **k):
        return self


class _Rec:
    def __init__(self):
        self.calls = []

    def __getattr__(self, name):
        def f(*a, **k):
            self.calls.append((name, a, k))
            return _Dummy()
        return f


def _free(ap):
    n = 1
    for d in ap.shape[1:]:
        n *= int(d)
    return n


_ACT_GROUP = {"Exp": "exp", "Ln": "exp", "Silu": "silu", "Gelu_apprx_tanh": "gelu", "Sigmoid": "sigmoid", "Sqrt": "sqrt"}


def _est(eng, name, a, k):
    if name == "matmul":
        rhs = a[2] if len(a) > 2 else k["rhs"]
        n = _free(rhs)
        m = 4.0 if rhs.dtype == F32 else 1.0
        return (0.02 + max(n, 64) * m * 0.00042) * PE_SCALE, 0.0, None
    if name == "transpose":
        in_ = k["in_"]
        return (0.02 + max(_free(in_), 64) * 0.00042) * PE_SCALE, 0.0, None
    if name == "dma_start":
        out = k["out"]
        tot = 1
        for d in out.shape:
            tot *= int(d)
        rows = max(1, tot // max(1, int(out.shape[-1])))
        esz = 2 if out.dtype == BF16 else 4
        issue = 0.1 if eng == "sp" else 0.5 + rows * 0.02
        return issue, 2.0 + tot * esz / 150e3, None
    out = k.get("out", a[0] if a else None)
    n = _free(out) if out is not None else 64
    if eng == "act":
        f = k.get("func")
        g = _ACT_GROUP.get(getattr(f, "name", str(f)).split(".")[-1]) if f is not None else None
        return 0.22 + n * 0.001 + (0.1 if k.get("accum_out") is not None else 0.0), 0.0, g
    if eng == "pool":
        return 0.3 + n * 0.0021, 0.0, None
    if name in ("bn_aggr",):
        return 0.2, 0.0, None
    return DVE_FIX + n * 0.00105 + (0.1 if k.get("accum_out") is not None else 0.0), 0.0, None


class Prog:
    ENG = ("pe", "act", "dve", "pool", "sp")
    WINDOW = int(_os0.environ.get('SWIN', '100'))
    ACT_PEN_DEFAULT = 0.0

    def __init__(self):
        self.nodes = []
        self.pending = {e: None for e in self.ENG}
        self.lastw = {}
        self.rd = {}
        self.seg = 0
        self.out_nodes = []
        self.ctx = None
        self.keymap = None
        self.last_rd_eng = {}

    @staticmethod
    def _canon(keys):
        return [k[:2] if (len(k) > 2 and k[0] == "b" and k[1] in "01234567") else k for k in keys]

    def fence(self):
        self.seg += 1

    def barrier(self, key, fn):
        node = {"eng": "pool", "fns": [fn], "kind": "op", "busy": 0.3, "lat": 0.0, "grp": None}
        nid = len(self.nodes)
        node["preds"] = set(range(nid))
        node["id"] = nid
        node["seg"] = self.seg
        self.nodes.append(node)
        self.lastw[key] = nid
        self.rd[key] = set()
        return nid

    def _add(self, node, reads, writes, after=()):
        if self.keymap is not None:
            reads = self.keymap(reads, True)
            writes = self.keymap(writes, False)
        reads = self._canon(reads)
        writes = self._canon(writes)
        nid = len(self.nodes)
        preds = set()
        for k in after:
            if k in self.lastw:
                preds.add(self.lastw[k])
            for r_ in self.rd.get(k, ()):
                preds.add(r_)
        for k in reads:
            if k in self.lastw:
                preds.add(self.lastw[k])
        for k in writes:
            if k in self.lastw:
                preds.add(self.lastw[k])
            for r_ in self.rd.get(k, ()):
                preds.add(r_)
        for k in reads:
            if (len(k) == 2 and k[0] == "b" and k[1] in "01234567") or k.startswith("ps"):
                lr = self.last_rd_eng.get(k)
                if lr is not None and lr[1] != node["eng"]:
                    preds.add(lr[0])
                self.last_rd_eng[k] = (nid, node["eng"])
        for k in writes:
            if k in self.last_rd_eng:
                self.last_rd_eng[k] = None
        preds.discard(nid)
        node["preds"] = preds
        node["id"] = nid
        node["seg"] = self.seg
        self.nodes.append(node)
        for k in reads:
            self.rd.setdefault(k, set()).add(nid)
        for k in writes:
            self.lastw[k] = nid
            self.rd[k] = set()
        return nid

    def op(self, eng, fn, reads=(), writes=(), sig=True, r=None, w=None):
        reads = list(r if r is not None else reads)
        writes = list(w if w is not None else writes)
        pend = self.pending[eng]
        if pend is None:
            pend = {"eng": eng, "fns": [], "reads": [], "writes": [], "kind": "op"}
        r0 = _Rec()
        fn(r0)
        calls0 = r0.calls
        fn = (lambda e, calls0=calls0: [getattr(e, n_)(*a_, **k_) for (n_, a_, k_) in calls0][-1])
        pend["fns"].append(fn)
        pend["reads"] += reads
        pend["writes"] += writes
        if not sig:
            self.pending[eng] = pend
            return None
        self.pending[eng] = None
        rec = _Rec()
        for f in pend["fns"]:
            f(rec)
        pend["calls"] = rec.calls
        busy = 0.0
        grp = None
        for (name, a, k) in rec.calls:
            b_, _, g_ = _est(eng, name, a, k)
            busy += b_
            grp = g_ or grp
        if JIT_SEED:
            self._rs = (getattr(self, "_rs", JIT_SEED * 7919 + 13) * 1103515245 + 12345) % 2147483648
            busy *= 1.0 + JIT_AMP * ((self._rs / 2147483648.0) - 0.5)
        pend["busy"] = busy
        pend["lat"] = 0.0
        pend["grp"] = grp
        return self._add(pend, pend["reads"], pend["writes"])

    def dma(self, qeng, fn, reads=(), writes=(), out=False, r=None, w=None, prio=0, after=()):
        reads = list(r if r is not None else reads)
        writes = list(w if w is not None else writes)
        rec = _Rec()
        fn(rec)
        name, a, k = rec.calls[0]
        fn = (lambda e, name=name, a=a, k=k: getattr(e, name)(*a, **k))
        issue, lat, _ = _est(qeng, name, a, k)
        node = {"eng": qeng, "fns": [fn], "kind": "dma", "busy": issue, "lat": lat, "grp": None, "prio": prio}
        nid = self._add(node, reads, writes, after=after)
        if out:
            self.out_nodes.append(nid)
        return nid

    def schedule(self):
        for e in self.ENG:
            assert self.pending[e] is None, "dangling unsignalled group on %s" % e
        nodes = self.nodes
        cp = [0.0] * len(nodes)
        if CP_PRIO:
            for nd in reversed(nodes):
                i_ = nd["id"]
                tot = cp[i_] + nd["busy"] + nd["lat"]
                for p in nd["preds"]:
                    if cp[p] < tot:
                        cp[p] = tot
        order = {e: [] for e in self.ENG}
        finish = {}
        eng_free = {e: 0.0 for e in self.ENG}
        act_grp = [None]
        nseg = self.seg + 1
        t_base = 0.0
        for sg in range(nseg):
            queues = {e: [n["id"] for n in nodes if n["seg"] == sg and n["eng"] == e] for e in self.ENG}
            heads = {e: 0 for e in self.ENG}
            done = set()
            remaining = sum(len(q) for q in queues.values())
            for e in self.ENG:
                eng_free[e] = max(eng_free[e], t_base)
            while remaining:
                best = None
                for e in self.ENG:
                    q = queues[e]
                    i = heads[e]
                    cnt = 0
                    while i < len(q) and cnt < self.WINDOW:
                        nid = q[i]
                        i += 1
                        if nid in done:
                            continue
                        cnt += 1
                        nd = nodes[nid]
                        ok = True
                        st = eng_free[e]
                        for p in nd["preds"]:
                            if p not in finish:
                                ok = False
                                break
                            lat = SAME_LAT if nodes[p]["eng"] == e and nodes[p]["kind"] == "op" else XLAT
                            if finish[p] + lat > st:
                                st = finish[p] + lat
                        if not ok:
                            continue
                        st_real = st
                        if e == "act" and nd["grp"] is not None and nd["grp"] != act_grp[0]:
                            st = st + ACT_PEN
                        key = (st, nd.get("prio", 0), -cp[nid] if CP_PRIO else 0.0, nid, st_real)
                        if best is None or key < best[0]:
                            best = (key, e, nid)
                        if (not CP_PRIO) and st_real <= eng_free[e] + 1e-9 and st == st_real:
                            break
                assert best is not None, "scheduler deadlock"
                (_, _, _, _, st), e, nid = best
                nd = nodes[nid]
                busy = nd["busy"]
                if e == "act" and nd["grp"] is not None and nd["grp"] != act_grp[0]:
                    busy += 1.3
                    act_grp[0] = nd["grp"]
                eng_free[e] = st + busy
                finish[nid] = st + busy + nd["lat"]
                order[e].append(nid)
                done.add(nid)
                remaining -= 1
                q = queues[e]
                while heads[e] < len(q) and q[heads[e]] in done:
                    heads[e] += 1
            t_base = max([t_base] + [finish[n["id"]] for n in nodes if n["seg"] == sg])
        self.order = order
        self.est_total = t_base
        return order

    def sem_names(self):
        names = ["pe", "act", "dve", "pool"]
        for qe in ("sp", "pool"):
            for i in range(NDS):
                names.append("d%s%d" % (qe, i))
        return names

    def emit(self, block, sems):
        order = self.schedule()
        nodes = self.nodes
        cnt = {e: 0 for e in self.ENG}
        ndma = {e: 0 for e in self.ENG}
        dma_cnt = {}
        tok = {}
        prev_same_sem = {}
        for e in self.ENG:
            for nid in order[e]:
                nd = nodes[nid]
                if nd["kind"] == "op":
                    cnt[e] += 1
                    tok[nid] = (e, cnt[e])
                else:
                    sname = "d%s%d" % (e, ndma[e] % NDS)
                    ndma[e] += 1
                    prev = dma_cnt.get(sname, 0)
                    prev_same_sem[nid] = (sname, prev)
                    dma_cnt[sname] = prev + 16
                    tok[nid] = (sname, prev + 16)
        seg_floor = {}
        for sg in range(1, self.seg + 1):
            fl = {}
            for nd in nodes:
                if nd["seg"] < sg:
                    s, v = tok[nd["id"]]
                    if fl.get(s, 0) < v:
                        fl[s] = v
            seg_floor[sg] = fl
        progs = {}
        for e in self.ENG:
            known = {}
            lst = []
            for nid in order[e]:
                nd = nodes[nid]
                need = dict(seg_floor.get(nd["seg"], {}))
                for p in nd["preds"]:
                    s, v = tok[p]
                    if need.get(s, 0) < v:
                        need[s] = v
                if nd["kind"] == "dma":
                    s, v = prev_same_sem[nid]
                    if v > 0 and need.get(s, 0) < v:
                        need[s] = v
                waits = []
                for s, v in need.items():
                    if s == "pe" and e == "pe":
                        continue
                    if known.get(s, 0) < v:
                        waits.append((s, v))
                        known[s] = v
                lst.append((waits, nd["fns"], tok[nid][0], 16 if nd["kind"] == "dma" else 1))
            progs[e] = lst
        final = {}
        for s, v in dma_cnt.items():
            final[s] = v

        def run(engobj, eng, fin=False):
            for waits, fns, sname, inc in progs[eng]:
                for s, v in waits:
                    engobj.wait_ge(sems[s], v)
                ins = None
                for f in fns:
                    ins = f(engobj)
                ins.then_inc(sems[sname], inc)
            if fin:
                for s, v in final.items():
                    engobj.wait_ge(sems[s], v)

        @block.tensor
        def _(e):
            run(e, "pe")

        @block.scalar
        def _(e):
            run(e, "act")

        @block.vector
        def _(e):
            run(e, "dve")

        @block.gpsimd
        def _(e):
            run(e, "pool")

        @block.sync
        def _(e):
            run(e, "sp", fin=True)


def build_program(dbg=0):
    nc = bass.Bass("TRN2", target_bir_lowering=False)
    P = Prog()

    def din(name, shape):
        return nc.dram_tensor(name, shape, F32, kind="ExternalInput").ap()

    def dout(name, shape):
        return nc.dram_tensor(name, shape, F32, kind="ExternalOutput").ap()

    x_p = din("x_p", [2048, D])
    x_s = din("x_s", [128, D])
    sconv = din("sconv", [48, D])
    sC = din("sC", [16, 4, 128, 128])
    sn = din("sn", [64, 128])
    sm = din("sm", [16, 4])
    wd_ = {}
    for nm, shp in [("f1_wg", [D, DFF]), ("f1_wu", [D, DFF]), ("f1_wd", [DFF, D]), ("ln1_g", [1, D]), ("ln1_b", [1, D]),
                    ("w_in", [D, 3080]), ("b_in", [1, 3080]), ("gm_ln_g", [1, 512]), ("gm_ln_b", [1, 512]),
                    ("gm_ws", [4, 128, 128]), ("gm_bs", [4, 128]), ("conv_w", [4, D]), ("conv_b", [1, D]),
                    ("gm_out_g", [1, 512]), ("ml_out_g", [1, 512]), ("w_out", [D, D]), ("ln2_g", [1, D]), ("ln2_b", [1, D]),
                    ("f2_wg", [D, DFF]), ("f2_wu", [D, DFF]), ("f2_wd", [DFF, D]), ("ln3_g", [1, D]), ("ln3_b", [1, D])]:
        wd_[nm] = din(nm, shp)

    y_p = dout("y_p", [2048, D])
    y_s = dout("y_s", [128, D])
    gmv_p = dout("gmv_p", [128, 512])
    gmv_s = dout("gmv_s", [128, 512])
    conv_p = dout("conv_p", [3, D])
    conv_s = dout("conv_s", [48, D])
    C_p = dout("C_p", [4, 128, 128])
    C_s = dout("C_s", [16, 4, 128, 128])
    n_p = dout("n_p", [4, 128])
    n_s = dout("n_s", [64, 128])
    m_p = dout("m_p", [4, 1])
    m_s = dout("m_s", [16, 4])

    import os as _os
    ARENA_B = 73472
    XT_B = 8 * NTOK * 2
    NSLOT = 16

    import contextlib
    with contextlib.ExitStack() as es:
        def sb(name, shape, dt=F32):
            return es.enter_context(nc.sbuf_tensor(name, shape, dt))

        acc = sb("acc", [128, NT, D])
        xtraw = sb("xtraw", [128, XT_B // 4])
        arena = sb("arena", [128, ARENA_B // 4])
        lnp = sb("lnp", [128, 2, D])
        ident_bf = sb("ident_bf", [128, 128], BF16)
        ident_f = sb("ident_f", [128, 128])
        negmask = sb("negmask", [128, 128])
        negmask_s = sb("negmask_s", [128, 128])
        wsT = sb("wsT", [128, 4, 128], BF16)
        wsT_s = sb("wsT_s", [128, 4, 128], BF16)
        colsC = sb("colsC", [128, 64])
        gmln = sb("gmln", [128, 2, 512])
        b4 = sb("b4", [4, 512], BF16)
        selb = sb("selb", [4, 4, 128], BF16)
        sel = sb("sel", [4, 4, 128])
        onehot = sb("onehot", [128, 16])
        bgate = sb("bgate", [4, 2])
        small = sb("small", [128, 320])
        gt = sb("gt", [4, NSLOT, 144])
        Cst = sb("Cst", [128, 4, 130])
        Cbf = sb("Cbf", [128, 4, 130], BF16)
        psum = [es.enter_context(nc.psum_tensor("ps%d" % i, [128, 512], F32)) for i in range(8)]

        def carve(raw, off, nbytes, dt):
            assert off % 4 == 0 and nbytes % 4 == 0, (off, nbytes)
            return raw[:, off // 4:(off + nbytes) // 4].bitcast(dt)

        XTK = ["xT%dh%d" % (t, h) for t in range(NT) for h in range(2)]

        def c_(fn, eng="pool", r=(), w=()):
            P.op(eng, fn, r, w)

        c_(lambda e: e.memset(ident_f[:], 1.0), w=["ident_f"])
        c_(lambda e: e.affine_select(out=ident_f[:], in_=ident_f[:], pattern=[[-1, 128]], compare_op=ALU.is_equal,
                                     fill=0.0, base=0, channel_multiplier=1), r=["ident_f"], w=["ident_f"])
        c_(lambda e: e.tensor_copy(out=ident_bf[:], in_=ident_f[:]), r=["ident_f"], w=["ident_bf"])
        c_(lambda e: e.memset(negmask[:], 0.0), w=["negmask"])
        c_(lambda e: e.affine_select(out=negmask[:], in_=negmask[:], pattern=[[1, 128]], compare_op=ALU.is_ge,
                                     fill=NEG, base=0, channel_multiplier=-1), r=["negmask"], w=["negmask"])
        c_(lambda e: e.affine_select(out=negmask_s[:].rearrange("p (j r) -> p j r", j=16),
                                     in_=negmask[:].rearrange("p (j r) -> p j r", j=16),
                                     pattern=[[-8, 16], [0, 8]], compare_op=ALU.is_ge,
                                     fill=NEG, base=0, channel_multiplier=1), r=["negmask"], w=["negmask_s"])
        c_(lambda e: e.memset(sel[:], 1.0), w=["sel"])
        c_(lambda e: e.affine_select(out=sel[:], in_=sel[:], pattern=[[-1, 4], [0, 128]], compare_op=ALU.is_equal,
                                     fill=0.0, base=0, channel_multiplier=1), r=["sel"], w=["sel"])
        c_(lambda e: e.tensor_copy(out=selb[:], in_=sel[:]), r=["sel"], w=["selb"])
        c_(lambda e: e.memset(onehot[:], 1.0), w=["onehot"])
        c_(lambda e: e.affine_select(out=onehot[:], in_=onehot[:], pattern=[[-8, 16]], compare_op=ALU.is_ge,
                                     fill=0.0, base=0, channel_multiplier=1), r=["onehot"], w=["onehot"])
        c_(lambda e: e.affine_select(out=onehot[:], in_=onehot[:], pattern=[[8, 16]], compare_op=ALU.is_ge,
                                     fill=0.0, base=7, channel_multiplier=-1), r=["onehot"], w=["onehot"])
        c_(lambda e: e.memset(Cst[:], 0.0), w=["Cst"])
        c_(lambda e: e.memset(Cbf[:], 0.0), w=["Cbf"])
        c_(lambda e: e.memset(gt[:], 0.0), w=["gt"])
        c_(lambda e: e.memset(small[:], 0.0), w=["small"])

        wtmp = carve(arena, 65664, 512 * 8, F32).rearrange("p (a b) -> p a b", a=8)
        rowsC = carve(arena, 65664 + 4096, 512, F32)
        win = wd_["w_in"]
        b_in = wd_["b_in"]
        SK = ["setup_rows"]
        P.dma("sp", lambda e: e.dma_start(out=rowsC[0:8, :], in_=b_in[0, 1024:2048].rearrange("(c p) -> c p", p=128)), w=["rows0"])
        P.dma("sp", lambda e: e.dma_start(out=rowsC[8:16, :], in_=wd_["conv_b"][0, :].rearrange("(c p) -> c p", p=128)), w=["rows1"])
        P.dma("sp", lambda e: e.dma_start(out=rowsC[16:48, :], in_=wd_["conv_w"].rearrange("j (c p) -> (j c) p", p=128)), w=["rows2"])
        P.dma("sp", lambda e: e.dma_start(out=rowsC[48:52, :], in_=wd_["gm_bs"][:, :]), w=["rows3"])
        bs_src = bass.AP(wd_["gm_bs"].tensor, 0, [[128, 4], [0, 16], [1, 8]])
        P.dma("sp", lambda e: e.dma_start(out=rowsC[52:56, :].rearrange("p (j r) -> p j r", j=16), in_=bs_src), w=["rows4"])
        P.dma("sp", lambda e: e.dma_start(out=rowsC[56:60, :], in_=wd_["gm_out_g"][0, :].rearrange("(c p) -> c p", p=128)), w=["rows5"])
        P.dma("sp", lambda e: e.dma_start(out=rowsC[60:64, :], in_=wd_["ml_out_g"][0, :].rearrange("(c p) -> c p", p=128)), w=["rows6"])
        P.op("pe", lambda e: e.transpose(out=psum[6][:, 0:64], in_=rowsC[0:64, :], identity=ident_f[0:64, 0:64]),
             ["rows%d" % i_ for i_ in range(7)] + ["ident_f"], ["b6"])
        P.op("dve", lambda e: e.tensor_copy(out=colsC[:], in_=psum[6][:, 0:64]), ["b6"], ["colsC"])
        P.dma("sp", lambda e: e.dma_start(out=bgate[:, 0:1], in_=b_in[0, 3072:3076].rearrange("(p o) -> p o", o=1)), w=["bgate"])
        P.dma("sp", lambda e: e.dma_start(out=bgate[:, 1:2], in_=b_in[0, 3076:3080].rearrange("(p o) -> p o", o=1)), w=["bgate"])
        P.dma("sp", lambda e: e.dma_start(out=gmln[:, 0, :], in_=wd_["gm_ln_g"].partition_broadcast(128)), w=["gmln"])
        P.dma("sp", lambda e: e.dma_start(out=gmln[:, 1, :], in_=wd_["gm_ln_b"].partition_broadcast(128)), w=["gmln"])
        for i, c0 in enumerate([0, 512, 2048, 2560]):
            P.dma("pool", lambda e, i=i, c0=c0: e.dma_start(out=b4[i:i + 1, :], in_=b_in[0:1, c0:c0 + 512]), w=["bias4"])

        for h in range(4):
            P.dma("sp", lambda e, h=h: e.dma_start(out=wtmp[:, h, :], in_=wd_["gm_ws"][h, :, :]), w=["wtmp%d" % h])
            P.op("pool", lambda e, h=h: e.affine_select(out=wtmp[:, h, :], in_=wtmp[:, h, :], pattern=[[-1, 128]],
                                                        compare_op=ALU.is_ge, fill=0.0, base=0, channel_multiplier=1), ["wtmp%d" % h], ["wtmp%d" % h])
            P.op("pe", lambda e, h=h: e.transpose(out=psum[7][:, h * 128:(h + 1) * 128], in_=wtmp[:, h, :], identity=ident_f[:]),
                 ["wtmp%d" % h, "ident_f"], ["b7"])
        P.op("dve", lambda e: e.tensor_copy(out=wsT[:].rearrange("p a b -> p (a b)"), in_=psum[7][:, :]), ["b7"], ["wsT"])
        w8 = carve(arena, 65664 + 4608, 128, F32).rearrange("p (h c) -> p h c", h=4)
        for h in range(4):
            w8_src = bass.AP(wd_["gm_ws"].tensor, h * 128 * 128, [[0, 16], [128, 8], [1, 8]])
            for j in range(16):
                pass
            P.dma("sp", lambda e, h=h, w8_src=w8_src: e.dma_start(out=w8[:, h, :], in_=w8_src), w=["w8_%d" % h])
        for h in range(4):
            hh = 4 + h
            for j in range(16):
                P.op("dve", lambda e, h=h, hh=hh, j=j: e.tensor_scalar(out=wtmp[:, hh, 8 * j:8 * j + 8], in0=w8[:, h, :],
                                                                       scalar1=onehot[:, j:j + 1], scalar2=None, op0=ALU.mult),
                     ["w8_%d" % h, "onehot"], ["wtmp%d_%d" % (hh, j)])
            WJ = ["wtmp%d_%d" % (hh, j) for j in range(16)]
            P.op("pool", lambda e, hh=hh: e.affine_select(out=wtmp[:, hh, :], in_=wtmp[:, hh, :], pattern=[[-1, 128]],
                                                          compare_op=ALU.is_ge, fill=0.0, base=0, channel_multiplier=1), WJ, WJ)
            P.op("pe", lambda e, h=h, hh=hh: e.transpose(out=psum[6][:, h * 128:(h + 1) * 128], in_=wtmp[:, hh, :], identity=ident_f[:]),
                 WJ + ["ident_f"], ["b6"])
        P.op("dve", lambda e: e.tensor_copy(out=wsT_s[:].rearrange("p a b -> p (a b)"), in_=psum[6][:, :]), ["b6"], ["wsT_s"])

        st6 = small[:, 0:12].rearrange("p (a b) -> p a b", a=2)
        mv = small[:, 12:14]
        sd = small[:, 14:15]
        rstd = small[:, 15:16]
        nmr = small[:, 16:17]

        LN_GAMMA_ENG = "dve"

        ln_junk = carve(arena, 61440, 2048, BF16)

        def layer_norm_tile(t, act_stats=False, affine=True, norm_on_dve=False):
            a = acc[:, t, :]
            k = "acc%d" % t
            if act_stats:
                s1 = small[:, 0:1]
                s2 = small[:, 1:2]
                msq = small[:, 2:3]
                P.op("act", lambda e: e.activation(out=ln_junk, in_=a, func=AF.Identity, accum_out=s1), [k], ["lnjunk", "ln_st0"])
                P.op("act", lambda e: e.activation(out=ln_junk, in_=a, func=AF.Square, accum_out=s2), [k], ["lnjunk", "ln_st1"])
                P.op("dve", lambda e: e.tensor_scalar(out=mv[:, 0:1], in0=s1, scalar1=1.0 / D, scalar2=None, op0=ALU.mult), ["ln_st0", "ln_mv"], ["ln_mv"])
                P.op("dve", lambda e: e.tensor_tensor(out=msq, in0=mv[:, 0:1], in1=mv[:, 0:1], op=ALU.mult), ["ln_mv"], ["ln_msq"])
                P.op("dve", lambda e: e.scalar_tensor_tensor(out=mv[:, 1:2], in0=s2, scalar=1.0 / D, in1=msq, op0=ALU.mult, op1=ALU.subtract),
                     ["ln_st1", "ln_msq", "ln_mv"], ["ln_mv"])
            else:
                P.op("dve", lambda e: e.bn_stats(out=st6[:, 0, :], in_=a[:, 0:512]), [k], ["ln_st0"])
                P.op("dve", lambda e: e.bn_stats(out=st6[:, 1, :], in_=a[:, 512:1024]), [k], ["ln_st1"])
                P.op("dve", lambda e: e.bn_aggr(out=mv, in_=small[:, 0:12]), ["ln_st0", "ln_st1"], ["ln_mv"])
            P.op("act", lambda e: e.activation(out=sd, in_=mv[:, 1:2], func=AF.Ln, bias=LN_EPS / (ALPHA * ALPHA), scale=1.0), ["ln_mv"], ["ln_sd"])
            P.op("act", lambda e: e.activation(out=rstd, in_=sd, func=AF.Exp, scale=-0.5), ["ln_sd"], ["ln_rstd"])
            P.op("dve", lambda e: e.tensor_scalar(out=nmr, in0=mv[:, 0:1], scalar1=rstd, scalar2=-1.0, op0=ALU.mult, op1=ALU.mult),
                 ["ln_mv", "ln_rstd"], ["ln_nmr"])
            if norm_on_dve:
                P.op("dve", lambda e: e.tensor_scalar(out=a, in0=a, scalar1=rstd, scalar2=nmr, op0=ALU.mult, op1=ALU.add), [k, "ln_rstd", "ln_nmr"], [k])
            else:
                P.op("act", lambda e: e.activation(out=a, in_=a, func=AF.Identity, bias=nmr, scale=rstd), [k, "ln_rstd", "ln_nmr"], [k])
            if affine:
                P.op(LN_GAMMA_ENG, lambda e: e.tensor_tensor(out=a, in0=a, in1=lnp[:, 0, :], op=ALU.mult), [k, "lnp"], [k])
                P.op("dve", lambda e: e.tensor_tensor(out=a, in0=a, in1=lnp[:, 1, :], op=ALU.add), [k, "lnp"], [k])

        def load_lnp(g, b):
            P.dma("sp", lambda e: e.dma_start(out=lnp[:, 0, :], in_=wd_[g].partition_broadcast(128)), w=["lnp"])
            P.dma("sp", lambda e: e.dma_start(out=lnp[:, 1, :], in_=wd_[b].partition_broadcast(128)), w=["lnp"])

        xT = xtraw[:, :].bitcast(BF16).rearrange("p (kc t) -> p kc t", kc=8)
        WB = 6144 * NPC
        f_wg = [carve(arena, b * WB, 2048 * NPC, BF16).rearrange("p (kc f) -> p kc f", kc=8) for b in range(2)]
        f_wu = [carve(arena, b * WB + 2048 * NPC, 2048 * NPC, BF16).rearrange("p (kc f) -> p kc f", kc=8) for b in range(2)]
        f_wd = [carve(arena, b * WB + 4096 * NPC, 2048 * NPC, BF16).rearrange("p (c d) -> p c d", c=NPC) for b in range(2)]
        o0 = 2 * WB
        f_hT = [carve(arena, o0 + b * 1024 * NPC, 1024 * NPC, BF16).rearrange("p (c t) -> p c t", c=NPC) for b in range(2)]
        o0 += 2 * 1024 * NPC
        f_sg = [carve(arena, o0 + b * 2048, 2048, F32) for b in range(2)]
        o0 += 4096
        assert o0 <= ARENA_B

        def transpose_tile_to(t, dst3, dkey, banks=(6, 7), bkeys=(("b6",), ("b7",)), eng_a="dve", eng_b="act", extra_r=()):
            k = "acc%d" % t
            for half in range(2):
                bank = banks[half]
                bk = list(bkeys[half])
                for q_ in range(4):
                    kc = half * 4 + q_
                    P.op("pe", lambda e, bank=bank, q_=q_, kc=kc: e.transpose(out=psum[bank][:, q_ * 128:(q_ + 1) * 128],
                                                                             in_=acc[:, t, kc * 128:(kc + 1) * 128], identity=ident_f[:]),
                         [k, "ident_f"], bk, sig=(q_ == 3))
                eng = eng_a if half == 0 else eng_b
                if eng == "act":
                    P.op("act", lambda e, bank=bank, half=half: e.activation(
                        out=dst3[:, half * 4:half * 4 + 4, :], in_=psum[bank][:, :].rearrange("p (a b) -> p a b", a=4), func=AF.Copy),
                        bk + list(extra_r), [dkey + ("h%d" % half)])
                else:
                    P.op(eng, lambda e, bank=bank, half=half: e.tensor_copy(
                        out=dst3[:, half * 4:half * 4 + 4, :], in_=psum[bank][:, :].rearrange("p (a b) -> p a b", a=4)),
                        bk + list(extra_r), [dkey + ("h%d" % half)])

        def ffn_stage(sidx, first, pre_affine=False, lnp_after_p0=None):
            pre = "f%d_" % sidx
            wg_v = wd_[pre + "wg"].rearrange("(kc kp) f -> kp kc f", kp=128)
            wu_v = wd_[pre + "wu"].rearrange("(kc kp) f -> kp kc f", kp=128)
            wdn_v = wd_[pre + "wd"].rearrange("(c fp) d -> fp c d", fp=128)
            cnt = 0
            hb_i = 0
            WALIAS = (["w_in%d" % c for c in range(8)] + ["w_out"]) if sidx == 2 else []
            for p, (c0, npc) in enumerate(PASSES):
                b = p % 2
                wprio = -1 if (sidx == 1 and p == 0) else 0
                for ci in range(npc):
                    P.dma("pool", lambda e, b=b, c0=c0, ci=ci: e.dma_start(out=f_wg[b][:, :, ci * 128:(ci + 1) * 128],
                                                                           in_=wg_v[:, :, (c0 + ci) * 128:(c0 + ci + 1) * 128]),
                          w=["wg%d_%d" % (b, ci)], prio=wprio, after=WALIAS)
                    P.dma("pool", lambda e, b=b, c0=c0, ci=ci: e.dma_start(out=f_wu[b][:, :, ci * 128:(ci + 1) * 128],
                                                                           in_=wu_v[:, :, (c0 + ci) * 128:(c0 + ci + 1) * 128]),
                          w=["wu%d_%d" % (b, ci)], prio=wprio, after=WALIAS)
                for ci in range(npc):
                    P.dma("pool", lambda e, b=b, c0=c0, ci=ci: e.dma_start(out=f_wd[b][:, ci, :], in_=wdn_v[:, c0 + ci, :]),
                          w=["wd%d_%d" % (b, ci)], prio=wprio, after=WALIAS)
                last = (p == len(PASSES) - 1)
                if p == 1 and lnp_after_p0 is not None:
                    load_lnp(*lnp_after_p0)
                for g, tiles in enumerate(GROUPS):
                    n = 128 * len(tiles)
                    t0 = tiles[0] * 128
                    xkeys = []
                    for t in tiles:
                        xkeys += ["xT%dh0" % t, "xT%dh1" % t]
                    if p == 0:
                        for t in tiles:
                            k = "acc%d" % t
                            a = acc[:, t, :]
                            if first:
                                src = x_p[t * 128:(t + 1) * 128, :] if t < 16 else x_s[:, :]
                                P.dma("sp", lambda e, a=a, src=src: e.dma_start(out=a, in_=src), w=[k], prio=-1)
                            if pre_affine:
                                P.op("dve", lambda e, a=a: e.tensor_tensor(out=a, in0=a, in1=lnp[:, 0, :], op=ALU.mult), [k, "lnp"], [k])
                                P.op("dve", lambda e, a=a: e.tensor_tensor(out=a, in0=a, in1=lnp[:, 1, :], op=ALU.add), [k, "lnp"], [k])
                            transpose_tile_to(t, xT[:, :, t * 128:(t + 1) * 128], "xT%d" % t, extra_r=(["BAR2"] if sidx == 2 else []))
                    hb = hb_i % 2
                    hb_i += 1
                    for ci in range(npc):
                        s = cnt % 2
                        cnt += 1
                        for kc in range(8):
                            P.op("pe", lambda e, s=s, b=b, kc=kc, ci=ci, t0=t0, n=n: e.matmul(
                                psum[s][:, 0:n], f_wg[b][:, kc, ci * 128:(ci + 1) * 128], xT[:, kc, t0:t0 + n],
                                start=(kc == 0), stop=(kc == 7)), ["wg%d_%d" % (b, ci)] + xkeys, ["psg%d" % s], sig=(kc == 7))
                        for kc in range(8):
                            P.op("pe", lambda e, s=s, b=b, kc=kc, ci=ci, t0=t0, n=n: e.matmul(
                                psum[2 + s][:, 0:n], f_wu[b][:, kc, ci * 128:(ci + 1) * 128], xT[:, kc, t0:t0 + n],
                                start=(kc == 0), stop=(kc == 7)), ["wu%d_%d" % (b, ci)] + xkeys, ["psu%d" % s], sig=(kc == 7))
                        P.op("act", lambda e, s=s, n=n: e.activation(out=f_sg[s][:, 0:n], in_=psum[s][:, 0:n], func=AF.Silu),
                             ["psg%d" % s], ["sg%d" % s])
                        P.op("dve", lambda e, s=s, n=n, hb=hb, ci=ci: e.tensor_tensor(out=f_hT[hb][:, ci, 0:n], in0=f_sg[s][:, 0:n],
                                                                                    in1=psum[2 + s][:, 0:n], op=ALU.mult),
                             ["sg%d" % s, "psu%d" % s], ["hT%d_%d" % (hb, ci)])
                    for ti, t in enumerate(tiles):
                        k = "acc%d" % t
                        for half in range(2):
                            for ci in range(npc):
                                P.op("pe", lambda e, half=half, hb=hb, ci=ci, ti=ti, b=b, npc=npc: e.matmul(
                                    psum[4 + half][:, :], f_hT[hb][:, ci, ti * 128:(ti + 1) * 128],
                                    f_wd[b][:, ci, half * 512:(half + 1) * 512], start=(ci == 0), stop=(ci == npc - 1)),
                                    ["hT%d_%d" % (hb, ci), "wd%d_%d" % (b, ci)], ["psy%d" % half], sig=(ci == npc - 1))
                            P.op("dve", lambda e, half=half, t=t: e.scalar_tensor_tensor(
                                out=acc[:, t, half * 512:(half + 1) * 512], in0=psum[4 + half][:, :], scalar=0.5 / ALPHA,
                                in1=acc[:, t, half * 512:(half + 1) * 512], op0=ALU.mult, op1=ALU.add),
                                ["psy%d" % half, k], [k])
                        if last:
                            layer_norm_tile(t, act_stats=True)

        w_in_sb = carve(arena, 0, 49280, BF16).rearrange("p (kc f) -> p kc f", kc=8)
        w_out_sb = carve(arena, 49280, 16384, BF16).rearrange("p (kc f) -> p kc f", kc=8)
        _regions = [[xtraw, 0, XT_B], [arena, 65664, ARENA_B]]

        def walloc(nbytes, dt):
            nb = (nbytes + 31) // 32 * 32
            for rg in _regions:
                if rg[1] + nb <= rg[2]:
                    v = carve(rg[0], rg[1], nb, dt)
                    rg[1] += nb
                    return v
            raise AssertionError("mixer working set does not fit")

        x1T = walloc(2048, BF16).rearrange("p (a b) -> p a b", a=8)
        mixT = walloc(2048, BF16).rearrange("p (a b) -> p a b", a=8)
        u_act = walloc(2048, F32)
        vn = walloc(2048, F32)
        vn_bf = walloc(1024, BF16)
        v_ext2 = [walloc(1040, BF16)[:, 0:520].rearrange("p (a b) -> p a b", a=4) for _ in range(2)]
        o_sig2 = [walloc(2048, F32) for _ in range(2)]
        extb = walloc(5632, F32)
        ext = extb[:, 0:8 * 131].rearrange("p (a b) -> p a b", a=8)
        ext_s = extb[:, 0:8 * 176].rearrange("p (a j r) -> p a j r", a=8, j=16)
        extbb = walloc(2816, BF16)
        ext_bf = extbb[:, 0:8 * 131].rearrange("p (a b) -> p a b", a=8)
        ext_s_bf = extbb[:, 0:8 * 176].rearrange("p (a j r) -> p a j r", a=8, j=16)
        Dg = lnp[:, :, :].rearrange("p a b -> p (a b)").bitcast(BF16).rearrange("p (i c) -> p i c", i=32)
        qkT_raw = [walloc(2048, BF16) for _ in range(2)]
        qkT_bf2 = [q_.rearrange("p (a b) -> p a b", a=8) for q_ in qkT_raw]
        kw_bf = walloc(1024, BF16).rearrange("p (a b) -> p a b", a=4)
        mixf = walloc(2048, F32)
        mix_bf = walloc(2048, BF16)
        Dm = [walloc(512, F32) for _ in range(2)]
        Pb = [walloc(256, BF16) for _ in range(2)]
        itw = walloc(528, F32)
        junk = walloc(256, BF16)
        C0s = [qkT_raw[1].bitcast(F32).rearrange("p (a b) -> p a b", a=4), o_sig2[1].rearrange("p (a b) -> p a b", a=4)]
        vmask = [walloc(1040, BF16)[:, 0:520].rearrange("p (a b) -> p a b", a=4) for _ in range(2)]
        qTf = walloc(2048, F32).rearrange("p (a b) -> p a b", a=4)
        convc = walloc(1536, F32).rearrange("p (a b) -> p a b", a=8)
        n0row = walloc(512, F32)
        n0T = walloc(256, F32)
        nnewT = walloc(256, F32)

        st4 = small[:, 20:44].rearrange("p (a b) -> p a b", a=4)
        mv4 = small[:, 44:52].rearrange("p (a b) -> p a b", a=4)
        sd4 = small[:, 52:56]
        rstd4 = small[:, 56:60]
        nmr4 = small[:, 60:64]
        ssq = small[:, 64:68]
        rr = small[:, 68:72]
        cA = small[:, 72:76]
        cG = small[:, 76:80]
        cM = small[:, 80:84]
        cAp = small[:, 84:88]
        Gprev_bc = small[:, 88:92]
        Gend_bc = small[:, 92:96]
        d1 = small[:, 96:100]
        winter = small[:, 100:104]
        floor_ = small[:, 104:108]
        d2 = small[:, 108:112]
        wk = small[:, 112:116]
        d3 = small[:, 116:120]
        dec = small[:, 120:124]
        dd = small[:, 124:125]
        rd = small[:, 125:126]
        ncol = small[:, 128:132]
        cGp_s = small[:, 132:136]
        cGe_s = small[:, 136:140]
        dq = small[:, 152:156]
        decbc = small[:, 160:224].rearrange("p (h j) -> p h j", h=4)

        def gs(i, n=128):
            return gt[:, i, 0:n]
        S_IG, S_FG, S_T, S_B, S_M, S_CAR, S_ONE, S_DG, S_RM, S_RA, S_GP, S_GE, S_AC, S_GC, S_MC, S_NR = range(16)

        PSMAP = {"A": {0: 0, 3: 1, 4: 0, 5: 2, 7: 2, 1: 1, 2: 0},
                 "B": {3: 3, 4: 4, 5: 5, 6: 6, 1: 7, 2: 4, 7: 7, 0: 3},
                 "C": {0: 3, 1: 5, 2: 6}}
        phase = ["A"]
        par = [0]

        class _PS:
            def __getitem__(self, old):
                return psum[PSMAP[phase[0]][old]]
        PS = _PS()
        DBL = set(["qk%d" % c for c in range(8)] + ["v_ext", "o_sig", "g_ig", "g_fg", "g_t", "g_B", "g_M", "cols", "cAp",
                                                    "d1", "winter", "floor", "d2", "wk", "d3", "dec"])
        SALIAS = {"g_gp": ["g_ig@1"], "g_ge": ["g_fg@1"], "g_ac": ["g_t@1"], "g_gc": ["g_B@1"], "g_mc": ["g_M@1"],
                  "C0s0": ["qk%d@1" % c for c in range(8)], "C0s1": ["o_sig@1"]}

        def mixer_keymap(keys, is_read):
            keys = list(keys)
            nobar = len(keys) > 0 and keys[0] == "__nobar__"
            if nobar:
                keys = keys[1:]
            out = ["BAR1"] if (is_read and not nobar) else []
            for k_ in keys:
                if len(k_) >= 2 and k_[0] == "b" and k_[1] in "01234567":
                    out.append("b%d" % PSMAP[phase[0]][int(k_[1])])
                elif k_ in DBL:
                    out.append("%s@%d" % (k_, par[0]))
                elif k_ in SALIAS:
                    out += SALIAS[k_]
                else:
                    out.append(k_)
            return out

        def head_rms_to_mixbf(off):
            for h in range(4):
                P.op("act", lambda e, h=h: e.activation(out=junk[:, 0:128], in_=mixf[:, h * 128:(h + 1) * 128], func=AF.Square,
                                                        accum_out=ssq[:, h:h + 1]), ["mixf"], ["ssq%d" % h, "junk"])
            P.op("dve", lambda e: e.tensor_scalar(out=rr, in0=ssq, scalar1=1.0 / 128.0, scalar2=LN_EPS, op0=ALU.mult, op1=ALU.add),
                 ["ssq%d" % h for h in range(4)], ["rr"])
            P.op("act", lambda e: e.activation(out=rr, in_=rr, func=AF.Ln), ["rr"], ["rr"])
            P.op("act", lambda e: e.activation(out=rr, in_=rr, func=AF.Exp, scale=-0.5), ["rr"], ["rr"])
            for h in range(4):
                P.op("dve", lambda e, h=h: e.tensor_scalar(out=mix_bf[:, off + h * 128:off + (h + 1) * 128],
                                                           in0=mixf[:, h * 128:(h + 1) * 128], scalar1=rr[:, h:h + 1], scalar2=None,
                                                           op0=ALU.mult), ["mixf", "rr"], ["mix_bf%d" % (off // 512)])

        def zblock(i, col0, bank):
            key = "b%d" % bank
            for kc in range(8):
                P.op("pe", lambda e, kc=kc: e.matmul(PS[bank][:, :], x1T[:, kc, :], w_in_sb[:, kc, col0:col0 + 512],
                                                     start=(kc == 0), stop=False), ["x1Th0", "x1Th1", "w_in%d" % kc], [key], sig=False)
            P.op("pe", lambda e: e.matmul(PS[bank][:, :], selb[:, i, :], b4[:, :], start=False, stop=True),
                 ["selb", "bias4"], [key])

        MIXSTOP = int(_os.environ.get("MIXSTOP", "99"))
        MIXTILES = int(_os.environ.get("MIXTILES", "99"))

        def mixer_tile(t):
            sample = (t == 16)
            k = "acc%d" % t
            a = acc[:, t, :]
            pp = 0 if sample else (t % 2)
            par[0] = pp
            phase[0] = "A"
            go = 0 if sample else 10 * pp
            co = 184 * pp
            SC = lambda lo, hi: small[:, lo + co:hi + co]
            cA, cG, cM, cAp = SC(72, 76), SC(76, 80), SC(80, 84), SC(84, 88)
            d1, winter, floor_, d2, wk, d3, dec = SC(96, 100), SC(100, 104), SC(104, 108), SC(108, 112), SC(112, 116), SC(116, 120), SC(120, 124)
            qkT_bf, v_ext, o_sig = qkT_bf2[pp], v_ext2[pp], o_sig2[pp]
            n_g = 144 if sample else 128
            transpose_tile_to(t, x1T, "x1T", banks=(0, 1), bkeys=(("b0",), ("b1",)))
            XK = ["x1Th0", "x1Th1"]
            if MIXSTOP <= 1:
                return
            for gi, col in enumerate([3072, 3076]):
                for kc in range(8):
                    P.op("pe", lambda e, gi=gi, col=col, kc=kc: e.matmul(PS[5][0:4, gi * 128:(gi + 1) * 128],
                                                                         w_in_sb[:, kc, col:col + 4], x1T[:, kc, :],
                                                                         start=(kc == 0), stop=(kc == 7)),
                         XK + ["w_in%d" % kc], ["b5a" if gi == 0 else "b5b"], sig=(kc == 7))
            if not sample:
                ig = gs(S_IG + go); fg = gs(S_FG + go); tm = gs(S_T + go); Bt = gs(S_B + go); Mt = gs(S_M + go)
                P.op("act", lambda e: e.activation(out=ig, in_=PS[5][0:4, 0:128], func=AF.Identity, bias=bgate[:, 0:1], scale=1.0),
                     ["b5a", "bgate"], ["g_ig"])
                P.op("act", lambda e: e.activation(out=fg, in_=PS[5][0:4, 128:256], func=AF.Identity, bias=bgate[:, 1:2], scale=1.0),
                     ["b5b", "bgate"], ["g_fg"])
            else:
                ig = gs(S_IG, 144); fg = gs(S_FG, 144); tm = gs(S_T, 144); Bt = gs(S_B, 144); Mt = gs(S_M, 144)
                v3 = lambda ap: ap.rearrange("p (j r) -> p j r", j=16)
                P.op("pool", lambda e: e.memset(ig, 0.0), (), ["g_ig"])
                P.op("pool", lambda e: e.memset(fg, 0.0), (), ["g_fg"])
                P.op("act", lambda e: e.activation(out=v3(ig)[:, :, 1:9], in_=PS[5][0:4, 0:128].rearrange("p (j r) -> p j r", j=16),
                                                   func=AF.Identity, bias=bgate[:, 0:1], scale=1.0), ["b5a", "bgate", "g_ig"], ["g_ig"])
                P.op("act", lambda e: e.activation(out=v3(fg)[:, :, 1:9], in_=PS[5][0:4, 128:256].rearrange("p (j r) -> p j r", j=16),
                                                   func=AF.Identity, bias=bgate[:, 1:2], scale=1.0), ["b5b", "bgate", "g_fg"], ["g_fg"])
            P.op("dve", lambda e: e.scalar_tensor_tensor(out=tm, in0=fg, scalar=-1.0, in1=fg, op0=ALU.mult, op1=ALU.max), ["g_fg"], ["g_t"])
            P.op("act", lambda e: e.activation(out=tm, in_=tm, func=AF.Exp, scale=-1.0), ["g_t"], ["g_t"])
            P.op("act", lambda e: e.activation(out=tm, in_=tm, func=AF.Ln, bias=1.0, scale=1.0), ["g_t"], ["g_t"])
            P.op("dve", lambda e: e.scalar_tensor_tensor(out=fg, in0=fg, scalar=0.0, in1=tm, op0=ALU.min, op1=ALU.subtract),
                 ["g_fg", "g_t"], ["g_fg"])
            Gt = tm
            if not sample:
                P.op("dve", lambda e: e.tensor_tensor_scan(out=Bt, data0=gs(S_ONE), data1=fg, initial=gt[:, S_CAR, 0:1],
                                                           op0=ALU.mult, op1=ALU.add), ["g_fg", "g_car", "g_one"], ["g_B"])
                P.op("dve", lambda e: e.tensor_tensor(out=ig, in0=ig, in1=Bt, op=ALU.subtract), ["g_ig", "g_B"], ["g_ig"])
                P.op("dve", lambda e: e.tensor_tensor_scan(out=Gt, data0=gs(S_ONE), data1=ig, initial=gt[:, S_CAR, 1:2],
                                                           op0=ALU.mult, op1=ALU.max), ["g_ig", "g_car", "g_one", "g_t"], ["g_t"])
                P.op("dve", lambda e: e.tensor_tensor(out=Mt, in0=Gt, in1=Bt, op=ALU.add), ["g_t", "g_B"], ["g_M"])
                Ac, Gc, Mc = ig, Gt, Mt
                G_end = Gt[:, 127:128]
            else:
                v3 = lambda ap: ap.rearrange("p (j r) -> p j r", j=16)
                P.op("pool", lambda e: e.memset(v3(fg)[:, :, 0:1], 0.0), ["g_fg"], ["g_fg"])
                P.op("dve", lambda e: e.tensor_tensor_scan(out=Bt, data0=gs(S_RM, 144), data1=fg, initial=0.0,
                                                           op0=ALU.mult, op1=ALU.add), ["g_fg", "g_rm"], ["g_B"])
                P.op("dve", lambda e: e.tensor_tensor(out=ig, in0=ig, in1=Bt, op=ALU.subtract), ["g_ig", "g_B"], ["g_ig"])
                P.op("pool", lambda e: e.tensor_copy(out=v3(ig)[:, :, 0:1], in_=gt[:, S_NR, 128:144].rearrange("p (j o) -> p j o", o=1)),
                     ["g_ig", "g_m0"], ["g_ig"])
                P.op("dve", lambda e: e.tensor_tensor_scan(out=Gt, data0=gs(S_RA, 144), data1=ig, initial=0.0,
                                                           op0=ALU.add, op1=ALU.max), ["g_ig", "g_ra", "g_t"], ["g_t"])
                P.op("dve", lambda e: e.tensor_tensor(out=Mt, in0=Gt, in1=Bt, op=ALU.add), ["g_t", "g_B"], ["g_M"])
                Ac, Gc, Mc = gs(S_AC), gs(S_GC), gs(S_MC)
                c3 = lambda ap: ap.rearrange("p (j r) -> p j r", j=16)
                P.op("pool", lambda e: e.tensor_copy(out=c3(Ac), in_=v3(ig)[:, :, 1:9]), ["g_ig"], ["g_ac"])
                P.op("pool", lambda e: e.tensor_copy(out=c3(Gc), in_=v3(Gt)[:, :, 1:9]), ["g_t"], ["g_gc"])
                P.op("pool", lambda e: e.tensor_copy(out=c3(Mc), in_=v3(Mt)[:, :, 1:9]), ["g_M"], ["g_mc"])
                for r_ in range(8):
                    P.op("pool", lambda e, r_=r_: e.tensor_copy(out=c3(gs(S_GP))[:, :, r_:r_ + 1], in_=v3(Gt)[:, :, 0:1]), ["g_t"], ["g_gp"])
                    P.op("pool", lambda e, r_=r_: e.tensor_copy(out=c3(gs(S_GE))[:, :, r_:r_ + 1], in_=v3(Gt)[:, :, 8:9]), ["g_t"], ["g_ge"])
            AK = "g_ac" if sample else "g_ig"
            GK = "g_gc" if sample else "g_t"
            MK = "g_mc" if sample else "g_M"
            if MIXSTOP <= 2:
                return
            P.op("pe", lambda e: e.transpose(out=PS[7][:, 258:262], in_=Ac, identity=ident_f[0:4, 0:4]), [AK, "ident_f"], ["b7c"], sig=False)
            P.op("pe", lambda e: e.transpose(out=PS[7][:, 262:266], in_=Gc, identity=ident_f[0:4, 0:4]), [GK, "ident_f"], ["b7c"], sig=False)
            P.op("pe", lambda e: e.transpose(out=PS[7][:, 266:270], in_=Mc, identity=ident_f[0:4, 0:4]), [MK, "ident_f"], ["b7c"])
            P.op("dve", lambda e: e.tensor_copy(out=SC(72, 84), in_=PS[7][:, 258:270]), ["b7c"], ["cols"])
            P.op("dve", lambda e: e.tensor_scalar(out=cAp, in0=cA, scalar1=LNC, scalar2=None, op0=ALU.add), ["cols"], ["cAp"])
            if not sample:
                P.op("dve", lambda e: e.tensor_scalar(out=gt[:, S_DG, 0:4], in0=ident_f[0:4, 0:4], scalar1=G_end, scalar2=None, op0=ALU.mult),
                     ["g_t", "ident_f"], ["g_dg"])
                P.op("pe", lambda e: e.matmul(PS[7][:, 270:274], gs(S_ONE), gt[:, S_DG, 0:4], start=True, stop=True),
                     ["g_one", "g_dg"], ["b7d"])
                P.op("dve", lambda e: e.tensor_copy(out=Gend_bc, in_=PS[7][:, 270:274]), ["b7d"], ["Gend_bc"])
                gp_col, ge_col = Gprev_bc, Gend_bc
                GPK, GEK = "Gprev_bc", "Gend_bc"
            else:
                P.op("pe", lambda e: e.transpose(out=PS[7][:, 270:274], in_=gs(S_GP), identity=ident_f[0:4, 0:4]), ["g_gp", "ident_f"], ["b7d"], sig=False)
                P.op("pe", lambda e: e.transpose(out=PS[7][:, 274:278], in_=gs(S_GE), identity=ident_f[0:4, 0:4]), ["g_ge", "ident_f"], ["b7d"])
                P.op("dve", lambda e: e.tensor_copy(out=small[:, 132:140], in_=PS[7][:, 270:278]), ["b7d"], ["cols_s"])
                gp_col, ge_col = cGp_s, cGe_s
                GPK, GEK = "cols_s", "cols_s"
            P.op("dve", lambda e: e.tensor_tensor(out=d1, in0=gp_col, in1=cG, op=ALU.subtract), [GPK, "cols"], ["d1"])
            P.op("act", lambda e: e.activation(out=winter, in_=d1, func=AF.Exp), ["d1"], ["winter"])
            P.op("act", lambda e: e.activation(out=floor_, in_=cM, func=AF.Exp, scale=-1.0), ["cols"], ["floor"])
            P.op("dve", lambda e: e.tensor_tensor(out=d2, in0=cAp, in1=ge_col, op=ALU.subtract), ["cAp", GEK], ["d2"])
            P.op("act", lambda e: e.activation(out=wk, in_=d2, func=AF.Exp), ["d2"], ["wk"])
            if not sample:
                P.op("dve", lambda e: e.tensor_tensor(out=d3, in0=Gprev_bc, in1=Gend_bc, op=ALU.subtract), ["Gprev_bc", "Gend_bc"], ["d3"])
                P.op("act", lambda e: e.activation(out=dec, in_=d3, func=AF.Exp), ["d3"], ["dec"])
                P.op("dve", lambda e: e.tensor_copy(out=gt[:, S_CAR, 0:1], in_=Bt[:, 127:128]), ["g_B", "g_car"], ["g_car"])
                P.op("dve", lambda e: e.tensor_copy(out=gt[:, S_CAR, 1:2], in_=Gt[:, 127:128]), ["g_t", "g_car"], ["g_car"])
                P.op("dve", lambda e: e.tensor_copy(out=Gprev_bc, in_=Gend_bc), ["Gend_bc", "Gprev_bc"], ["Gprev_bc"])
            else:
                v3 = lambda ap: ap.rearrange("p (j r) -> p j r", j=16)
                drow = gt[:, S_DG, 16:32]
                P.op("dve", lambda e: e.tensor_tensor(out=drow.rearrange("p (j o) -> p j o", o=1), in0=v3(Gt)[:, :, 0:1], in1=v3(Gt)[:, :, 8:9],
                                                      op=ALU.subtract), ["g_t"], ["g_dg"])
                P.op("act", lambda e: e.activation(out=drow, in_=drow, func=AF.Exp), ["g_dg"], ["g_dg"])
                for h in range(4):
                    P.op("pe", lambda e, h=h: e.matmul(PS[7][:, 278 + 16 * h:278 + 16 * (h + 1)], sel[:, h, :], drow, start=True, stop=True),
                         ["sel", "g_dg"], ["b7e"], sig=(h == 3))
                P.op("dve", lambda e: e.tensor_copy(out=small[:, 160:224], in_=PS[7][:, 278:342]), ["b7e"], ["decbc"])

            if MIXSTOP <= 3:
                return
            want_f32 = (t == 15)
            for cc in range(8):
                bank = 3 + cc % 2
                col = (cc // 2) * 128
                key = "b%d" % bank
                for kc in range(8):
                    P.op("pe", lambda e, bank=bank, col=col, cc=cc, kc=kc: e.matmul(
                        PS[bank][:, col:col + 128], w_in_sb[:, kc, 1024 + cc * 128:1024 + (cc + 1) * 128], x1T[:, kc, :],
                        start=(kc == 0), stop=(kc == 7)), XK + ["w_in%d" % kc], [key], sig=(kc == 7))
                if not sample:
                    P.op("act", lambda e, bank=bank, col=col, cc=cc: e.activation(
                        out=ext_bf[:, cc, 3:131], in_=PS[bank][:, col:col + 128], func=AF.Identity, bias=colsC[:, cc:cc + 1], scale=1.0),
                        [key, "colsC"], ["extb%d" % cc])
                    if want_f32:
                        P.op("act", lambda e, bank=bank, col=col, cc=cc: e.activation(
                            out=ext[:, cc, 3:131], in_=PS[bank][:, col:col + 128], func=AF.Identity, bias=colsC[:, cc:cc + 1], scale=1.0),
                            [key, "colsC"], ["ext%d" % cc])
                    cs = 342 if cc % 2 == 0 else 128
                    for j in range(4):
                        P.op("pe", lambda e, cs=cs, cc=cc, j=j: e.matmul(
                            PS[5][:, cs:cs + 128], Dg[:, j * 8 + cc, :], ext_bf[:, cc, j:j + 128], start=(j == 0), stop=(j == 3)),
                            ["lnp", "extb%d" % cc, "extcb"], ["b5"], sig=(j == 3))
                    P.op("act", lambda e, cs=cs, cc=cc: e.activation(out=qkT_bf[:, cc, :], in_=PS[5][:, cs:cs + 128], func=AF.Silu,
                                                                     bias=colsC[:, 8 + cc:9 + cc], scale=1.0), ["b5", "colsC"], ["qk%d" % cc])
                else:
                    P.op("act", lambda e, bank=bank, col=col, cc=cc: e.activation(
                        out=ext_s[:, cc, :, 3:11], in_=PS[bank][:, col:col + 128].rearrange("p (j r) -> p j r", j=16),
                        func=AF.Identity, bias=colsC[:, cc:cc + 1], scale=1.0), [key, "colsC", "extc"], ["ext%d" % cc])
            if sample:
                for cc in range(8):
                    ca = Dm[cc % 2]
                    ck = "Dm%d" % (cc % 2)
                    src = lambda j, cc=cc: ext_s[:, cc, :, j:j + 8]
                    cav = ca[:, :].rearrange("p (j r) -> p j r", j=16)
                    P.op("dve", lambda e, cc=cc, src=src, cav=cav: e.tensor_scalar(out=cav, in0=src(0), scalar1=colsC[:, 16 + cc:17 + cc],
                                                                                 scalar2=None, op0=ALU.mult),
                         ["ext%d" % cc, "extc", "colsC"], [ck])
                    for j in range(1, 4):
                        P.op("dve", lambda e, cc=cc, j=j, src=src, cav=cav: e.scalar_tensor_tensor(
                            out=cav, in0=src(j), scalar=colsC[:, 16 + j * 8 + cc:17 + j * 8 + cc], in1=cav, op0=ALU.mult, op1=ALU.add),
                            ["ext%d" % cc, "extc", ck], [ck])
                    P.op("act", lambda e, cc=cc, ca=ca: e.activation(out=qkT_bf[:, cc, :], in_=ca[:, :], func=AF.Silu,
                                                                     bias=colsC[:, 8 + cc:9 + cc], scale=1.0), [ck, "colsC"], ["qk%d" % cc])
                    if cc < 4:
                        P.op("act", lambda e, cc=cc, ca=ca: e.activation(out=qTf[:, cc, :], in_=ca[:, :], func=AF.Silu,
                                                                         bias=colsC[:, 8 + cc:9 + cc], scale=1.0), [ck, "colsC"], ["qTf%d" % cc])
            EXK = ["ext%d" % cc for cc in range(8)]
            if t == 15:
                for cc in range(8):
                    bank = 1 + cc // 4
                    P.op("pe", lambda e, cc=cc, bank=bank: e.transpose(out=PS[bank][0:3, (cc % 4) * 128:(cc % 4 + 1) * 128],
                                                                       in_=ext[:, cc, 128:131], identity=ident_f[:]),
                         ["ext%d" % cc, "ident_f"], ["b%d" % bank], sig=(cc % 4 == 3))
                P.op("dve", lambda e: e.tensor_copy(out=mixf[0:3, 0:512], in_=PS[1][0:3, :]), ["b1"], ["mixf"])
                P.op("act", lambda e: e.activation(out=vn[0:3, 0:512], in_=PS[2][0:3, :], func=AF.Copy), ["b2"], ["vn"])
                P.dma("sp", lambda e: e.dma_start(out=conv_p[:, 0:512], in_=mixf[0:3, 0:512]), ["mixf"], [], out=True)
                P.dma("sp", lambda e: e.dma_start(out=conv_p[:, 512:1024], in_=vn[0:3, 0:512]), ["vn"], [], out=True)
            if not sample and t < 15:
                P.op("pool", lambda e: e.tensor_copy(out=ext_bf[:, :, 0:3], in_=ext_bf[:, :, 128:131]),
                     ["extb%d" % c_ for c_ in range(8)] + ["extcb"], ["extcb"])
            if sample:
                P.op("pool", lambda e: e.tensor_copy(out=convc[:, :, :].rearrange("p a (j r) -> p a j r", j=16), in_=ext_s[:, :, :, 8:11]),
                     EXK, ["convc"])
                for cc in range(8):
                    bank = 1 + cc // 4
                    P.op("pe", lambda e, cc=cc, bank=bank: e.transpose(out=PS[bank][0:48, (cc % 4) * 128:(cc % 4 + 1) * 128],
                                                                       in_=convc[:, cc, :], identity=ident_f[:]),
                         ["convc", "ident_f"], ["b%d" % bank], sig=(cc % 4 == 3))
                P.op("dve", lambda e: e.tensor_copy(out=mixf[0:48, 0:512], in_=PS[1][0:48, :]), ["b1"], ["mixf"])
                P.op("act", lambda e: e.activation(out=vn[0:48, 0:512], in_=PS[2][0:48, :], func=AF.Copy), ["b2"], ["vn"])
                P.dma("sp", lambda e: e.dma_start(out=conv_s[:, 0:512], in_=mixf[0:48, 0:512]), ["mixf"], [], out=True)
                P.dma("sp", lambda e: e.dma_start(out=conv_s[:, 512:1024], in_=vn[0:48, 0:512]), ["vn"], [], out=True)

            if MIXSTOP <= 4:
                return
            zblock(1, 512, 1)
            P.op("act", lambda e: e.activation(out=vn[:, :], in_=PS[1][:, :], func=AF.Gelu_apprx_tanh), ["b1"], ["vn"])
            zblock(0, 0, 2)
            P.op("act", lambda e: e.activation(out=u_act[:, :], in_=PS[2][:, :], func=AF.Gelu_apprx_tanh), ["b2"], ["u_act"])
            for h in range(4):
                P.op("dve", lambda e, h=h: e.bn_stats(out=st4[:, h, :], in_=vn[:, h * 128:(h + 1) * 128]), ["vn"], ["st4_%d" % h])
                P.op("dve", lambda e, h=h: e.bn_aggr(out=mv4[:, h, :], in_=st4[:, h, :]), ["st4_%d" % h], ["mv4_%d" % h])
            MVK = ["mv4_%d" % h for h in range(4)]
            P.op("act", lambda e: e.activation(out=sd4, in_=mv4[:, :, 1], func=AF.Ln, bias=LN_EPS, scale=1.0), MVK, ["sd4"])
            P.op("act", lambda e: e.activation(out=rstd4, in_=sd4, func=AF.Exp, scale=-0.5), ["sd4"], ["rstd4"])
            P.op("dve", lambda e: e.scalar_tensor_tensor(out=nmr4, in0=mv4[:, :, 0], scalar=-1.0, in1=rstd4, op0=ALU.mult, op1=ALU.mult),
                 MVK + ["rstd4"], ["nmr4"])
            for h in range(4):
                P.op("dve", lambda e, h=h: e.tensor_scalar(out=vn[:, h * 128:(h + 1) * 128], in0=vn[:, h * 128:(h + 1) * 128],
                                                           scalar1=rstd4[:, h:h + 1], scalar2=nmr4[:, h:h + 1], op0=ALU.mult, op1=ALU.add),
                     ["vn", "rstd4", "nmr4"], ["vn"])
            P.op("dve", lambda e: e.tensor_tensor(out=vn[:, :], in0=vn[:, :], in1=gmln[:, 0, :], op=ALU.mult), ["vn", "gmln"], ["vn"])
            P.op("dve", lambda e: e.tensor_tensor(out=vn[:, :], in0=vn[:, :], in1=gmln[:, 1, :], op=ALU.add), ["vn", "gmln"], ["vn"])
            P.op("dve", lambda e: e.tensor_copy(out=vn_bf[:, :], in_=vn[:, :]), ["vn"], ["vn_bf"])
            if t == 15:
                P.dma("sp", lambda e: e.dma_start(out=gmv_p[:, :], in_=vn[:, :]), ["vn"], [], out=True)
            if sample:
                P.dma("sp", lambda e: e.dma_start(out=gmv_s[:, :], in_=vn[:, :]), ["vn"], [], out=True)
            zblock(2, 2048, 1)
            P.op("act", lambda e: e.activation(out=v_ext[:, :, 0:128], in_=PS[1][:, :].rearrange("p (a b) -> p a b", a=4), func=AF.Copy),
                 ["b1"], ["v_ext"])
            zblock(3, 2560, 2)
            P.op("act", lambda e: e.activation(out=o_sig[:, :], in_=PS[2][:, :], func=AF.Sigmoid), ["b2"], ["o_sig"])

            if MIXSTOP <= 5:
                return
            phase[0] = "B"
            wsx = wsT_s if sample else wsT
            wsk = "wsT_s" if sample else "wsT"
            bsc = 52 if sample else 48
            for h in range(4):
                P.op("pe", lambda e, h=h: e.matmul(PS[3][:, h * 128:(h + 1) * 128], wsx[:, h, :], vn_bf[:, h * 128:(h + 1) * 128],
                                                   start=True, stop=True), [wsk, "vn_bf"], ["b3_%d" % h])
                P.op("dve", lambda e, h=h: e.scalar_tensor_tensor(out=mixf[:, h * 128:(h + 1) * 128], in0=PS[3][:, h * 128:(h + 1) * 128],
                                                                  scalar=colsC[:, bsc + h:bsc + h + 1], in1=u_act[:, h * 128:(h + 1) * 128],
                                                                  op0=ALU.add, op1=ALU.mult), ["b3_%d" % h, "u_act", "colsC"], ["mixf"])
            head_rms_to_mixbf(0)

            if MIXSTOP <= 6:
                return
            ps4_bf = PS[4][:, 0:256].bitcast(BF16)
            for h in range(4):
                P.op("pe", lambda e, h=h: e.transpose(out=ps4_bf[:, h * 128:(h + 1) * 128], in_=qkT_bf[:, 4 + h, :], identity=ident_bf[:]),
                     ["qk%d" % (4 + h), "ident_bf"], ["b4_0", "b4_1"], sig=(h == 3))
            for h in range(4):
                P.op("dve", lambda e, h=h: e.tensor_scalar(out=kw_bf[:, h, :], in0=ps4_bf[:, h * 128:(h + 1) * 128], scalar1=wk[:, h:h + 1],
                                                           scalar2=None, op0=ALU.mult), ["b4_0", "b4_1", "wk"], ["kw%d" % h])
            nm = negmask_s if sample else negmask
            if sample:
                for h in range(4):
                    P.op("pe", lambda e, h=h: e.transpose(out=PS[4][:, h * 128:(h + 1) * 128], in_=qTf[:, h, :], identity=ident_f[:]),
                         ["qTf%d" % h, "ident_f"], ["b4_%d" % h], sig=(h == 3))
                B4K = ["b4_%d" % h for h in range(4)]
                P.op("act", lambda e: e.activation(out=u_act[:, :], in_=PS[4][:, :], func=AF.Copy), B4K, ["u_act"])
                for r_ in range(8):
                    P.dma("sp", lambda e, r_=r_: e.dma_start(out=mixf[r_:128:8, :], in_=sn.rearrange("(j h) d -> j (h d)", h=4)),
                          [], ["mixf"])
                for h in range(4):
                    P.op("dve", lambda e, h=h: e.scalar_tensor_tensor(out=junk[:, 0:128], in0=u_act[:, h * 128:(h + 1) * 128], scalar=1.0,
                                                                      in1=mixf[:, h * 128:(h + 1) * 128], op0=ALU.mult, op1=ALU.mult,
                                                                      accum_out=dq[:, h:h + 1]), ["u_act", "mixf"], ["dq%d" % h, "junk"])
                P.dma("sp", lambda e: e.dma_start(out=n0row[0:64, :], in_=sn[:, :]), [], ["n0row"])
                P.op("pe", lambda e: e.transpose(out=PS[7][:, 342:406], in_=n0row[0:64, :], identity=ident_f[0:64, 0:64]),
                     ["n0row", "ident_f"], ["b7f"])
                P.op("dve", lambda e: e.tensor_copy(out=n0T[:, :], in_=PS[7][:, 342:406]), ["b7f"], ["n0T"])
                for j in range(16):
                    cb = C0s[j % 2]
                    ckey = "C0s%d" % (j % 2)
                    P.dma("sp", lambda e, j=j, cb=cb: e.dma_start(out=cb[:, :, :], in_=sC[j].rearrange("h d e -> d h e")), [], [ckey])
                    for h in range(4):
                        P.op("pe", lambda e, j=j, h=h, cb=cb: e.matmul(PS[3][:, h * 128 + j * 8:h * 128 + j * 8 + 8], cb[:, h, :],
                                                                       qTf[:, h, j * 8:(j + 1) * 8], start=True, stop=True),
                             [ckey, "qTf%d" % h], ["b3_%d" % h])
                    vmk = vmask[j % 2]
                    vkey = "vmask%d" % (j % 2)
                    P.op("dve", lambda e, j=j, vmk=vmk: e.tensor_scalar(out=vmk[:, :, :], in0=v_ext[:, :, :], scalar1=onehot[:, j:j + 1],
                                                                        scalar2=None, op0=ALU.mult), ["v_ext", "onehot"], [vkey])
                    for h in range(4):
                        off = 0
                        pb_ = 1 + h % 2
                        pk = "b%d" % pb_
                        P.op("pe", lambda e, h=h, off=off, vmk=vmk, pb_=pb_: e.matmul(PS[pb_][:, off:off + 129], kw_bf[:, h, :], vmk[:, h, 0:129],
                                                                            start=True, stop=True), ["kw%d" % h, vkey], [pk])
                        P.op("dve", lambda e, j=j, h=h, off=off, cb=cb, pb_=pb_: e.scalar_tensor_tensor(
                            out=cb[:, h, :], in0=cb[:, h, :], scalar=decbc[:, h, j:j + 1], in1=PS[pb_][:, off:off + 128],
                            op0=ALU.mult, op1=ALU.add), [ckey, pk, "decbc"], [ckey])
                        P.op("dve", lambda e, j=j, h=h, off=off, pb_=pb_: e.scalar_tensor_tensor(
                            out=nnewT[:, j * 4 + h:j * 4 + h + 1], in0=n0T[:, j * 4 + h:j * 4 + h + 1], scalar=decbc[:, h, j:j + 1],
                            in1=PS[pb_][:, off + 128:off + 129], op0=ALU.mult, op1=ALU.add), ["n0T", pk, "decbc"], ["nnewT"])
                    P.dma("sp", lambda e, j=j, cb=cb: e.dma_start(out=C_s[j].rearrange("h d e -> d h e"), in_=cb[:, :, :]), [ckey], [], out=True)
                P.op("pe", lambda e: e.transpose(out=PS[7][0:64, 342:470], in_=nnewT[:, :], identity=ident_f[:]), ["nnewT", "ident_f"], ["b7f"])
                P.op("dve", lambda e: e.tensor_copy(out=n0row[0:64, :], in_=PS[7][0:64, 342:470]), ["b7f"], ["n0row"])
                P.dma("sp", lambda e: e.dma_start(out=n_s[:, :], in_=n0row[0:64, :]), ["n0row"], [], out=True)
                B3K = ["b3_%d" % h for h in range(4)]
                P.op("act", lambda e: e.activation(out=vn[:, :], in_=PS[3][:, :], func=AF.Copy), B3K, ["vn"])
                for h in range(4):
                    P.op("pe", lambda e, h=h: e.transpose(out=PS[4][:, h * 128:(h + 1) * 128], in_=vn[:, h * 128:(h + 1) * 128], identity=ident_f[:]),
                         ["vn", "ident_f"], ["b4_%d" % h])

            for h in range(4):
                s_ = h % 2
                bank = 5 + s_
                ka, kb, kc_ = "b%da" % bank, "b%db" % bank, "b%dc" % bank
                P.op("pe", lambda e, h=h, bank=bank: e.matmul(PS[bank][:, 0:128], sel[:, h, :], Gc, start=True, stop=True),
                     ["sel", GK], [ka])
                P.op("dve", lambda e, s_=s_, bank=bank: e.scalar_tensor_tensor(out=Dm[s_][:, :], in0=PS[bank][:, 0:128], scalar=-1.0,
                                                                               in1=nm[:, :], op0=ALU.mult, op1=ALU.add),
                     [ka, "negmask", "negmask_s"], ["Dm%d" % s_])
                P.op("act", lambda e, s_=s_, h=h: e.activation(out=Dm[s_][:, :], in_=Dm[s_][:, :], func=AF.Exp, bias=cAp[:, h:h + 1], scale=1.0),
                     ["Dm%d" % s_, "cAp"], ["Dm%d" % s_])
                P.op("pe", lambda e, h=h, bank=bank: e.matmul(PS[bank][:, 128:256], qkT_bf[:, 4 + h, :], qkT_bf[:, h, :], start=True, stop=True),
                     ["qk%d" % h, "qk%d" % (4 + h)], [kb])
                P.op("dve", lambda e, s_=s_, bank=bank: e.tensor_tensor(out=Pb[s_][:, :], in0=PS[bank][:, 128:256], in1=Dm[s_][:, :], op=ALU.mult),
                     [kb, "Dm%d" % s_], ["Pb%d" % s_])
                P.op("pe", lambda e, h=h, s_=s_, bank=bank: e.matmul(PS[bank][:, 256:385], Pb[s_][:, :], v_ext[:, h, 0:129], start=True, stop=True),
                     ["Pb%d" % s_, "v_ext"], [kc_])
                if not sample:
                    P.op("pe", lambda e, h=h: e.matmul(PS[1][:, 0:129], qkT_bf[:, h, :], Cbf[:, h, 0:129], start=True, stop=True),
                         ["qk%d" % h, "Cbf%d" % h], ["b1"])
                    P.op("dve", lambda e, h=h: e.tensor_scalar(out=itw[:, 0:129], in0=PS[1][:, 0:129], scalar1=winter[:, h:h + 1], scalar2=None,
                                                               op0=ALU.mult), ["b1", "winter"], ["itw"])
                else:
                    P.op("act", lambda e, h=h: e.activation(out=itw[:, 0:128], in_=PS[4][:, h * 128:(h + 1) * 128], func=AF.Identity,
                                                            scale=winter[:, h:h + 1]), ["b4_%d" % h, "winter"], ["itw"])
                    P.op("dve", lambda e, h=h: e.tensor_tensor(out=itw[:, 128:129], in0=dq[:, h:h + 1], in1=winter[:, h:h + 1], op=ALU.mult),
                         ["dq%d" % h, "winter", "itw"], ["itw"])
                P.op("dve", lambda e, bank=bank: e.tensor_tensor(out=itw[:, 0:129], in0=PS[bank][:, 256:385], in1=itw[:, 0:129], op=ALU.add),
                     [kc_, "itw"], ["itw"])
                P.op("dve", lambda e: e.scalar_tensor_tensor(out=dd, in0=itw[:, 128:129], scalar=-1.0, in1=itw[:, 128:129],
                                                             op0=ALU.mult, op1=ALU.max), ["itw"], ["dd"])
                P.op("dve", lambda e, h=h: e.tensor_tensor(out=dd, in0=dd, in1=floor_[:, h:h + 1], op=ALU.max), ["dd", "floor"], ["dd"])
                P.op("dve", lambda e: e.reciprocal(out=rd, in_=dd), ["dd"], ["rd"])
                P.op("dve", lambda e, h=h: e.scalar_tensor_tensor(out=mixf[:, h * 128:(h + 1) * 128], in0=itw[:, 0:128], scalar=rd,
                                                                  in1=o_sig[:, h * 128:(h + 1) * 128], op0=ALU.mult, op1=ALU.mult),
                     ["itw", "rd", "o_sig", "mix_bf0"], ["mixf"])
                if not sample:
                    P.op("pe", lambda e, h=h: e.matmul(PS[2][:, 0:129], kw_bf[:, h, :], v_ext[:, h, 0:129], start=True, stop=True),
                         ["kw%d" % h, "v_ext"], ["b2"])
                    P.op("dve", lambda e, h=h: e.scalar_tensor_tensor(out=Cst[:, h, 0:129], in0=Cst[:, h, 0:129], scalar=dec[:, h:h + 1],
                                                                      in1=PS[2][:, 0:129], op0=ALU.mult, op1=ALU.add),
                         ["Cst%d" % h, "dec", "b2"], ["Cst%d" % h])
                    P.op("dve", lambda e, h=h: e.tensor_copy(out=Cbf[:, h, 0:129], in_=Cst[:, h, 0:129]), ["Cst%d" % h], ["Cbf%d" % h])
            head_rms_to_mixbf(512)

            if MIXSTOP <= 7:
                return
            if t == 15:
                P.dma("sp", lambda e: e.dma_start(out=m_p[:, :], in_=Mt[:, 127:128]), ["g_M"], [], out=True)
                CK = ["Cst%d" % h for h in range(4)]
                P.dma("sp", lambda e: e.dma_start(out=C_p.rearrange("h d e -> d h e"), in_=Cst[:, :, 0:128]), CK, [], out=True)
                P.op("pool", lambda e: e.tensor_copy(out=ncol, in_=Cst[:, :, 128]), CK, ["ncol"])
                P.op("pe", lambda e: e.transpose(out=PS[7][0:4, 342:470], in_=ncol, identity=ident_f[:]), ["ncol", "ident_f"], ["b7f"])
                P.op("dve", lambda e: e.tensor_copy(out=gs(S_NR), in_=PS[7][0:4, 342:470]), ["b7f"], ["g_nr"])
                P.dma("sp", lambda e: e.dma_start(out=n_p[:, :], in_=gs(S_NR)), ["g_nr"], [], out=True)
            if sample:
                v3 = lambda ap: ap.rearrange("p (j r) -> p j r", j=16)
                P.op("pool", lambda e: e.tensor_copy(out=gt[:, S_DG, 32:48].rearrange("p (j o) -> p j o", o=1), in_=v3(Mt)[:, :, 8:9]), ["g_M", "g_dg"], ["g_dg"])
                P.dma("sp", lambda e: e.dma_start(out=m_s.rearrange("j h -> h j"), in_=gt[:, S_DG, 32:48], allow_slow_non_contiguous=True),
                      ["g_dg"], [], out=True)

            if MIXSTOP <= 8:
                return
            phase[0] = "C"
            for kc in range(8):
                bank = 0
                P.op("pe", lambda e, kc=kc: e.transpose(out=PS[0][:, :].bitcast(BF16)[:, kc * 128:(kc + 1) * 128],
                                                        in_=mix_bf[:, kc * 128:(kc + 1) * 128], identity=ident_bf[:]),
                     ["mix_bf0", "mix_bf1", "ident_bf"], ["b0"], sig=(kc == 7))
            P.op("dve", lambda e: e.tensor_copy(out=mixT[:, :, :], in_=PS[0][:, :].bitcast(BF16).rearrange("p (a b) -> p a b", a=8)),
                 ["b0"], ["mixT"])
            for half in range(2):
                bank = 1 + half
                for kc in range(8):
                    P.op("pe", lambda e, kc=kc, half=half, bank=bank: e.matmul(PS[bank][:, :], mixT[:, kc, :],
                                                                               w_out_sb[:, kc, half * 512:(half + 1) * 512],
                                                                               start=(kc == 0), stop=(kc == 7)),
                         ["mixT", "w_out"], ["b%d" % bank], sig=(kc == 7))
                P.op("dve", lambda e, half=half, bank=bank: e.scalar_tensor_tensor(
                    out=acc[:, t, half * 512:(half + 1) * 512], in0=PS[bank][:, :], scalar=1.0 / ALPHA,
                    in1=acc[:, t, half * 512:(half + 1) * 512], op0=ALU.mult, op1=ALU.add), [k, "b%d" % bank], [k])
            layer_norm_tile(t, affine=False, norm_on_dve=True)

        def mixer_stage(tiles):
            P.keymap = mixer_keymap
            phase[0] = "A"
            par[0] = 0
            win_v = wd_["w_in"].rearrange("(kc kp) f -> kp kc f", kp=128)
            AK_ = ["wg0_%d" % c for c in range(NPC)] + ["wu0_%d" % c for c in range(NPC)] + ["wd0_%d" % c for c in range(NPC)]
            BK_ = ["wg1_%d" % c for c in range(NPC)] + ["wu1_%d" % c for c in range(NPC)] + ["wd1_%d" % c for c in range(NPC)]
            HK_ = ["hT%d_%d" % (b_, c) for b_ in range(2) for c in range(NPC)] + ["sg0", "sg1"]
            for kc in range(8):
                aft = AK_ if kc < 3 else (AK_ + BK_ + HK_ if kc == 3 else BK_ + HK_)
                P.dma("pool", lambda e, kc=kc: e.dma_start(out=w_in_sb[:, kc, :], in_=win_v[:, kc, :]), r=["__nobar__"], w=["__nobar__", "w_in%d" % kc],
                      after=aft)
            P.dma("pool", lambda e: e.dma_start(out=w_out_sb[:, :, :], in_=wd_["w_out"].rearrange("(kc kp) f -> kp kc f", kp=128)),
                  r=["__nobar__"], w=["__nobar__", "w_out"], after=HK_ + ["lnjunk"])
            for kc in range(8):
                if kc % 2 == 0:
                    P.op("act", lambda e, kc=kc: e.activation(out=w_out_sb[:, kc, :], in_=w_out_sb[:, kc, :], func=AF.Identity,
                                                              scale=colsC[:, 56 + kc:57 + kc]), ["__nobar__", "w_out", "colsC"], ["__nobar__", "w_out"])
                else:
                    P.op("dve", lambda e, kc=kc: e.tensor_scalar(out=w_out_sb[:, kc, :], in0=w_out_sb[:, kc, :], scalar1=colsC[:, 56 + kc:57 + kc],
                                                                 scalar2=None, op0=ALU.mult), ["__nobar__", "w_out", "colsC"], ["__nobar__", "w_out"])
            P.op("pool", lambda e: e.memset(extb[:, :], 0.0), (), ["extc"] + ["ext%d" % c for c in range(8)])
            P.op("pool", lambda e: e.memset(extbb[:, :], 0.0), (), ["extcb"] + ["extb%d" % c for c in range(8)])
            for idx in range(32):
                P.op("dve", lambda e, idx=idx: e.tensor_scalar(out=Dg[:, idx, :], in0=ident_bf[:, :], scalar1=colsC[:, 16 + idx:17 + idx],
                                                               scalar2=None, op0=ALU.mult), ["ident_bf", "colsC", "lnp"], ["lnp"])
            for q_ in range(2):
                par[0] = q_
                P.op("pool", lambda e, q_=q_: e.memset(v_ext2[q_][:, :, :], 1.0), (), ["v_ext"])
            par[0] = 0
            P.op("pool", lambda e: e.memset(gs(S_ONE, 144), 1.0), (), ["g_one"])
            P.op("pool", lambda e: e.memset(gs(S_RM, 144), 1.0), (), ["g_rm"])
            P.op("pool", lambda e: e.memset(gs(S_RM, 144).rearrange("p (j r) -> p j r", j=16)[:, :, 0:1], 0.0), ["g_rm"], ["g_rm"])
            P.op("pool", lambda e: e.memset(gs(S_RA, 144), 0.0), (), ["g_ra"])
            P.op("pool", lambda e: e.memset(gs(S_RA, 144).rearrange("p (j r) -> p j r", j=16)[:, :, 0:1], NEG), ["g_ra"], ["g_ra"])
            P.op("pool", lambda e: e.memset(gt[:, S_CAR, 0:2], 0.0), (), ["g_car"])
            P.op("pool", lambda e: e.memset(Gprev_bc, 0.0), (), ["Gprev_bc"])
            if 16 in tiles:
                tiles = [16] + [t_ for t_ in tiles if t_ != 16]
            for t in tiles:
                if t == 0:
                    phase[0] = "A"
                    par[0] = 0
                    P.op("pool", lambda e: e.memset(extb[:, :], 0.0), ["extc"] + ["ext%d" % c for c in range(8)],
                         ["extc"] + ["ext%d" % c for c in range(8)])
                    P.op("pool", lambda e: e.memset(extbb[:, :], 0.0), ["extcb"] + ["extb%d" % c for c in range(8)], ["extcb"] + ["extb%d" % c for c in range(8)])
                if t == 16:
                    phase[0] = "A"
                    par[0] = 0
                    P.op("pool", lambda e: e.memset(extb[:, :], 0.0), ["extc"] + ["ext%d" % c for c in range(8)],
                         ["extc"] + ["ext%d" % c for c in range(8)])
                    P.op("pool", lambda e: e.memset(extbb[:, :], 0.0), ["extcb"] + ["extb%d" % c for c in range(8)], ["extcb"] + ["extb%d" % c for c in range(8)])
                    P.dma("sp", lambda e: e.dma_start(out=mixf[0:48, :], in_=sconv[:, 0:512]), [], ["mixf"])
                    P.dma("sp", lambda e: e.dma_start(out=vn[0:48, :], in_=sconv[:, 512:1024]), [], ["vn"])
                    for cc in range(8):
                        srcb = mixf if cc < 4 else vn
                        P.op("pe", lambda e, cc=cc, srcb=srcb: e.transpose(out=PS[1][:, cc * 48:(cc + 1) * 48],
                                                                           in_=srcb[0:48, (cc % 4) * 128:(cc % 4 + 1) * 128],
                                                                           identity=ident_f[0:48, 0:48]),
                             ["mixf", "vn", "ident_f"], ["b1"], sig=(cc == 7))
                    P.op("dve", lambda e: e.tensor_copy(out=ext_s[:, :, :, 0:3], in_=PS[1][:, 0:384].rearrange("p (a j r) -> p a j r", a=8, j=16)),
                         ["b1", "extc"], ["extc"])
                    P.dma("sp", lambda e: e.dma_start(out=gt[:, S_NR, 128:144], in_=sm.rearrange("j h -> h j"),
                                                      allow_slow_non_contiguous=True), [], ["g_m0"])
                mixer_tile(t)
            P.keymap = None

        if _os.environ.get("SKIPFFN1"):
            for t in range(NT):
                src = x_p[t * 128:(t + 1) * 128, :] if t < 16 else x_s[:, :]
                P.dma("sp", lambda e, t=t, src=src: e.dma_start(out=acc[:, t, :], in_=src), w=["acc%d" % t])
        else:
            load_lnp("ln1_g", "ln1_b")
            ffn_stage(1, True)

        if dbg == 1:
            dbg_o = dout("dbg", [NTOK, D])
            for t in range(NT):
                P.dma("sp", lambda e, t=t: e.dma_start(out=dbg_o[t * 128:(t + 1) * 128, :], in_=acc[:, t, :]), ["acc%d" % t], [], out=True)
        else:
            P.barrier("BAR1", lambda e: e.memset(small[:, 318:319], 0.0))
            mixer_stage([int(v) for v in _os.environ['MIXLIST'].split(',')] if _os.environ.get('MIXLIST') else [t for t in (list(range(NT)) if dbg != 2 else list(range(16))) if t < MIXTILES])
            if dbg in (2, 3):
                dbg_o = dout("dbg", [NTOK, D])
                for t in range(NT):
                    P.dma("sp", lambda e, t=t: e.dma_start(out=dbg_o[t * 128:(t + 1) * 128, :], in_=acc[:, t, :]), ["acc%d" % t], [], out=True)
            else:
                P.barrier("BAR2", lambda e: e.memset(small[:, 319:320], 0.0))
                load_lnp("ln2_g", "ln2_b")
                ffn_stage(2, False, pre_affine=True, lnp_after_p0=("ln3_g", "ln3_b"))
                for t in range(NT):
                    dst = y_p[t * 128:(t + 1) * 128, :] if t < 16 else y_s[:, :]
                    P.dma("sp", lambda e, t=t, dst=dst: e.dma_start(out=dst, in_=acc[:, t, :]), ["acc%d" % t], [], out=True)

        sems = {}
        for nm_ in P.sem_names():
            sems[nm_] = es.enter_context(nc.semaphore(nm_))
        with nc.Block() as block:
            P.emit(block, sems)
    return nc


def _prep_inputs(inputs):
    f32 = lambda a: np.ascontiguousarray(np.asarray(a, dtype=np.float32))
    shared = {
        "f1_wg": f32(inputs["ffn1_wg"][0]), "f1_wu": f32(inputs["ffn1_wu"][0]), "f1_wd": f32(inputs["ffn1_wd"][0]),
        "ln1_g": f32(inputs["ln1_g"]), "ln1_b": f32(inputs["ln1_b"]),
        "w_in": f32(inputs["w_in"][0]), "b_in": f32(inputs["b_in"]),
        "gm_ln_g": f32(inputs["gm_ln_g"]).reshape(1, 512), "gm_ln_b": f32(inputs["gm_ln_b"]).reshape(1, 512),
        "gm_ws": f32(inputs["gm_ws"][0]), "gm_bs": f32(inputs["gm_bs"][0]),
        "conv_w": f32(inputs["conv_w"][0]), "conv_b": f32(inputs["conv_b"]),
        "gm_out_g": f32(inputs["gm_out_g"]).reshape(1, 512), "ml_out_g": f32(inputs["ml_out_g"]).reshape(1, 512),
        "w_out": f32(inputs["w_out"][0]), "ln2_g": f32(inputs["ln2_g"]), "ln2_b": f32(inputs["ln2_b"]),
        "f2_wg": f32(inputs["ffn2_wg"][0]), "f2_wu": f32(inputs["ffn2_wu"][0]), "f2_wd": f32(inputs["ffn2_wd"][0]),
        "ln3_g": f32(inputs["ln3_g"]), "ln3_b": f32(inputs["ln3_b"]),
    }
    xp = f32(inputs["x_prompt"]); xs = f32(inputs["x_sample"])
    sc = f32(inputs["state_conv"][0]); sC = f32(inputs["state_C"][0]); sn = f32(inputs["state_n"][0]); sm = f32(inputs["state_m"][0])
    maps = []
    for i in range(NCORES):
        m = dict(shared)
        sl = slice(16 * i, 16 * i + 16)
        m["x_p"] = xp[i]
        m["x_s"] = np.ascontiguousarray(xs[sl].reshape(128, D))
        m["sconv"] = np.ascontiguousarray(sc[sl].reshape(48, D))
        m["sC"] = np.ascontiguousarray(sC[sl])
        m["sn"] = np.ascontiguousarray(sn[sl].reshape(64, 128))
        m["sm"] = np.ascontiguousarray(sm[sl])
        maps.append(m)
    return maps


def kernel(**inputs):
    maps = _prep_inputs(inputs)
    nc = build_program()
    res = run_bass_kernel_spmd(nc, maps, core_ids=list(range(NCORES)))
    R = res.results
    cat = lambda k: [np.asarray(r[k]) for r in R]
    y_p = np.stack(cat("y_p"), 0)
    y_s = np.concatenate(cat("y_s"), 0).reshape(128, 8, D)
    gmv_p = np.stack(cat("gmv_p"), 0).reshape(1, 8, 128, 4, 128)
    gmv_s = np.concatenate(cat("gmv_s"), 0).reshape(1, 128, 8, 4, 128)
    conv_p = np.stack(cat("conv_p"), 0).reshape(1, 8, 3, D)
    conv_s = np.concatenate(cat("conv_s"), 0).reshape(1, 128, 3, D)
    C_p = np.stack(cat("C_p"), 0).reshape(1, 8, 4, 128, 128)
    C_s = np.concatenate(cat("C_s"), 0).reshape(1, 128, 4, 128, 128)
    n_p = np.stack(cat("n_p"), 0).reshape(1, 8, 4, 128)
    n_s = np.concatenate(cat("n_s"), 0).reshape(1, 128, 4, 128)
    m_p = np.stack(cat("m_p"), 0).reshape(1, 8, 4)
    m_s = np.concatenate(cat("m_s"), 0).reshape(1, 128, 4)
    return (y_p, y_s, gmv_p, gmv_s, conv_p, conv_s, C_p, C_s, n_p, n_s, m_p, m_s)
```
